# Optimizing a Trainium2 kernel written in Bass

```python
import math
import jax, jax.numpy as jnp
from jax import lax
import numpy as np

D_MODEL = 2048
BATCH = 16
SEQ = 2048
DEPTH = 4

CTX_LEN = 256
GRID_W = 64
N_MIXERS = 3
N_RET_LAYERS = (DEPTH + 2) // 3
N_GDN_LAYERS = (DEPTH + 1) // 3
N_NA_LAYERS = DEPTH // 3
N_MOD = 6
D_FF = 4 * D_MODEL
NORM_EPS = 1e-6
ROPE_BASE = 10000.0

RET_HEADS = 8
RET_DK = D_MODEL // RET_HEADS
RET_DV = 2 * D_MODEL // RET_HEADS
RET_QK = RET_HEADS * RET_DK
RET_V = RET_HEADS * RET_DV
RET_CHUNK = 128

GDN_DK = 128
GDN_DV = 128
GDN_QK_HEADS = D_MODEL // 128
GDN_V_HEADS = D_MODEL // 64
GDN_QK_DIM = GDN_QK_HEADS * GDN_DK
GDN_V_DIM = GDN_V_HEADS * GDN_DV
GDN_CONV_DIM = 2 * GDN_QK_DIM + GDN_V_DIM
GDN_CONV = 5
GDN_CHUNK = 64

NA_HEADS = 16
NA_HD = D_MODEL // NA_HEADS
WIN_H = 8
WIN_W = 16
NA_QROWS = 2

kernel_name = 'hybrid_ret_gdn_natten_dit_trunk'


def rms_norm(x, g):
    xf = x.astype(jnp.float32)
    y = xf * lax.rsqrt(jnp.mean(xf * xf, axis=-1, keepdims=True) + NORM_EPS)
    return (y * g.astype(jnp.float32)).astype(x.dtype)


def head_rms(y):
    return y * lax.rsqrt(jnp.mean(y * y, axis=-1, keepdims=True) + NORM_EPS)


def l2norm(t):
    t = t.astype(jnp.float32)
    return t * lax.rsqrt(jnp.sum(t * t, axis=-1, keepdims=True) + 1e-6)


def to_heads(t, n):
    b, s, _ = t.shape
    return t.reshape(b, s, n, -1).transpose(0, 2, 1, 3)


def merge_heads(t):
    b, h, s, d = t.shape
    return t.transpose(0, 2, 1, 3).reshape(b, s, h * d)


def same_time(t):
    return t


def flip_time(t):
    return jnp.flip(t, axis=2)


def axial_rope(n_tokens, head_dim):
    t = jnp.arange(n_tokens)
    row = (t // GRID_W).astype(jnp.float32)
    col = (t % GRID_W).astype(jnp.float32)
    n_pairs = head_dim // 2
    inv = ROPE_BASE ** (-jnp.arange(0, n_pairs, 2, dtype=jnp.float32) / n_pairs)
    ang = jnp.concatenate([row[:, None] * inv, col[:, None] * inv], axis=-1)
    return jnp.cos(ang), jnp.sin(ang)


def apply_rope(x, cos, sin):
    x1, x2 = jnp.split(x, 2, axis=-1)
    c, s = cos.astype(x.dtype), sin.astype(x.dtype)
    return jnp.concatenate([x1 * c - x2 * s, x1 * s + x2 * c], axis=-1)


def short_conv(x, w):
    k = w.shape[0]
    return lax.conv_general_dilated(
        x, w[:, None, :].astype(x.dtype), window_strides=(1,),
        padding=[((k - 1) // 2, k // 2)], dimension_numbers=('NWC', 'WIO', 'NWC'),
        feature_group_count=x.shape[-1])


def sq_relu_mlp(h, w1, w2):
    return jnp.square(jax.nn.relu(h @ w1)) @ w2


def retention_log_decays():
    fwd = jnp.log(1.0 - 2.0 ** (-5.0 - jnp.arange(RET_HEADS, dtype=jnp.float32)))
    return jnp.stack([fwd, fwd[::-1]])


def retention_scan(q, k, v, log_gamma, state0, with_out):
    b, h, t, _ = k.shape
    dv = v.shape[-1]
    c = RET_CHUNK
    n = t // c
    pos = jnp.arange(c, dtype=jnp.float32)
    lg = log_gamma[:, None]
    zeta = jnp.exp(lg * (c - 1 - pos))
    xi = jnp.exp(lg * (pos + 1))
    chunk_decay = jnp.exp(log_gamma * c)

    def chunks(a):
        return jnp.moveaxis(a.astype(jnp.float32).reshape(b, h, n, c, a.shape[-1]), 2, 0)

    def update(R, kb, vb):
        return R * chunk_decay[None, :, None, None] + jnp.einsum(
            'bhcd,bhce->bhde', kb, vb * zeta[None, :, :, None])

    if not with_out:
        def step_state(R, inp):
            return update(R, inp[0], inp[1]), None
        R, _ = lax.scan(step_state, state0, (chunks(k), chunks(v)))
        return R, None

    dist = pos[:, None] - pos[None, :]
    intra = jnp.where(dist >= 0, jnp.exp(lg[:, :, None] * jnp.maximum(dist, 0.0)), 0.0)

    def step(R, inp):
        qb, kb, vb = inp
        s = jnp.einsum('bhcd,bhmd->bhcm', qb, kb) * intra[None]
        o = jnp.einsum('bhcm,bhme->bhce', s, vb) + jnp.einsum(
            'bhcd,bhde->bhce', qb, R) * xi[None, :, :, None]
        return update(R, kb, vb), o

    R, o = lax.scan(step, state0, (chunks(q), chunks(k), chunks(v)))
    return R, jnp.moveaxis(o, 0, 2).reshape(b, h, t, dv)


def retention_output(y, g, w_out):
    yn = merge_heads(head_rms(y))
    return (jax.nn.silu(g.astype(jnp.float32)) * yn).astype(g.dtype) @ w_out


def retention_mixer(h_lat, h_ctx, w_in, w_out, with_ctx_out):
    b, t, _ = h_lat.shape

    def project(h):
        q, k, v, g = jnp.split(h @ w_in, [RET_QK, 2 * RET_QK, 2 * RET_QK + RET_V], axis=-1)
        return (to_heads(q, RET_HEADS), to_heads(k, RET_HEADS) * RET_DK ** -0.5,
                to_heads(v, RET_HEADS), g)

    cos, sin = axial_rope(t, RET_DK)
    ql, kl, vl, gl = project(h_lat)
    ql, kl = apply_rope(ql, cos, sin), apply_rope(kl, cos, sin)
    qc, kc, vc, gc = project(h_ctx)
    log_gammas = retention_log_decays()
    outs_lat, outs_ctx = [], []
    for d, f in enumerate((same_time, flip_time)):
        zero = jnp.zeros((b, RET_HEADS, RET_DK, RET_DV), jnp.float32)
        s_ctx, o_ctx = retention_scan(f(qc), f(kc), f(vc), log_gammas[d], zero, with_ctx_out)
        _, o_lat = retention_scan(f(ql), f(kl), f(vl), log_gammas[d], s_ctx, True)
        outs_lat.append(f(o_lat))
        if with_ctx_out:
            outs_ctx.append(f(o_ctx))
    y_lat = retention_output(outs_lat[0] + outs_lat[1], gl, w_out)
    y_ctx = retention_output(outs_ctx[0] + outs_ctx[1], gc, w_out) if with_ctx_out else None
    return y_lat, y_ctx


def gated_delta_scan(q, k, v, g, beta, state0, with_out):
    b, h, t, _ = k.shape
    dv = v.shape[-1]
    c = GDN_CHUNK
    n = t // c

    def chunks(a):
        return jnp.moveaxis(a.reshape(b, h, n, c, *a.shape[3:]), 2, 0)

    tril = jnp.tril(jnp.ones((c, c), dtype=bool))
    strict = jnp.tril(jnp.ones((c, c), dtype=bool), -1)
    eye = jnp.eye(c, dtype=jnp.float32)

    def step(S, inp):
        kb, vb, gb, bb = inp[0], inp[1], inp[2], inp[3]
        gb = jnp.cumsum(gb, axis=-1)
        decay = jnp.exp(jnp.where(tril, gb[..., :, None] - gb[..., None, :], -jnp.inf))
        m = jnp.where(strict, jnp.einsum('bhcd,bhmd->bhcm', kb, kb) * decay, 0.0) * bb[..., :, None]
        rhs = jnp.concatenate([vb * bb[..., None], kb * (bb * jnp.exp(gb))[..., None]], axis=-1)
        sol = lax.linalg.triangular_solve(m + eye, rhs, left_side=True, lower=True)
        u, w = sol[..., :dv], sol[..., dv:]
        v_new = u - jnp.einsum('bhcd,bhde->bhce', w, S)
        g_last = gb[..., -1:]
        S_new = S * jnp.exp(g_last)[..., None] + jnp.einsum(
            'bhcd,bhce->bhde', kb * jnp.exp(g_last - gb)[..., None], v_new)
        if not with_out:
            return S_new, None
        qb = inp[4]
        a = jnp.einsum('bhcd,bhmd->bhcm', qb, kb) * decay
        o = jnp.einsum('bhcd,bhde->bhce', qb * jnp.exp(gb)[..., None], S) + jnp.einsum(
            'bhcm,bhme->bhce', a, v_new)
        return S_new, o

    xs = (chunks(k), chunks(v), chunks(g), chunks(beta))
    if with_out:
        xs = (xs[0], xs[1], xs[2], xs[3], chunks(q))
    S, o = lax.scan(step, state0, xs)
    if not with_out:
        return S, None
    return S, jnp.moveaxis(o, 0, 2).reshape(b, h, t, dv)


def gdn_project(h, w_in, conv_w):
    z = h @ w_in
    qkv = jax.nn.silu(short_conv(z[..., :GDN_CONV_DIM], conv_w))
    gate = z[..., GDN_CONV_DIM:]
    q, k, v = jnp.split(qkv, [GDN_QK_DIM, 2 * GDN_QK_DIM], axis=-1)
    rep = GDN_V_HEADS // GDN_QK_HEADS
    q = jnp.repeat(l2norm(to_heads(q, GDN_QK_HEADS)), rep, axis=1) * GDN_DK ** -0.5
    k = jnp.repeat(l2norm(to_heads(k, GDN_QK_HEADS)), rep, axis=1)
    v = to_heads(v, GDN_V_HEADS).astype(jnp.float32)
    return q, k, v, gate


def gdn_gates(h, w_ab, a_log, dt_bias):
    ab = (h @ w_ab).astype(jnp.float32)
    a, bt = ab[..., :GDN_V_HEADS], ab[..., GDN_V_HEADS:]
    g = -jnp.exp(a_log.astype(jnp.float32)) * jax.nn.softplus(a + dt_bias.astype(jnp.float32))
    return g.transpose(0, 2, 1), jax.nn.sigmoid(bt).transpose(0, 2, 1)


def gdn_output(y, gate, norm_w, w_out):
    yn = merge_heads(head_rms(y) * norm_w.astype(jnp.float32))
    return (yn * jax.nn.silu(gate.astype(jnp.float32))).astype(gate.dtype) @ w_out


def gdn_mixer(h_lat, h_ctx, w_in, conv_w, w_ab, a_log, dt_bias, norm_w, w_out, with_ctx_out):
    b = h_lat.shape[0]
    ql, kl, vl, zl = gdn_project(h_lat, w_in, conv_w)
    qc, kc, vc, zc = gdn_project(h_ctx, w_in, conv_w)
    outs_lat, outs_ctx = [], []
    for d, f in enumerate((same_time, flip_time)):
        gl, bl = gdn_gates(h_lat, w_ab[d], a_log[d], dt_bias[d])
        gc, bc = gdn_gates(h_ctx, w_ab[d], a_log[d], dt_bias[d])
        zero = jnp.zeros((b, GDN_V_HEADS, GDN_DK, GDN_DV), jnp.float32)
        s_ctx, o_ctx = gated_delta_scan(f(qc), f(kc), f(vc), f(gc), f(bc), zero, with_ctx_out)
        _, o_lat = gated_delta_scan(f(ql), f(kl), f(vl), f(gl), f(bl), s_ctx, True)
        outs_lat.append(f(o_lat))
        if with_ctx_out:
            outs_ctx.append(f(o_ctx))
    y_lat = gdn_output(outs_lat[0] + outs_lat[1], zl, norm_w, w_out)
    y_ctx = gdn_output(outs_ctx[0] + outs_ctx[1], zc, norm_w, w_out) if with_ctx_out else None
    return y_lat, y_ctx


def na_mixer(h_lat, h_ctx, w_in, w_out, rpb, with_ctx_out):
    b, t, _ = h_lat.shape
    rows = t // GRID_W
    kh, kw = min(WIN_H, rows), min(WIN_W, GRID_W)
    nb = min(kh + NA_QROWS - 1, rows)
    qblk = NA_QROWS * GRID_W
    nloc = nb * GRID_W
    scale = NA_HD ** -0.5

    def project(h):
        q, k, v = jnp.split(h @ w_in, 3, axis=-1)
        return to_heads(q, NA_HEADS) * scale, to_heads(k, NA_HEADS), to_heads(v, NA_HEADS)

    q, k, v = project(h_lat)
    qc, kc, vc = project(h_ctx)
    k_grid = k.reshape(b, NA_HEADS, rows, GRID_W, NA_HD)
    v_grid = v.reshape(b, NA_HEADS, rows, GRID_W, NA_HD)
    q_blocks = jnp.moveaxis(q.reshape(b, NA_HEADS, rows // NA_QROWS, qblk, NA_HD), 2, 0)
    row0 = jnp.arange(rows // NA_QROWS, dtype=jnp.int32) * NA_QROWS
    lq = jnp.arange(qblk, dtype=jnp.int32)
    lk = jnp.arange(nloc, dtype=jnp.int32)
    q_col, k_col = lq % GRID_W, lk % GRID_W
    col_start = jnp.clip(q_col - kw // 2, 0, GRID_W - kw)
    col_ok = (k_col[None] >= col_start[:, None]) & (k_col[None] < col_start[:, None] + kw)
    dc = jnp.clip(k_col[None] - q_col[:, None] + WIN_W - 1, 0, 2 * WIN_W - 2)

    def attend(inp):
        qb, r0 = inp
        b0 = jnp.clip(r0 - kh // 2, 0, rows - nb)
        kb = lax.dynamic_slice_in_dim(k_grid, b0, nb, axis=2).reshape(b, NA_HEADS, nloc, NA_HD)
        vb = lax.dynamic_slice_in_dim(v_grid, b0, nb, axis=2).reshape(b, NA_HEADS, nloc, NA_HD)
        q_row = r0 + lq // GRID_W
        k_row = b0 + lk // GRID_W
        row_start = jnp.clip(q_row - kh // 2, 0, rows - kh)
        row_ok = (k_row[None] >= row_start[:, None]) & (k_row[None] < row_start[:, None] + kh)
        dr = jnp.clip(k_row[None] - q_row[:, None] + WIN_H - 1, 0, 2 * WIN_H - 2)
        s_loc = jnp.einsum('bhqd,bhkd->bhqk', qb, kb, preferred_element_type=jnp.float32)
        s_loc = jnp.where(row_ok & col_ok, s_loc + rpb[:, dr, dc].astype(jnp.float32), -jnp.inf)
        s_ctx = jnp.einsum('bhqd,bhkd->bhqk', qb, kc, preferred_element_type=jnp.float32)
        p = jax.nn.softmax(jnp.concatenate([s_loc, s_ctx], axis=-1), axis=-1).astype(vb.dtype)
        return (jnp.einsum('bhqk,bhkd->bhqd', p[..., :nloc], vb)
                + jnp.einsum('bhqk,bhkd->bhqd', p[..., nloc:], vc))

    o = lax.map(attend, (q_blocks, row0))
    o = jnp.moveaxis(o, 0, 2).reshape(b, NA_HEADS, t, NA_HD)
    y_lat = merge_heads(o) @ w_out
    y_ctx = None
    if with_ctx_out:
        pc = jax.nn.softmax(jnp.einsum('bhqd,bhkd->bhqk', qc, kc, preferred_element_type=jnp.float32),
                            axis=-1).astype(vc.dtype)
        y_ctx = merge_heads(jnp.einsum('bhqk,bhkd->bhqd', pc, vc)) @ w_out
    return y_lat, y_ctx


def setup_inputs(seed: int = 0) -> dict:
    key = jax.random.key(seed)
    keys = iter(jax.random.split(key, 32))

    def normal(shape, std):
        return jax.random.normal(next(keys), shape, jnp.float32) * std

    D = D_MODEL
    x = normal((BATCH, SEQ, D), 1.0)
    c = normal((BATCH, D), 1.0)
    ctx = normal((BATCH, CTX_LEN, D), 1.0)
    c_ctx = normal((D,), 1.0)
    ada_w = normal((DEPTH, D, N_MOD * D), 0.5 * D ** -0.5)
    ada_b = normal((DEPTH, N_MOD * D), 0.02)
    norm_g = 1.0 + normal((DEPTH, 4, D), 0.05)
    mlp_w1 = normal((DEPTH, D, D_FF), D ** -0.5)
    mlp_w2 = normal((DEPTH, D_FF, D), D_FF ** -0.5)
    ret_w_in = normal((N_RET_LAYERS, D, 2 * RET_QK + 2 * RET_V), D ** -0.5)
    ret_w_out = normal((N_RET_LAYERS, RET_V, D), RET_V ** -0.5)
    gdn_w_in = normal((N_GDN_LAYERS, D, GDN_CONV_DIM + GDN_V_DIM), D ** -0.5)
    gdn_conv_w = normal((N_GDN_LAYERS, GDN_CONV, GDN_CONV_DIM), GDN_CONV ** -0.5)
    gdn_w_ab = normal((N_GDN_LAYERS, 2, D, 2 * GDN_V_HEADS), D ** -0.5)
    gdn_a_log = jnp.log(jax.random.uniform(next(keys), (N_GDN_LAYERS, 2, GDN_V_HEADS),
                                           jnp.float32, 1.0, 16.0))
    dt = jnp.exp(jax.random.uniform(next(keys), (N_GDN_LAYERS, 2, GDN_V_HEADS), jnp.float32,
                                    math.log(1e-3), math.log(1e-1)))
    gdn_dt_bias = dt + jnp.log(-jnp.expm1(-dt))
    gdn_norm_w = 1.0 + normal((N_GDN_LAYERS, GDN_DV), 0.05)
    gdn_w_out = normal((N_GDN_LAYERS, GDN_V_DIM, D), GDN_V_DIM ** -0.5)
    na_w_in = normal((N_NA_LAYERS, D, 3 * D), D ** -0.5)
    na_w_out = normal((N_NA_LAYERS, D, D), D ** -0.5)
    na_rpb = normal((N_NA_LAYERS, NA_HEADS, 2 * WIN_H - 1, 2 * WIN_W - 1), 0.1)
    return {'x': x, 'c': c, 'ctx': ctx, 'c_ctx': c_ctx, 'ada_w': ada_w, 'ada_b': ada_b,
            'norm_g': norm_g, 'mlp_w1': mlp_w1, 'mlp_w2': mlp_w2,
            'ret_w_in': ret_w_in, 'ret_w_out': ret_w_out,
            'gdn_w_in': gdn_w_in, 'gdn_conv_w': gdn_conv_w, 'gdn_w_ab': gdn_w_ab,
            'gdn_a_log': gdn_a_log, 'gdn_dt_bias': gdn_dt_bias, 'gdn_norm_w': gdn_norm_w,
            'gdn_w_out': gdn_w_out, 'na_w_in': na_w_in, 'na_w_out': na_w_out, 'na_rpb': na_rpb}


def reference(x, c, ctx, c_ctx, ada_w, ada_b, norm_g, mlp_w1, mlp_w2, ret_w_in, ret_w_out,
              gdn_w_in, gdn_conv_w, gdn_w_ab, gdn_a_log, gdn_dt_bias, gdn_norm_w, gdn_w_out,
              na_w_in, na_w_out, na_rpb):
    D = D_MODEL
    s_lat = jax.nn.silu(c)
    s_ctx = jax.nn.silu(c_ctx)
    i_ret = i_gdn = i_na = 0
    for i in range(DEPTH):
        last = i == DEPTH - 1
        m_lat = jnp.split((s_lat @ ada_w[i] + ada_b[i])[:, None, :], N_MOD, axis=-1)
        n_ctx_mod = 2 if last else N_MOD
        m_ctx = jnp.split(s_ctx @ ada_w[i][:, :n_ctx_mod * D] + ada_b[i][:n_ctx_mod * D],
                          n_ctx_mod, axis=-1)
        h_lat = rms_norm(x, norm_g[i, 0]) * (1 + m_lat[1]) + m_lat[0]
        h_ctx = rms_norm(ctx, norm_g[i, 0]) * (1 + m_ctx[1]) + m_ctx[0]
        kind = i % N_MIXERS
        if kind == 0:
            y_lat, y_ctx = retention_mixer(h_lat, h_ctx, ret_w_in[i_ret], ret_w_out[i_ret], not last)
            i_ret += 1
        elif kind == 1:
            y_lat, y_ctx = gdn_mixer(h_lat, h_ctx, gdn_w_in[i_gdn], gdn_conv_w[i_gdn], gdn_w_ab[i_gdn],
                                     gdn_a_log[i_gdn], gdn_dt_bias[i_gdn], gdn_norm_w[i_gdn],
                                     gdn_w_out[i_gdn], not last)
            i_gdn += 1
        else:
            y_lat, y_ctx = na_mixer(h_lat, h_ctx, na_w_in[i_na], na_w_out[i_na], na_rpb[i_na], not last)
            i_na += 1
        x = x + m_lat[2] * rms_norm(y_lat, norm_g[i, 1])
        h = rms_norm(x, norm_g[i, 2]) * (1 + m_lat[4]) + m_lat[3]
        x = x + m_lat[5] * rms_norm(sq_relu_mlp(h, mlp_w1[i], mlp_w2[i]), norm_g[i, 3])
        if not last:
            ctx = ctx + m_ctx[2] * rms_norm(y_ctx, norm_g[i, 1])
            hc = rms_norm(ctx, norm_g[i, 2]) * (1 + m_ctx[4]) + m_ctx[3]
            ctx = ctx + m_ctx[5] * rms_norm(sq_relu_mlp(hc, mlp_w1[i], mlp_w2[i]), norm_g[i, 3])
    return x
```

```python
import math
import os
import numpy as np
import concourse.bass as bass
import concourse.mybir as mybir
from concourse.bass_utils import run_bass_kernel_spmd

F32 = mybir.dt.float32
BF16 = mybir.dt.bfloat16
AF = mybir.ActivationFunctionType
ALU = mybir.AluOpType
AX = mybir.AxisListType

COMPUTE = ("pe", "act", "dve", "pool")
ALLQ = ("pe", "act", "dve", "pool", "sp")
SB_LO = 16512
SB_HI = 229344

D = 2048
KC = 16
NCORE = 8
BPC = 2
CTX = 256
SEQ = 2048
T = CTX + SEQ
NTILE = T // 128
DEPTH = 4
EPS = 1e-6


def _dsize(dt):
    return 4 if dt == F32 else 2


class DSem:
    __slots__ = ("h", "cnt")

    def __init__(self, h):
        self.h = h
        self.cnt = 0


class Buf:
    __slots__ = ("t", "w", "r", "dsem", "name", "wd", "rd", "persist", "excl")

    def __init__(self, t, name, persist=False, excl=False):
        self.excl = excl
        self.t = t
        self.name = name
        self.w = {}
        self.r = {}
        self.dsem = {}
        self.wd = False
        self.rd = False
        self.persist = persist

    def __getitem__(self, k):
        return self.t[k]


class Ins:
    __slots__ = ("fn", "waits", "dwaits", "inc", "dinc")

    def __init__(self, fn):
        self.fn = fn
        self.waits = []
        self.dwaits = []
        self.inc = False
        self.dinc = None


class FW:
    def __init__(self, nc):
        self.nc = nc
        self.q = {e: [] for e in ALLQ}
        self.seen = {e: {} for e in ALLQ}
        self.dseen = {e: {} for e in ALLQ}
        self.bufs = []
        self.free_dsems = {"hw": [], "sw": []}
        self.all_dsems = []
        self.sb_ptr = SB_LO
        self.sb_persist = SB_LO
        self.uid = 0
        self.ps = []
        for i in range(8):
            t = nc.alloc_psum_tensor("psb%d" % i, [128, 512], F32)
            b = Buf(t, "psb%d" % i, persist=True, excl=True)
            self.bufs.append(b)
            self.ps.append(b)

    def sb(self, name, shape, dtype, persist=False):
        n = 1
        for s in shape[1:]:
            n *= s
        nbytes = (n * _dsize(dtype) + 63) // 64 * 64
        off = self.sb_ptr
        self.sb_ptr += nbytes
        assert self.sb_ptr <= SB_HI, "SBUF overflow at %s: %d" % (name, self.sb_ptr - SB_LO)
        self.uid += 1
        t = self.nc.alloc_sbuf_tensor_at("%s_%d" % (name, self.uid), list(shape), dtype, offset=off)
        b = Buf(t, name, persist=persist)
        self.bufs.append(b)
        return b

    def persist_done(self):
        self.sb_persist = self.sb_ptr

    def view(self, ap, name):
        b = Buf(ap, name)
        self.bufs.append(b)
        return b

    def phase(self):
        self.barrier()
        keep = []
        for b in self.bufs:
            if b.persist:
                keep.append(b)
            else:
                for kind, ds in b.dsem.items():
                    self.free_dsems[kind].append(ds)
                b.dsem = {}
        self.bufs = keep
        self.sb_ptr = self.sb_persist

    def mark(self):
        return (self.sb_ptr, len(self.bufs))

    def release_to(self, m):
        self.barrier()
        ptr, nb = m
        for b in self.bufs[nb:]:
            assert not b.persist
            for kind, ds in b.dsem.items():
                self.free_dsems[kind].append(ds)
            b.dsem = {}
        self.bufs = self.bufs[:nb]
        self.sb_ptr = ptr

    def _get_dsem(self, b, kind):
        if kind not in b.dsem:
            if self.free_dsems[kind]:
                b.dsem[kind] = self.free_dsems[kind].pop()
            else:
                h = self.nc.alloc_semaphore("dq%s%d" % (kind, len(self.all_dsems)))
                b.dsem[kind] = DSem(h)
                self.all_dsems.append(b.dsem[kind])
        return b.dsem[kind]

    def _need(self, e, ins, oe, oidx):
        if self.seen[e].get(oe, -1) >= oidx:
            return
        self.seen[e][oe] = oidx
        ins.waits.append((oe, oidx))
        tgt = self.q[oe][oidx]
        assert tgt.dinc is None and tgt.fn is not None
        tgt.inc = True

    def _dneed(self, e, ins, b):
        for ds in b.dsem.values():
            if ds.cnt == 0:
                continue
            key = id(ds)
            val = 16 * ds.cnt
            if self.dseen[e].get(key, 0) >= val:
                continue
            self.dseen[e][key] = val
            ins.dwaits.append((ds.h, val))

    def _track(self, e, ins, idx, reads, writes, is_dma=False):
        for b in reads:
            if b is None:
                continue
            for oe, oidx in b.w.items():
                if oe == e and e == "pe" and not is_dma:
                    continue
                self._need(e, ins, oe, oidx)
            if b.excl:
                for oe, oidx in b.r.items():
                    if oe != e:
                        self._need(e, ins, oe, oidx)
            if b.wd:
                self._dneed(e, ins, b)
        for b in writes:
            if b is None:
                continue
            for oe, oidx in b.r.items():
                if oe == e and not is_dma:
                    continue
                self._need(e, ins, oe, oidx)
            for oe, oidx in b.w.items():
                if oe == e and not is_dma:
                    continue
                self._need(e, ins, oe, oidx)
            if is_dma:
                if b.rd:
                    self._dneed(e, ins, b)
            elif b.wd or b.rd:
                self._dneed(e, ins, b)
        if not is_dma:
            for b in reads:
                if b is None:
                    continue
                if b.r.get(e, -1) < idx:
                    b.r[e] = idx
            for b in writes:
                if b is None:
                    continue
                if b.r:
                    b.r = {}
                    b.w = {e: idx}
                else:
                    b.w[e] = idx
                if b.wd or b.rd:
                    b.wd = False
                    b.rd = False

    def op(self, e, fn, reads=(), writes=()):
        ins = Ins(fn)
        idx = len(self.q[e])
        self._track(e, ins, idx, reads, writes)
        self.q[e].append(ins)
        return ins

    def dma(self, e, out, in_, src=None, dst=None, **kw):
        def fn(eng):
            return eng.dma_start(out=out, in_=in_, **kw)
        ins = Ins(fn)
        idx = len(self.q[e])
        self._track(e, ins, idx, [src], [dst], is_dma=True)
        assert not (src is not None and dst is not None)
        b = dst if dst is not None else src
        ds = self._get_dsem(b, "sw" if e == "pool" else "hw")
        ds.cnt += 1
        ins.dinc = ds.h
        if dst is not None:
            dst.wd = True
            if dst.r:
                dst.r = {}
                dst.w = {}
        if src is not None:
            src.rd = True
        self.q[e].append(ins)
        return ins

    def barrier(self):
        lasts = {}
        for e in COMPUTE:
            for i in range(len(self.q[e]) - 1, -1, -1):
                if self.q[e][i].fn is not None and self.q[e][i].dinc is None:
                    lasts[e] = i
                    break
        for e in ALLQ:
            ins = Ins(None)
            for oe, oidx in lasts.items():
                self._need(e, ins, oe, oidx)
            for b in self.bufs:
                if b.wd or b.rd:
                    self._dneed(e, ins, b)
            self.q[e].append(ins)
        for b in self.bufs:
            b.wd = False
            b.rd = False
            b.w = {}
            b.r = {}

    def emit(self):
        nc = self.nc
        esem = {e: nc.alloc_semaphore("es_" + e) for e in COMPUTE}
        val = {}
        for e in COMPUTE:
            c = 0
            arr = []
            for ins in self.q[e]:
                if ins.inc:
                    c += 1
                arr.append(c)
            val[e] = arr
        q = self.q

        def run(e):
            def body(engine):
                for ins in q[e]:
                    for oe, oidx in ins.waits:
                        engine.wait_ge(esem[oe], val[oe][oidx])
                    for s, v in ins.dwaits:
                        engine.wait_ge(s, v)
                    if ins.fn is None:
                        continue
                    bi = ins.fn(engine)
                    if ins.dinc is not None:
                        bi.then_inc(ins.dinc, 16)
                    elif ins.inc:
                        bi.then_inc(esem[e], 1)
            return body

        with nc.Block() as block:
            block.sync(run("sp"))
            block.tensor(run("pe"))
            block.scalar(run("act"))
            block.vector(run("dve"))
            block.gpsimd(run("pool"))


RET_HEADS = 8
RET_DK = 256
RET_DV = 512
GRID_W = 64


def _rope_tables():
    t = np.arange(SEQ)
    row = (t // GRID_W).astype(np.float32)
    col = (t % GRID_W).astype(np.float32)
    n_pairs = RET_DK // 2
    inv = (10000.0 ** (-np.arange(0, n_pairs, 2, dtype=np.float32) / n_pairs)).astype(np.float32)
    ang = np.concatenate([row[:, None] * inv, col[:, None] * inv], axis=-1).astype(np.float32)
    return np.ascontiguousarray(np.cos(ang).T.astype(np.float32)), np.ascontiguousarray(np.sin(ang).T.astype(np.float32))


def _ret_tables():
    fwd = np.log(1.0 - 2.0 ** (-5.0 - np.arange(RET_HEADS, dtype=np.float64)))
    lg = np.stack([fwd, fwd[::-1]])
    C = 128
    pos = np.arange(C, dtype=np.float64)
    maskT = np.zeros((RET_HEADS * 2, C, C), np.float32)
    xi = np.zeros((C, RET_HEADS * 2), np.float32)
    zeta = np.zeros((C, RET_HEADS * 2), np.float32)
    cdec = np.zeros((RET_HEADS * 2,), np.float64)
    sc = RET_DK ** -0.5
    for h in range(RET_HEADS):
        for d in range(2):
            g = lg[d, h]
            i = h * 2 + d
            m = pos[:, None]
            c = pos[None, :]
            if d == 0:
                maskT[i] = np.where(c >= m, np.exp(g * np.maximum(c - m, 0)), 0.0) * sc
                xi[:, i] = np.exp(g * (pos + 1))
                zeta[:, i] = np.exp(g * (C - 1 - pos)) * sc
            else:
                maskT[i] = np.where(m >= c, np.exp(g * np.maximum(m - c, 0)), 0.0) * sc
                xi[:, i] = np.exp(g * (C - pos))
                zeta[:, i] = np.exp(g * pos) * sc
            cdec[i] = np.exp(g * C)
    return maskT, xi, zeta, cdec


def _gdn_masks():
    c = np.arange(128)[:, None]
    m = np.arange(128)[None, :]
    tiles = [(m < c), (m <= c), (m > c), (m >= c)]
    return np.ascontiguousarray(np.concatenate([t.astype(np.float32) for t in tiles], axis=1))


def _na_bias(rpb):
    out = np.empty((16, 5, 128, 640), np.float32)
    lq = np.arange(128)
    lk = np.arange(640)
    q_col = lq % 64
    k_col = lk % 64
    col_start = np.clip(q_col - 8, 0, 64 - 16)
    col_ok = (k_col[None] >= col_start[:, None]) & (k_col[None] < col_start[:, None] + 16)
    dc = np.clip(k_col[None] - q_col[:, None] + 15, 0, 30)
    for p, i in enumerate((0, 1, 2, 14, 15)):
        lo = min(max(i - 2, 0), 11)
        q_row = 2 * i + lq // 64
        k_row = 2 * lo + lk // 64
        row_start = np.clip(q_row - 4, 0, 32 - 8)
        row_ok = (k_row[None] >= row_start[:, None]) & (k_row[None] < row_start[:, None] + 8)
        dr = np.clip(k_row[None] - q_row[:, None] + 7, 0, 14)
        ok = row_ok & col_ok
        out[:, p] = np.where(ok[None], rpb[:, dr, dc], np.float32(-30000.0))
    return np.ascontiguousarray(out.reshape(16 * 5 * 128, 640))


def _gdn_lvmasks():
    c = np.arange(128)[:, None]
    m = np.arange(128)[None, :]
    out = np.zeros((2, 7, 128, 256), np.float32)
    for lv in range(7):
        b = 1 << lv
        low = ((c // (2 * b)) == (m // (2 * b))) & ((c % (2 * b)) >= b) & ((m % (2 * b)) < b)
        low = low.astype(np.float32)
        out[0, lv, :, 0:128] = low
        out[0, lv, :, 128:256] = low.T
        out[1, lv, :, 0:128] = low.T
        out[1, lv, :, 128:256] = low
    return np.ascontiguousarray(out.reshape(14 * 128, 256))


class Stream:
    def __init__(self, items, load_fn, depth):
        self.items = items
        self.load_fn = load_fn
        self.depth = depth
        self.loaded = 0
        self.res = {}

    def get(self, i):
        while self.loaded < len(self.items) and self.loaded < i + self.depth:
            self.res[self.loaded] = self.load_fn(self.loaded, self.items[self.loaded])
            self.loaded += 1
        r = self.res.pop(i)
        return r


def build_program(layers=(0, 1, 2, 3), dbg=False, stop=None):
    nc = bass.Bass("TRN2", target_bir_lowering=False)
    fw = FW(nc)
    PS = fw.ps

    def din(name, shape, dt=F32):
        return nc.dram_tensor(name, list(shape), dt, kind="ExternalInput").ap()

    def dscr(name, shape, dt):
        return nc.dram_tensor(name, list(shape), dt).ap()

    x_in = din("x", [BPC * SEQ, D])
    ctx_in = din("ctx", [BPC * CTX, D])
    cs_in = din("cs", [3, D])
    ada_w = din("ada_w", [DEPTH * D, 6 * D])
    ada_b = din("ada_b", [DEPTH * 96, 128])
    norm_g = din("norm_g", [DEPTH * 64, 128])
    mlp_w1 = din("mlp_w1", [DEPTH * D, 4 * D])
    mlp_w2 = din("mlp_w2", [DEPTH * 4 * D, D])
    ret_w_in = din("ret_w_in", [2 * D, 6 * D])
    ret_w_out = din("ret_w_out", [2 * 2 * D, D])
    gdn_w_in = din("gdn_w_in", [D, 6 * D])
    gdn_conv_w = din("gdn_conv_w", [5 * 64, 128])
    gdn_w_ab = din("gdn_w_ab", [2 * D, 64])
    gdn_a_log = din("gdn_a_log", [1, 64])
    gdn_dt_bias = din("gdn_dt_bias", [1, 64])
    gdn_norm_w = din("gdn_norm_w", [1, 128])
    gdn_w_out = din("gdn_w_out", [2 * D, D])
    na_w_in = din("na_w_in", [D, 3 * D])
    na_w_out = din("na_w_out", [D, D])
    na_bias = din("na_bias", [16 * 5 * 128, 640])
    gmask_in = din("gdn_masks", [128, 4 * 128])
    glv_in = din("gdn_lvmasks", [14 * 128, 256])
    ident_in = din("ident", [128, 128])
    cos_in = din("ropecos", [128, SEQ])
    sin_in = din("ropesin", [128, SEQ])
    rmask_in = din("ret_maskT", [16 * 128, 128])
    rxi_in = din("ret_xi", [128, 16])
    rzeta_in = din("ret_zeta", [128, 16])
    y_out = nc.dram_tensor("y", [BPC * SEQ, D], F32, kind="ExternalOutput").ap()

    xT = (nc.dram_tensor("xT", [BPC, D, T], F32, kind="ExternalOutput").ap() if dbg else dscr("xT", [BPC, D, T], F32))
    def dscr_dbg0(name, shape, dt):
        if dbg:
            return nc.dram_tensor(name, list(shape), dt, kind="ExternalOutput").ap()
        return dscr(name, shape, dt)
    qT_s = dscr_dbg0("qT_s", [D, T], BF16)
    kT_s = dscr_dbg0("kT_s", [D, T], BF16)
    v_s = dscr("v_s", [T, 2 * D], BF16)
    g_s = dscr("g_s", [T, 2 * D], BF16)
    def dscr_dbg(name, shape, dt):
        if dbg:
            return nc.dram_tensor(name, list(shape), dt, kind="ExternalOutput").ap()
        return dscr(name, shape, dt)
    vT_s = dscr_dbg("vT_s", [2 * D, T], BF16)
    ynT = dscr("ynT", [BPC, 2 * D, T], BF16)
    wo_b = dscr("wo_b", [2 * D, D], BF16)
    w1_b = dscr("w1_b", [D, 4 * D], BF16)
    w2_b = dscr("w2_b", [16, 128, 64, 128], BF16)

    _, _, _, ret_cdec = _ret_tables()

    ident = fw.sb("ident", [128, 128], F32, persist=True)
    identb = fw.sb("identb", [128, 128], BF16, persist=True)
    onesb = fw.sb("onesb", [128, 128], BF16, persist=True)
    ones512 = fw.sb("ones512", [128, 128], BF16, persist=True)
    bvec = fw.sb("bvec", [128, DEPTH * 96], F32, persist=True)
    gvec = fw.sb("gvec", [128, DEPTH * 64], F32, persist=True)
    sT = fw.sb("sT", [128, KC, 3], BF16, persist=True)
    modv = fw.sb("modv", [128, 96, 3], F32, persist=True)
    A1 = fw.sb("A1", [128, KC, 3], F32, persist=True)
    G1 = fw.sb("G1", [128, KC, 3], F32, persist=True)
    A2 = fw.sb("A2", [128, KC, 3], F32, persist=True)
    G2 = fw.sb("G2", [128, KC, 3], F32, persist=True)
    epsc = fw.sb("epsc", [128, 1], F32, persist=True)
    ones1b = fw.sb("ones1b", [128, 128], BF16, persist=True)
    ones1f = fw.sb("ones1f", [128, 128], F32, persist=True)
    fw.persist_done()

    def mm(ps_ap, lhsT, rhs, start, stop, reads, psb):
        fw.op("pe", lambda e: e.matmul(ps_ap, lhsT, rhs, start=start, stop=stop), reads=reads, writes=[psb])

    def rsqrt(o_ap, o_buf, i_ap, i_buf, scale):
        fw.op("act", lambda e: e.activation(out=o_ap, in_=i_ap, func=AF.Sqrt, bias=epsc[:, 0:1], scale=scale),
              reads=[i_buf, epsc], writes=[o_buf])
        fw.op("dve", lambda e: e.reciprocal(o_ap, o_ap), reads=[o_buf], writes=[o_buf])

    def tr(ps_ap, in_ap, id_ap, reads, psb):
        fw.op("pe", lambda e: e.transpose(ps_ap, in_ap, id_ap), reads=reads, writes=[psb])

    fw.dma("sp", ident[:], ident_in, dst=ident)
    fw.op("dve", lambda e: e.tensor_copy(identb[:], ident[:]), reads=[ident], writes=[identb])
    fw.op("dve", lambda e: e.memset(onesb[:], 1.0 / D), writes=[onesb])
    fw.op("dve", lambda e: e.memset(ones512[:], 1.0 / 512), writes=[ones512])
    fw.op("dve", lambda e: e.memset(epsc[:], EPS), writes=[epsc])
    fw.op("dve", lambda e: e.memset(ones1b[:], 1.0), writes=[ones1b])
    fw.op("dve", lambda e: e.memset(ones1f[:], 1.0), writes=[ones1f])
    tmpv = fw.sb("tmpv", [128, 5, 128], F32)
    for i in range(3):
        fw.dma("sp", tmpv[:, i, :], ada_b[i * 128:(i + 1) * 128, :], dst=tmpv)
    for i in range(2):
        fw.dma("sp", tmpv[:, 3 + i, :], norm_g[i * 128:(i + 1) * 128, :], dst=tmpv)
    for i in range(5):
        tr(PS[0][:, i * 128:(i + 1) * 128] if i < 4 else PS[1][:, 0:128], tmpv[:, i, :], ident[:],
           [tmpv, ident], PS[0] if i < 4 else PS[1])
    fw.op("dve", lambda e: e.tensor_copy(bvec[:], PS[0][:, 0:384]), reads=[PS[0]], writes=[bvec])
    fw.op("dve", lambda e: e.tensor_copy(gvec[:, 0:128], PS[0][:, 384:512]), reads=[PS[0]], writes=[gvec])
    fw.op("dve", lambda e: e.tensor_copy(gvec[:, 128:256], PS[1][:, 0:128]), reads=[PS[1]], writes=[gvec])
    cs_sb = fw.sb("cs_sb", [3, D], F32)
    cs_sg = fw.sb("cs_sg", [3, D], F32)
    fw.dma("sp", cs_sb[:], cs_in, dst=cs_sb)
    fw.op("act", lambda e: e.activation(out=cs_sg[:], in_=cs_sb[:], func=AF.Sigmoid), reads=[cs_sb], writes=[cs_sg])
    fw.op("dve", lambda e: e.tensor_tensor(out=cs_sb[:], in0=cs_sb[:], in1=cs_sg[:], op=ALU.mult),
          reads=[cs_sb, cs_sg], writes=[cs_sb])
    for kc in range(KC):
        tr(PS[2][:, kc * 3:kc * 3 + 3], cs_sb[:, kc * 128:(kc + 1) * 128], ident[0:3, 0:3], [cs_sb, ident], PS[2])
    fw.op("dve", lambda e: e.tensor_copy(sT[:].rearrange("p k v -> p (k v)"), PS[2][:, 0:48]), reads=[PS[2]], writes=[sT])
    fw.phase()

    xin_b = [fw.sb("xin%d" % i, [128, D], F32) for i in range(2)]
    xst_b = [fw.sb("xst%d" % i, [128, KC, 128], F32) for i in range(2)]
    n = 0
    for b in range(BPC):
        for tt in range(NTILE):
            xin = xin_b[n % 2]
            xst = xst_b[n % 2]
            src = ctx_in[b * CTX + tt * 128: b * CTX + (tt + 1) * 128, :] if tt < 2 else \
                x_in[b * SEQ + (tt - 2) * 128: b * SEQ + (tt - 1) * 128, :]
            fw.dma("sp", xin[:], src, dst=xin)
            for g4 in range(4):
                psb = PS[(n * 4 + g4) % 8]
                for j in range(4):
                    kc = g4 * 4 + j
                    tr(psb[:, j * 128:(j + 1) * 128], xin[:, kc * 128:(kc + 1) * 128], ident[:], [xin, ident], psb)
                eng = "act" if g4 % 2 == 0 else "dve"
                dst_ap = xst[:, g4 * 4:(g4 + 1) * 4, :].rearrange("p k t -> p (k t)")
                if eng == "act":
                    fw.op("act", lambda e, o=dst_ap, p=psb: e.copy(o, p[:]), reads=[psb], writes=[xst])
                else:
                    fw.op("dve", lambda e, o=dst_ap, p=psb: e.tensor_copy(o, p[:]), reads=[psb], writes=[xst])
            fw.dma("sp", xT[b, :, tt * 128:(tt + 1) * 128].rearrange("(k p) t -> p k t", p=128), xst[:], src=xst)
            n += 1
    fw.phase()

    def adaln(li):
        wb = [fw.sb("adw%d" % i, [128, KC, 512], BF16) for i in range(3)]
        W = ada_w[li * D:(li + 1) * D, :]

        def load(i, cb):
            buf = wb[i % 3]
            fw.dma("pool", buf[:], W[:, cb * 512:(cb + 1) * 512].rearrange("(k p) n -> p k n", p=128), dst=buf)
            return buf
        st = Stream(list(range(24)), load, 3)
        for cb in range(24):
            buf = st.get(cb)
            psb = PS[cb % 2]
            for j in range(4):
                for kc in range(KC):
                    mm(psb[:, j * 4:j * 4 + 3], buf[:, kc, j * 128:(j + 1) * 128], sT[:, kc, :], kc == 0, kc == KC - 1,
                       [buf, sT], psb)
            for j in range(4):
                fc = cb * 4 + j
                fw.op("dve", lambda e, fc=fc, j=j, psb=psb: e.tensor_scalar(
                    out=modv[:, fc, :], in0=psb[:, j * 4:j * 4 + 3], scalar1=bvec[:, li * 96 + fc:li * 96 + fc + 1],
                    scalar2=None, op0=ALU.add), reads=[psb, bvec], writes=[modv])
        g0 = li * 64
        for c in range(KC):
            fw.op("dve", lambda e, c=c: e.tensor_scalar(out=A1[:, c, :], in0=modv[:, 16 + c, :], scalar1=1.0,
                                                         scalar2=gvec[:, g0 + c:g0 + c + 1], op0=ALU.add, op1=ALU.mult),
                  reads=[modv, gvec], writes=[A1])
            fw.op("dve", lambda e, c=c: e.tensor_scalar(out=G1[:, c, :], in0=modv[:, 32 + c, :],
                                                         scalar1=gvec[:, g0 + 16 + c:g0 + 16 + c + 1], scalar2=None, op0=ALU.mult),
                  reads=[modv, gvec], writes=[G1])
            fw.op("dve", lambda e, c=c: e.tensor_scalar(out=A2[:, c, :], in0=modv[:, 64 + c, :], scalar1=1.0,
                                                         scalar2=gvec[:, g0 + 32 + c:g0 + 32 + c + 1], op0=ALU.add, op1=ALU.mult),
                  reads=[modv, gvec], writes=[A2])
            fw.op("dve", lambda e, c=c: e.tensor_scalar(out=G2[:, c, :], in0=modv[:, 80 + c, :],
                                                         scalar1=gvec[:, g0 + 48 + c:g0 + 48 + c + 1], scalar2=None, op0=ALU.mult),
                  reads=[modv, gvec], writes=[G2])
        fw.phase()

    def precast(li, w_out_ap, KO):
        dmy = fw.sb("pcdummy", [128, 8], F32)
        rr_ = KO * 128 // 4
        for i in range(4):
            fw.dma("pool", wo_b[i * rr_:(i + 1) * rr_, :], w_out_ap[i * rr_:(i + 1) * rr_, :], dst=dmy)
        W1 = mlp_w1[li * D:(li + 1) * D, :]
        for i in range(4):
            fw.dma("pool", w1_b[i * 512:(i + 1) * 512, :], W1[i * 512:(i + 1) * 512, :], dst=dmy)
        W2 = mlp_w2[li * 4 * D:(li + 1) * 4 * D, :]
        for fc in range(16):
            fw.dma("pool", w2_b[fc], W2[:, fc * 128:(fc + 1) * 128].rearrange("(k p) c -> p k c", p=128), dst=dmy)
        return dmy

    def norm_block(xblk, nt, Avec, Bcol_base, v, out_fn, sqb, tmpb, rstd, psb):
        for kc in range(KC):
            fw.op("act", lambda e, kc=kc: e.activation(out=sqb[:, kc % 2, 0:nt], in_=xblk[:, kc, 0:nt], func=AF.Square),
                  reads=[xblk], writes=[sqb])
            mm(psb[:, 0:nt], onesb[:], sqb[:, kc % 2, 0:nt], kc == 0, kc == KC - 1, [onesb, sqb], psb)
        rsqrt(rstd[:, 0:nt], rstd, psb[:, 0:nt], psb, 1.0)
        for c in range(KC):
            fw.op("dve", lambda e, c=c: e.tensor_tensor(out=tmpb[:, c % 2, 0:nt], in0=xblk[:, c, 0:nt], in1=rstd[:, 0:nt],
                                                        op=ALU.mult), reads=[xblk, rstd], writes=[tmpb])
            o_ap, o_buf = out_fn(c)
            fw.op("act", lambda e, c=c, o_ap=o_ap: e.activation(
                out=o_ap, in_=tmpb[:, c % 2, 0:nt], func=AF.Identity,
                bias=modv[:, Bcol_base + c, v:v + 1], scale=Avec[:, c, v:v + 1]),
                reads=[tmpb, modv, Avec], writes=[o_buf])

    def ret_inproj(b, W, hT, last):
        wb = [fw.sb("wi%d" % i, [128, KC, 512], BF16) for i in range(2)]
        cosT = fw.sb("cosT", [128, SEQ], F32)
        sinT = fw.sb("sinT", [128, SEQ], F32)
        fw.dma("sp", cosT[:], cos_in, dst=cosT)
        fw.dma("sp", sinT[:], sin_in, dst=sinT)
        rt = [fw.sb("rt%d" % i, [128, 512], F32) for i in range(4)]
        qst = [fw.sb("qst%d" % i, [128, 2, 512], BF16) for i in range(2)]
        vst = [fw.sb("vst%d" % i, [128, 512], BF16) for i in range(3)]

        def load(i, cb):
            buf = wb[i % 2]
            fw.dma("pool", buf[:], W[:, cb * 512:(cb + 1) * 512].rearrange("(k p) n -> p k n", p=128), dst=buf)
            return buf
        st = Stream(list(range(24)), load, 2)
        tblocks = [(0, CTX)] + [(CTX + 512 * j, 512) for j in range(4)]
        nq = 0
        nv = 0
        npb = 0
        for cb in range(24):
            buf = st.get(cb)
            if cb < 8:
                dst_s = qT_s if cb < 4 else kT_s
                for hh in range(2):
                    row0 = (cb % 4) * 512 + hh * 256
                    for (t0, nt) in tblocks:
                        p1 = PS[npb % 8]
                        p2 = PS[(npb + 1) % 8]
                        npb += 2
                        for half, pb in ((0, p1), (1, p2)):
                            c0 = hh * 256 + half * 128
                            for kc in range(KC):
                                mm(pb[:, 0:nt], buf[:, kc, c0:c0 + 128], hT[:, kc, t0:t0 + nt], kc == 0, kc == KC - 1,
                                   [buf, hT], pb)
                        qs = qst[nq % 2]
                        nq += 1
                        if t0 < CTX:
                            fw.op("act", lambda e, qs=qs, p1=p1, nt=nt: e.copy(qs[:, 0, 0:nt], p1[:, 0:nt]), reads=[p1], writes=[qs])
                            fw.op("act", lambda e, qs=qs, p2=p2, nt=nt: e.copy(qs[:, 1, 0:nt], p2[:, 0:nt]), reads=[p2], writes=[qs])
                        else:
                            l0 = t0 - CTX
                            cs_ = cosT[:, l0:l0 + nt]
                            sn_ = sinT[:, l0:l0 + nt]
                            fw.op("dve", lambda e, p1=p1, cs_=cs_: e.tensor_tensor(out=rt[0][:], in0=p1[:], in1=cs_, op=ALU.mult),
                                  reads=[p1, cosT], writes=[rt[0]])
                            fw.op("dve", lambda e, p2=p2, sn_=sn_: e.tensor_tensor(out=rt[1][:], in0=p2[:], in1=sn_, op=ALU.mult),
                                  reads=[p2, sinT], writes=[rt[1]])
                            fw.op("dve", lambda e, p1=p1, sn_=sn_: e.tensor_tensor(out=rt[2][:], in0=p1[:], in1=sn_, op=ALU.mult),
                                  reads=[p1, sinT], writes=[rt[2]])
                            fw.op("dve", lambda e, p2=p2, cs_=cs_: e.tensor_tensor(out=rt[3][:], in0=p2[:], in1=cs_, op=ALU.mult),
                                  reads=[p2, cosT], writes=[rt[3]])
                            fw.op("pool", lambda e, qs=qs: e.tensor_tensor(out=qs[:, 0, :], in0=rt[0][:], in1=rt[1][:], op=ALU.subtract),
                                  reads=[rt[0], rt[1]], writes=[qs])
                            fw.op("pool", lambda e, qs=qs: e.tensor_tensor(out=qs[:, 1, :], in0=rt[2][:], in1=rt[3][:], op=ALU.add),
                                  reads=[rt[2], rt[3]], writes=[qs])
                        fw.dma("sp", dst_s[row0:row0 + 256, t0:t0 + nt].rearrange("(h p) t -> p h t", p=128),
                               qs[:, :, 0:nt], src=qs)
            else:
                dst_s = v_s if cb < 16 else g_s
                col0 = ((cb - 8) % 8) * 512
                for tt in range(NTILE):
                    if last and tt < 2 and cb >= 16:
                        continue
                    pb = PS[npb % 8]
                    npb += 1
                    for kc in range(KC):
                        mm(pb[:], hT[:, kc, tt * 128:(tt + 1) * 128], buf[:, kc, :], kc == 0, kc == KC - 1, [buf, hT], pb)
                    vs = vst[nv % 3]
                    nv += 1
                    if cb < 16:
                        if nv % 2 == 0:
                            fw.op("act", lambda e, vs=vs, pb=pb: e.copy(vs[:], pb[:]), reads=[pb], writes=[vs])
                        else:
                            fw.op("dve", lambda e, vs=vs, pb=pb: e.tensor_copy(vs[:], pb[:]), reads=[pb], writes=[vs])
                    else:
                        fw.op("act", lambda e, vs=vs, pb=pb: e.activation(out=vs[:], in_=pb[:], func=AF.Silu), reads=[pb], writes=[vs])
                    fw.dma("sp", dst_s[tt * 128:(tt + 1) * 128, col0:col0 + 512], vs[:], src=vs)

    def ret_scan(b, last):
        maskT = fw.sb("rmask", [128, 16, 128], F32)
        xi = fw.sb("rxi", [128, 16], F32)
        zeta = fw.sb("rzeta", [128, 16], F32)
        fw.dma("sp", maskT[:], rmask_in.rearrange("(i m) c -> m i c", m=128), dst=maskT)
        fw.dma("sp", xi[:], rxi_in, dst=xi)
        fw.dma("sp", zeta[:], rzeta_in, dst=zeta)
        qh = [fw.sb("qh%d" % i, [128, 2, T], BF16) for i in range(2)]
        kh = [fw.sb("kh%d" % i, [128, 2, T], BF16) for i in range(2)]
        vh = [fw.sb("vh%d" % i, [128, NTILE, 512], BF16) for i in range(2)]
        gh = [fw.sb("gh%d" % i, [128, NTILE, 512], BF16) for i in range(2)]
        oacc = fw.sb("oacc", [128, NTILE, 512], F32)
        yst = fw.sb("yst", [128, 4, T], BF16)
        R = fw.sb("R", [128, 2, 512], F32)
        Rb = fw.sb("Rb", [128, 2, 512], BF16)
        AT = [fw.sb("AT%d" % i, [128, 128], BF16) for i in range(2)]
        kz = [fw.sb("kz%d" % i, [128, 256], BF16) for i in range(2)]
        junk = [fw.sb("junk%d" % i, [128, 512], F32) for i in range(2)]
        ss = fw.sb("ss", [128, NTILE], F32)
        rstd = fw.sb("rstdh", [128, NTILE], F32)
        yn = [fw.sb("yn%d" % i, [128, 512], BF16) for i in range(2)]

        def load(i, h):
            fw.dma("sp", qh[i % 2][:], qT_s[h * 256:(h + 1) * 256, :].rearrange("(c p) t -> p c t", p=128), dst=qh[i % 2])
            fw.dma("sp", kh[i % 2][:], kT_s[h * 256:(h + 1) * 256, :].rearrange("(c p) t -> p c t", p=128), dst=kh[i % 2])
            fw.dma("sp", vh[i % 2][:], v_s[:, h * 512:(h + 1) * 512].rearrange("(j p) e -> p j e", p=128), dst=vh[i % 2])
            fw.dma("sp", gh[i % 2][:], g_s[:, h * 512:(h + 1) * 512].rearrange("(j p) e -> p j e", p=128), dst=gh[i % 2])
            return (qh[i % 2], kh[i % 2], vh[i % 2], gh[i % 2])
        st = Stream(list(range(RET_HEADS)), load, 2)
        nstep = 0
        for h in range(RET_HEADS):
            q_, k_, v_, g_ = st.get(h)
            for d in range(2):
                hd = h * 2 + d
                order = list(range(NTILE)) if d == 0 else [1, 0] + list(range(NTILE - 1, 1, -1))
                for si, j in enumerate(order):
                    tsl = slice(j * 128, (j + 1) * 128)
                    need_out = not (last and j < 2)
                    first = si == 0
                    lastst = si == len(order) - 1
                    p_s = PS[0 + nstep % 2]
                    p_o = PS[2 + nstep % 2]
                    p_i = PS[4 + nstep % 2]
                    p_r = (PS[6], PS[7])
                    p_k = PS[0 + nstep % 2]
                    at = AT[nstep % 2]
                    kzz = kz[nstep % 2]
                    nstep += 1
                    if need_out:
                        for dc in range(2):
                            mm(p_s[:, 0:128], k_[:, dc, tsl], q_[:, dc, tsl], dc == 0, dc == 1, [k_, q_], p_s)
                        fw.op("dve", lambda e, at=at, p_s=p_s, hd=hd: e.tensor_tensor(out=at[:], in0=p_s[:, 0:128], in1=maskT[:, hd, :], op=ALU.mult),
                              reads=[p_s, maskT], writes=[at])
                        mm(p_o[:], at[:], v_[:, j, :], True, True, [at, v_], p_o)
                        if not first:
                            for dc in range(2):
                                mm(p_i[:], q_[:, dc, tsl], Rb[:, dc, :], dc == 0, dc == 1, [q_, Rb], p_i)
                        if d == 0:
                            fw.op("act", lambda e, j=j, p_o=p_o: e.copy(oacc[:, j, :], p_o[:]), reads=[p_o], writes=[oacc])
                        else:
                            fw.op("dve", lambda e, j=j, p_o=p_o: e.tensor_tensor(out=oacc[:, j, :], in0=p_o[:], in1=oacc[:, j, :], op=ALU.add),
                                  reads=[p_o, oacc], writes=[oacc])
                        if not first:
                            fw.op("dve", lambda e, j=j, p_i=p_i, hd=hd: e.scalar_tensor_tensor(
                                out=oacc[:, j, :], in0=p_i[:], scalar=xi[:, hd:hd + 1], in1=oacc[:, j, :], op0=ALU.mult, op1=ALU.add),
                                reads=[p_i, xi, oacc], writes=[oacc])
                    if not lastst:
                        pkv = p_k[:].bitcast(BF16)
                        for dc in range(2):
                            tr(pkv[:, 512 + dc * 128:512 + (dc + 1) * 128], k_[:, dc, tsl], identb[:], [k_, identb], p_k)
                        fw.op("act", lambda e, kzz=kzz, pkv=pkv, hd=hd: e.activation(out=kzz[:], in_=pkv[:, 512:768], func=AF.Identity,
                                                                                     scale=zeta[:, hd:hd + 1]),
                              reads=[p_k, zeta], writes=[kzz])
                        for dc in range(2):
                            mm(p_r[dc][:], kzz[:, dc * 128:(dc + 1) * 128], v_[:, j, :], True, True, [kzz, v_], p_r[dc])
                        for dc in range(2):
                            if first:
                                fw.op("dve", lambda e, dc=dc: e.tensor_copy(R[:, dc, :], p_r[dc][:]), reads=[p_r[dc]], writes=[R])
                            else:
                                fw.op("dve", lambda e, dc=dc, hd=hd: e.scalar_tensor_tensor(
                                    out=R[:, dc, :], in0=R[:, dc, :], scalar=float(ret_cdec[hd]), in1=p_r[dc][:], op0=ALU.mult, op1=ALU.add),
                                    reads=[R, p_r[dc]], writes=[R])
                        fw.op("pool", lambda e: e.tensor_copy(Rb[:], R[:]), reads=[R], writes=[Rb])
            j0 = 2 if last else 0
            for j in range(j0, NTILE):
                jk = junk[j % 2]
                fw.op("act", lambda e, j=j, jk=jk: e.activation(out=jk[:], in_=oacc[:, j, :], func=AF.Square),
                      reads=[oacc], writes=[jk])
                fw.op("dve", lambda e, j=j, jk=jk: e.tensor_reduce(out=ss[:, j:j + 1], in_=jk[:], axis=AX.X, op=ALU.add),
                      reads=[jk], writes=[ss])
            rsqrt(rstd[:, j0:NTILE], rstd, ss[:, j0:NTILE], ss, 1.0 / RET_DV)
            for j in range(j0, NTILE):
                y = yn[j % 2]
                fw.op("dve", lambda e, j=j, y=y, g_=g_: e.scalar_tensor_tensor(out=y[:], in0=oacc[:, j, :], scalar=rstd[:, j:j + 1],
                                                                        in1=g_[:, j, :], op0=ALU.mult, op1=ALU.mult),
                      reads=[oacc, rstd, g_], writes=[y])
                pt = PS[4 + j % 2]
                ptv = pt[:].bitcast(BF16)
                for ec in range(4):
                    tr(ptv[:, ec * 128:(ec + 1) * 128], y[:, ec * 128:(ec + 1) * 128], identb[:], [y, identb], pt)
                fw.op("act", lambda e, j=j, ptv=ptv: e.copy(yst[:, :, j * 128:(j + 1) * 128],
                                                           ptv[:, 0:512].rearrange("p (c t) -> p c t", c=4)),
                      reads=[pt], writes=[yst])
            t0 = j0 * 128
            fw.dma("sp", ynT[b, h * 512:(h + 1) * 512, t0:T].rearrange("(c p) t -> p c t", p=128), yst[:, :, t0:T], src=yst)


    ZL = 2314
    ZN = 2310
    ab_s = dscr("ab_s", [T, 128], F32)

    def gdn_inproj(b, hT):
        W = gdn_w_in
        wb = [fw.sb("wi%d" % i, [128, KC, 512], BF16) for i in range(2)]
        wab = fw.sb("wab", [128, KC, 128], BF16)
        for d in range(2):
            fw.dma("pool", wab[:, :, d * 64:(d + 1) * 64], gdn_w_ab[d * D:(d + 1) * D, :].rearrange("(k p) n -> p k n", p=128), dst=wab)
        cw = fw.sb("cw", [128, 320], F32)
        tmpc = fw.sb("tmpc", [128, 3, 128], F32)
        for i in range(3):
            n = 128 if i < 2 else 64
            fw.dma("sp", tmpc[0:n, i, :], gdn_conv_w[i * 128:i * 128 + n, :], dst=tmpc)
        for i in range(3):
            n = 128 if i < 2 else 64
            tr(PS[0][:, i * 128:i * 128 + n], tmpc[0:n, i, :], ident[0:n, 0:n], [tmpc, ident], PS[0])
        fw.op("dve", lambda e: e.tensor_copy(cw[:], PS[0][:, 0:320]), reads=[PS[0]], writes=[cw])
        zc = [fw.sb("zc%d" % i, [128, ZL], F32) for i in range(2)]
        for z in zc:
            fw.op("pool", lambda e, z=z: e.memset(z[:], 0.0), writes=[z])
        acc = fw.sb("acc", [128, ZN], F32)
        sl = fw.sb("sl", [128, ZN], F32)
        sqq = fw.sb("sqq", [128, ZN], BF16)
        rs = fw.sb("rs", [128, ZN], F32)
        ob = [fw.sb("ob%d" % i, [128, ZN], BF16) for i in range(2)]
        vst = [fw.sb("vst%d" % i, [128, 512], BF16) for i in range(3)]
        abst = [fw.sb("abst%d" % i, [128, 128], F32) for i in range(2)]

        def load(i, cb):
            buf = wb[i % 2]
            fw.dma("pool", buf[:], W[:, cb * 512:(cb + 1) * 512].rearrange("(k p) n -> p k n", p=128), dst=buf)
            return buf
        st = Stream(list(range(24)), load, 2)
        tblocks = [(0, CTX, 2)] + [(CTX + 512 * j, 512, 262 + 512 * j) for j in range(4)]
        npb = 0
        nv = 0
        ncc = 0
        for tt in range(NTILE):
            pb = PS[npb % 8]
            npb += 1
            for kc in range(KC):
                mm(pb[:, 0:128], hT[:, kc, tt * 128:(tt + 1) * 128], wab[:, kc, :], kc == 0, kc == KC - 1, [wab, hT], pb)
            a_ = abst[tt % 2]
            fw.op("act", lambda e, a_=a_, pb=pb: e.copy(a_[:], pb[:, 0:128]), reads=[pb], writes=[a_])
            fw.dma("sp", ab_s[tt * 128:(tt + 1) * 128, :], a_[:], src=a_)
        for cb in range(24):
            buf = st.get(cb)
            if cb < 16:
                for jj in range(4):
                    cc = cb * 4 + jj
                    z = zc[ncc % 2]
                    o_ = ob[ncc % 2]
                    ncc += 1
                    for (t0, nt, zo) in tblocks:
                        pb = PS[npb % 8]
                        npb += 1
                        for kc in range(KC):
                            mm(pb[:, 0:nt], buf[:, kc, jj * 128:(jj + 1) * 128], hT[:, kc, t0:t0 + nt], kc == 0, kc == KC - 1,
                               [buf, hT], pb)
                        fw.op("act", lambda e, z=z, pb=pb, zo=zo, nt=nt: e.copy(z[:, zo:zo + nt], pb[:, 0:nt]), reads=[pb], writes=[z])
                    fw.op("dve", lambda e, z=z, cc=cc: e.tensor_scalar(out=acc[:], in0=z[:, 0:ZN], scalar1=cw[:, cc:cc + 1], scalar2=None,
                                                                       op0=ALU.mult), reads=[z, cw], writes=[acc])
                    for j in range(1, 5):
                        fw.op("dve", lambda e, z=z, cc=cc, j=j: e.scalar_tensor_tensor(
                            out=acc[:], in0=z[:, j:j + ZN], scalar=cw[:, j * 64 + cc:j * 64 + cc + 1], in1=acc[:], op0=ALU.mult, op1=ALU.add),
                            reads=[z, cw, acc], writes=[acc])
                    if cc >= 32:
                        fw.op("act", lambda e, o_=o_: e.activation(out=o_[:], in_=acc[:], func=AF.Silu), reads=[acc], writes=[o_])
                        dst_s, r0 = vT_s, (cc - 32) * 128
                    else:
                        fw.op("act", lambda e: e.activation(out=sl[:], in_=acc[:], func=AF.Silu), reads=[acc], writes=[sl])
                        fw.op("pool", lambda e: e.tensor_tensor(out=sqq[:], in0=sl[:], in1=sl[:], op=ALU.mult), reads=[sl], writes=[sqq])
                        for c0 in range(0, ZN, 512):
                            n = min(512, ZN - c0)
                            pb = PS[npb % 8]
                            npb += 1
                            mm(pb[:, 0:n], ones1b[:], sqq[:, c0:c0 + n], True, True, [ones1b, sqq], pb)
                            rsqrt(rs[:, c0:c0 + n], rs, pb[:, 0:n], pb, 1.0)
                        qs = 128 ** -0.5 if cc < 16 else 1.0
                        fw.op("dve", lambda e, o_=o_, qs=qs: e.scalar_tensor_tensor(out=o_[:], in0=sl[:], scalar=qs, in1=rs[:],
                                                                                   op0=ALU.mult, op1=ALU.mult), reads=[sl, rs], writes=[o_])
                        dst_s, r0 = (qT_s, cc * 128) if cc < 16 else (kT_s, (cc - 16) * 128)
                    fw.dma("sp", dst_s[r0:r0 + 128, 0:CTX], o_[:, 0:CTX], src=o_)
                    fw.dma("sp", dst_s[r0:r0 + 128, CTX:T], o_[:, 260:260 + SEQ], src=o_)
            else:
                col0 = (cb - 16) * 512
                for tt in range(NTILE):
                    pb = PS[npb % 8]
                    npb += 1
                    for kc in range(KC):
                        mm(pb[:], hT[:, kc, tt * 128:(tt + 1) * 128], buf[:, kc, :], kc == 0, kc == KC - 1, [buf, hT], pb)
                    vs = vst[nv % 3]
                    nv += 1
                    fw.op("act", lambda e, vs=vs, pb=pb: e.activation(out=vs[:], in_=pb[:], func=AF.Silu), reads=[pb], writes=[vs])
                    fw.dma("sp", g_s[tt * 128:(tt + 1) * 128, col0:col0 + 512], vs[:], src=vs)

    def gdn_scan(b):
        BETA = fw.sb("BETA", [128, NTILE, 64], F32)
        GB = fw.sb("GB", [128, NTILE, 64], F32)
        EGB = fw.sb("EGB", [128, NTILE, 64], F32)
        KDEC = fw.sb("KDEC", [128, NTILE, 64], F32)
        BG = fw.sb("BG", [128, NTILE, 64], F32)
        EGL = fw.sb("EGL", [128, NTILE, 64], F32)
        nwb = fw.sb("nwb", [128, 128], F32)
        gm = fw.sb("gm", [128, 4, 128], F32)
        rowt = fw.sb("rowt", [1, 256], F32)
        fw.dma("sp", rowt[0:1, 0:128], gdn_norm_w, dst=rowt)
        fw.dma("sp", rowt[0:1, 128:192], gdn_dt_bias, dst=rowt)
        fw.dma("sp", rowt[0:1, 192:256], gdn_a_log, dst=rowt)
        mm(PS[4][:, 0:256], ones1f[0:1, 0:128], rowt[0:1, :], True, True, [ones1f, rowt], PS[4])
        fw.op("dve", lambda e: e.tensor_copy(nwb[:], PS[4][:, 0:128]), reads=[PS[4]], writes=[nwb])
        fw.dma("sp", gm[:], gmask_in.rearrange("p (i m) -> p i m", i=4), dst=gm)
        m0 = fw.mark()
        abt = fw.sb("abt", [128, NTILE, 128], F32)
        fw.dma("sp", abt[:], ab_s.rearrange("(j p) n -> p j n", p=128), dst=abt)
        dtb = fw.sb("dtb", [128, 64], F32)
        negA = fw.sb("negA", [128, 64], F32)
        fw.op("dve", lambda e: e.tensor_copy(dtb[:], PS[4][:, 128:192]), reads=[PS[4]], writes=[dtb])
        fw.op("dve", lambda e: e.tensor_copy(negA[:], PS[4][:, 192:256]), reads=[PS[4]], writes=[negA])
        fw.op("act", lambda e: e.activation(out=negA[:], in_=negA[:], func=AF.Exp), reads=[negA], writes=[negA])
        fw.op("dve", lambda e: e.tensor_scalar(out=negA[:], in0=negA[:], scalar1=-1.0, scalar2=None, op0=ALU.mult), reads=[negA], writes=[negA])
        G = fw.sb("G", [128, NTILE, 64], F32)
        GS = fw.sb("GS", [128, NTILE, 64], F32)
        if stop == "g1":
            dd = nc.dram_tensor("dbg_abt", list(abt.t.shape[:1]) + [int(np.prod(abt.t.shape[1:]))], F32, kind="ExternalOutput").ap()
            fw.dma("sp", dd, abt[:] if len(abt.t.shape) == 2 else abt[:].rearrange("p j n -> p (j n)"), src=abt)
            dd = nc.dram_tensor("dbg_dtb", list(dtb.t.shape[:1]) + [int(np.prod(dtb.t.shape[1:]))], F32, kind="ExternalOutput").ap()
            fw.dma("sp", dd, dtb[:] if len(dtb.t.shape) == 2 else dtb[:].rearrange("p j n -> p (j n)"), src=dtb)
            dd = nc.dram_tensor("dbg_negA", list(negA.t.shape[:1]) + [int(np.prod(negA.t.shape[1:]))], F32, kind="ExternalOutput").ap()
            fw.dma("sp", dd, negA[:] if len(negA.t.shape) == 2 else negA[:].rearrange("p j n -> p (j n)"), src=negA)
            dd = nc.dram_tensor("dbg_nwb", list(nwb.t.shape[:1]) + [int(np.prod(nwb.t.shape[1:]))], F32, kind="ExternalOutput").ap()
            fw.dma("sp", dd, nwb[:] if len(nwb.t.shape) == 2 else nwb[:].rearrange("p j n -> p (j n)"), src=nwb)
            fw.barrier()
            return
        for tt in range(NTILE):
            for d in range(2):
                fw.op("dve", lambda e, tt=tt, d=d: e.tensor_tensor(out=G[:, tt, d * 32:(d + 1) * 32], in0=abt[:, tt, d * 64:d * 64 + 32],
                                                                   in1=dtb[:, d * 32:(d + 1) * 32], op=ALU.add), reads=[abt, dtb], writes=[G])
                fw.op("dve", lambda e, tt=tt, d=d: e.tensor_copy(BETA[:, tt, d * 32:(d + 1) * 32], abt[:, tt, d * 64 + 32:d * 64 + 64]),
                      reads=[abt], writes=[BETA])
        fw.op("act", lambda e: e.activation(out=G[:], in_=G[:], func=AF.Exp), reads=[G], writes=[G])
        fw.op("act", lambda e: e.activation(out=G[:], in_=G[:], func=AF.Ln, bias=ones1f[:, 0:1]), reads=[G, ones1f], writes=[G])
        fw.op("act", lambda e: e.activation(out=BETA[:], in_=BETA[:], func=AF.Sigmoid), reads=[BETA], writes=[BETA])
        for tt in range(NTILE):
            fw.op("dve", lambda e, tt=tt: e.tensor_tensor(out=G[:, tt, :], in0=G[:, tt, :], in1=negA[:], op=ALU.mult), reads=[G, negA], writes=[G])
        if stop == "g2":
            dd = nc.dram_tensor("dbg_G", list(G.t.shape[:1]) + [int(np.prod(G.t.shape[1:]))], F32, kind="ExternalOutput").ap()
            fw.dma("sp", dd, G[:] if len(G.t.shape) == 2 else G[:].rearrange("p j n -> p (j n)"), src=G)
            dd = nc.dram_tensor("dbg_BETA", list(BETA.t.shape[:1]) + [int(np.prod(BETA.t.shape[1:]))], F32, kind="ExternalOutput").ap()
            fw.dma("sp", dd, BETA[:] if len(BETA.t.shape) == 2 else BETA[:].rearrange("p j n -> p (j n)"), src=BETA)
            fw.barrier()
            return
        for tt in range(NTILE):
            pb = PS[tt % 4]
            for d in range(2):
                tri = gm[:, 3, :] if d == 0 else gm[:, 1, :]
                mm(pb[:, d * 32:(d + 1) * 32], tri, G[:, tt, d * 32:(d + 1) * 32], True, True, [gm, G], pb)
            mm(pb[:, 64:128], ones1f[:], G[:, tt, :], True, True, [ones1f, G], pb)
            fw.op("act", lambda e, tt=tt, pb=pb: e.copy(GB[:, tt, :], pb[:, 0:64]), reads=[pb], writes=[GB])
            fw.op("dve", lambda e, tt=tt, pb=pb: e.tensor_copy(GS[:, tt, :], pb[:, 64:128]), reads=[pb], writes=[GS])
        if stop == "g3":
            dd = nc.dram_tensor("dbg_GB", list(GB.t.shape[:1]) + [int(np.prod(GB.t.shape[1:]))], F32, kind="ExternalOutput").ap()
            fw.dma("sp", dd, GB[:] if len(GB.t.shape) == 2 else GB[:].rearrange("p j n -> p (j n)"), src=GB)
            dd = nc.dram_tensor("dbg_GS", list(GS.t.shape[:1]) + [int(np.prod(GS.t.shape[1:]))], F32, kind="ExternalOutput").ap()
            fw.dma("sp", dd, GS[:] if len(GS.t.shape) == 2 else GS[:].rearrange("p j n -> p (j n)"), src=GS)
            fw.barrier()
            return
        fw.op("act", lambda e: e.activation(out=EGB[:], in_=GB[:], func=AF.Exp), reads=[GB], writes=[EGB])
        fw.op("act", lambda e: e.activation(out=EGL[:], in_=GS[:], func=AF.Exp), reads=[GS], writes=[EGL])
        fw.op("dve", lambda e: e.tensor_tensor(out=KDEC[:], in0=GS[:], in1=GB[:], op=ALU.subtract), reads=[GS, GB], writes=[KDEC])
        fw.op("act", lambda e: e.activation(out=KDEC[:], in_=KDEC[:], func=AF.Exp), reads=[KDEC], writes=[KDEC])
        fw.op("dve", lambda e: e.tensor_tensor(out=BG[:], in0=BETA[:], in1=EGB[:], op=ALU.mult), reads=[BETA, EGB], writes=[BG])
        fw.release_to(m0)
        if stop == "gates":
            for nm, t_ in (("GB", GB), ("BETA", BETA), ("EGL", EGL), ("KDEC", KDEC), ("BG", BG)):
                dd = nc.dram_tensor("dbg_" + nm, [128, NTILE * 64], F32, kind="ExternalOutput").ap()
                fw.dma("sp", dd, t_[:].rearrange("p j n -> p (j n)"), src=t_)
            fw.barrier()
            return

        qh = [fw.sb("gq%d" % i, [128, T], BF16) for i in range(2)]
        kh = [fw.sb("gk%d" % i, [128, T], BF16) for i in range(2)]
        vh = [fw.sb("gv%d" % i, [128, 2, T], BF16) for i in range(2)]
        zh = [fw.sb("gz%d" % i, [128, NTILE, 256], BF16) for i in range(2)]
        oacc = [fw.sb("goacc%d" % j, [128, 256], F32) for j in range(NTILE)]
        yst = fw.sb("gyst", [128, 2, T], BF16)
        NCH = 4
        NW = 3
        U = [[fw.sb("U%d_%d" % (c, w), [128, 128], BF16) for w in range(NW)] for c in range(NCH)]
        WT = [[fw.sb("WT%d_%d" % (c, w), [128, 128], BF16) for w in range(NW)] for c in range(NCH)]
        KP = [[fw.sb("KP%d_%d" % (c, w), [128, 128], BF16) for w in range(NW)] for c in range(NCH)]
        ATT = [[fw.sb("ATT%d_%d" % (c, w), [128, 128], BF16) for w in range(NW)] for c in range(NCH)]
        S = [fw.sb("S%d" % c, [128, 128], F32) for c in range(NCH)]
        Sb = [fw.sb("Sb%d" % c, [128, 128], BF16) for c in range(NCH)]
        kkS = [fw.sb("kkS%d" % i, [128, 128], F32) for i in range(2)]
        qkI = [fw.sb("qkI%d" % i, [128, 128], F32) for i in range(2)]
        dg = [fw.sb("dg%d" % i, [128, 128], F32) for i in range(NCH)]
        tq = [fw.sb("tq%d" % i, [128, 128], F32) for i in range(NCH)]
        Dm = [fw.sb("Dm%d" % i, [128, 128], F32) for i in range(NCH)]
        MM = [fw.sb("MM%d" % i, [128, 256], BF16) for i in range(NCH)]
        Am = [fw.sb("Am%d" % i, [128, 128], BF16) for i in range(NCH)]
        TT = [[fw.sb("TT%d_%d" % (i, k), [128, 256], BF16) for k in range(2)] for i in range(NCH)]
        Cs = [[fw.sb("Cs%d_%d" % (i, k), [128, 256], BF16) for k in range(2)] for i in range(NCH)]
        YY = [fw.sb("YY%d" % i, [128, 256], BF16) for i in range(NCH)]
        rb = [fw.sb("rb%d" % i, [128, 256], BF16) for i in range(NCH)]
        lvm = fw.sb("lvm", [128, 14, 256], BF16)
        lvf = fw.sb("lvf", [128, 14, 256], F32)
        fw.dma("sp", lvf[:], glv_in.rearrange("(i p) n -> p i n", p=128), dst=lvf)
        fw.op("dve", lambda e: e.tensor_copy(lvm[:], lvf[:]), reads=[lvf], writes=[lvm])
        ident2b = fw.sb("ident2b", [128, 256], BF16)
        fw.op("dve", lambda e: e.tensor_copy(ident2b[:, 0:128], identb[:]), reads=[identb], writes=[ident2b])
        fw.op("dve", lambda e: e.tensor_copy(ident2b[:, 128:256], identb[:]), reads=[identb], writes=[ident2b])
        vn = [fw.sb("vn%d" % i, [128, 128], BF16) for i in range(NCH)]
        jk = [fw.sb("gjk%d" % i, [128, 128], F32) for i in range(2)]
        ss = fw.sb("gss", [128, NTILE * 2], F32)
        rstd = fw.sb("grstd", [128, NTILE * 2], F32)
        ynt = [fw.sb("gyn%d" % i, [128, 256], BF16) for i in range(2)]
        tmpn = [fw.sb("gtn%d" % i, [128, 256], F32) for i in range(2)]

        def load(i, hq):
            fw.dma("sp", qh[i % 2][:], qT_s[hq * 128:(hq + 1) * 128, :], dst=qh[i % 2])
            fw.dma("sp", kh[i % 2][:], kT_s[hq * 128:(hq + 1) * 128, :], dst=kh[i % 2])
            fw.dma("sp", vh[i % 2][:], vT_s[hq * 256:(hq + 1) * 256, :].rearrange("(c p) t -> p c t", p=128), dst=vh[i % 2])
            fw.dma("sp", zh[i % 2][:], g_s[:, hq * 256:(hq + 1) * 256].rearrange("(j p) e -> p j e", p=128), dst=zh[i % 2])
            return (qh[i % 2], kh[i % 2], vh[i % 2], zh[i % 2])
        st = Stream(list(range(16)), load, 2)
        rr = [0]
        pbn = [0]
        nsh = [0]

        def nbank():
            pbn[0] += 1
            return PS[2 + pbn[0] % 4]
        pan = [0]

        orders = [list(range(NTILE)), [1, 0] + list(range(NTILE - 1, 1, -1))]

        def prep(hq, q_, k_, v_, si):
            w = si % NW
            cx = []
            for d in range(2):
                j = orders[d][si]
                tsl = slice(j * 128, (j + 1) * 128)
                pa = PS[pan[0] % 2]
                pan[0] += 1
                mm(pa[:, 0:128], k_[:, tsl], k_[:, tsl], True, True, [k_], pa)
                mm(pa[:, 128:256], q_[:, tsl], k_[:, tsl], True, True, [q_, k_], pa)
                pav = pa[:].bitcast(BF16)
                tr(pav[:, 512:640], k_[:, tsl], identb[:], [k_, identb], pa)
                for hv2 in range(2):
                    tr(pav[:, 640 + hv2 * 128:768 + hv2 * 128], v_[:, hv2, tsl], identb[:], [v_, identb], pa)
                ks = kkS[d]
                qi = qkI[d]
                fw.op("dve", lambda e, ks=ks, pa=pa, d=d: e.tensor_tensor(out=ks[:], in0=pa[:, 0:128], in1=gm[:, 2 * d, :], op=ALU.mult),
                      reads=[pa, gm], writes=[ks])
                fw.op("dve", lambda e, qi=qi, pa=pa, d=d: e.tensor_tensor(out=qi[:], in0=pa[:, 128:256], in1=gm[:, 2 * d + 1, :], op=ALU.mult),
                      reads=[pa, gm], writes=[qi])
                for hv2 in range(2):
                    ch = hv2 * 2 + d
                    cx.append((ch, hv2, d, j, d * 32 + hq * 2 + hv2, pa, pav, ks, qi))
            cx.sort()
            for (ch, hv2, d, j, col, pa, pav, ks, qi) in cx:
                gbc = GB[:, j, col:col + 1]
                fw.op("pool", lambda e, ch=ch, gbc=gbc: e.tensor_scalar(out=dg[ch][:], in0=ident[:], scalar1=gbc, scalar2=None, op0=ALU.mult),
                      reads=[ident, GB], writes=[dg[ch]])
            for (ch, hv2, d, j, col, pa, pav, ks, qi) in cx:
                pr = PS[2 + ch]
                mm(pr[:, 0:128], ones1f[:], dg[ch][:], True, True, [ones1f, dg[ch]], pr)
            for (ch, hv2, d, j, col, pa, pav, ks, qi) in cx:
                pr = PS[2 + ch]
                gbc = GB[:, j, col:col + 1]
                fw.op("dve", lambda e, ch=ch, pr=pr, gbc=gbc: e.tensor_scalar(out=tq[ch][:], in0=pr[:, 0:128], scalar1=gbc, scalar2=0.0,
                                                                            op0=ALU.subtract, op1=ALU.max), reads=[pr, GB], writes=[tq[ch]])
            for (ch, hv2, d, j, col, pa, pav, ks, qi) in cx:
                fw.op("act", lambda e, ch=ch: e.activation(out=Dm[ch][:], in_=tq[ch][:], func=AF.Exp, scale=-1.0), reads=[tq[ch]], writes=[Dm[ch]])
            for (ch, hv2, d, j, col, pa, pav, ks, qi) in cx:
                fw.op("dve", lambda e, ch=ch, ks=ks, j=j, col=col: e.scalar_tensor_tensor(
                    out=MM[ch][:, 0:128], in0=Dm[ch][:], scalar=BETA[:, j, col:col + 1], in1=ks[:], op0=ALU.mult, op1=ALU.mult),
                    reads=[Dm[ch], BETA, ks], writes=[MM[ch]])
                fw.op("pool", lambda e, ch=ch, qi=qi: e.tensor_tensor(out=Am[ch][:], in0=Dm[ch][:], in1=qi[:], op=ALU.mult),
                      reads=[Dm[ch], qi], writes=[Am[ch]])
            for (ch, hv2, d, j, col, pa, pav, ks, qi) in cx:
                pr = PS[2 + ch]
                prv = pr[:].bitcast(BF16)
                tr(prv[:, 512:640], MM[ch][:, 0:128], identb[:], [MM[ch], identb], pr)
                tr(prv[:, 640:768], Am[ch][:], identb[:], [Am[ch], identb], pr)
            for (ch, hv2, d, j, col, pa, pav, ks, qi) in cx:
                pr = PS[2 + ch]
                prv = pr[:].bitcast(BF16)
                fw.op("act", lambda e, ch=ch, prv=prv: e.copy(MM[ch][:, 128:256], prv[:, 512:640]), reads=[pr], writes=[MM[ch]])
                att = ATT[ch][w]
                fw.op("act", lambda e, att=att, prv=prv: e.copy(att[:], prv[:, 640:768]), reads=[pr], writes=[att])
                fw.op("dve", lambda e, ch=ch, pav=pav, hv2=hv2, j=j, col=col: e.tensor_scalar(
                    out=rb[ch][:, 0:128], in0=pav[:, 640 + hv2 * 128:768 + hv2 * 128], scalar1=BETA[:, j, col:col + 1], scalar2=None, op0=ALU.mult),
                    reads=[pa, BETA], writes=[rb[ch]])
                fw.op("dve", lambda e, ch=ch, pav=pav, j=j, col=col: e.tensor_scalar(
                    out=rb[ch][:, 128:256], in0=pav[:, 512:640], scalar1=BG[:, j, col:col + 1], scalar2=None, op0=ALU.mult),
                    reads=[pa, BG], writes=[rb[ch]])
                kp = KP[ch][w]
                fw.op("act", lambda e, kp=kp, pav=pav, j=j, col=col: e.activation(out=kp[:], in_=pav[:, 512:640], func=AF.Identity,
                                                                                  scale=KDEC[:, j, col:col + 1]), reads=[pa, KDEC], writes=[kp])
                fw.op("pool", lambda e, ch=ch: e.tensor_copy(TT[ch][0][:], ident2b[:]), reads=[ident2b], writes=[TT[ch][0]])
            for lv in range(7):
                for (ch, hv2, d, j, col, pa, pav, ks, qi) in cx:
                    fw.op("pool", lambda e, ch=ch, d=d, lv=lv: e.tensor_tensor(out=Cs[ch][lv % 2][:], in0=MM[ch][:], in1=lvm[:, d * 7 + lv, :], op=ALU.mult),
                          reads=[MM[ch], lvm], writes=[Cs[ch][lv % 2]])
                for (ch, hv2, d, j, col, pa, pav, ks, qi) in cx:
                    pp = PS[2 + ch]
                    C_ = Cs[ch][lv % 2]
                    Tc = TT[ch][lv % 2]
                    mm(pp[:, 0:128], C_[:, 128:256], Tc[:, 0:128], True, True, [C_, Tc], pp)
                    mm(pp[:, 128:256], C_[:, 0:128], Tc[:, 128:256], True, True, [C_, Tc], pp)
                for (ch, hv2, d, j, col, pa, pav, ks, qi) in cx:
                    pp = PS[2 + ch]
                    fw.op("act", lambda e, ch=ch, pp=pp: e.copy(YY[ch][:], pp[:, 0:256]), reads=[pp], writes=[YY[ch]])
                for (ch, hv2, d, j, col, pa, pav, ks, qi) in cx:
                    pp = PS[2 + ch]
                    Tc = TT[ch][lv % 2]
                    mm(pp[:, 256:384], Tc[:, 128:256], YY[ch][:, 0:128], True, True, [Tc, YY[ch]], pp)
                    mm(pp[:, 384:512], Tc[:, 0:128], YY[ch][:, 128:256], True, True, [Tc, YY[ch]], pp)
                for (ch, hv2, d, j, col, pa, pav, ks, qi) in cx:
                    pp = PS[2 + ch]
                    Tc = TT[ch][lv % 2]
                    Tn = TT[ch][(lv + 1) % 2]
                    fw.op("dve", lambda e, Tc=Tc, Tn=Tn, pp=pp: e.tensor_tensor(out=Tn[:], in0=Tc[:], in1=pp[:, 256:512], op=ALU.subtract),
                          reads=[Tc, pp], writes=[Tn])
            for (ch, hv2, d, j, col, pa, pav, ks, qi) in cx:
                pp = PS[2 + ch]
                Tf = TT[ch][1]
                mm(pp[:, 0:128], Tf[:, 128:256], rb[ch][:, 0:128], True, True, [Tf, rb[ch]], pp)
                mm(pp[:, 128:256], rb[ch][:, 128:256], Tf[:, 128:256], True, True, [Tf, rb[ch]], pp)
            for (ch, hv2, d, j, col, pa, pav, ks, qi) in cx:
                pp = PS[2 + ch]
                u = U[ch][w]
                wt = WT[ch][w]
                fw.op("act", lambda e, u=u, pp=pp: e.copy(u[:], pp[:, 0:128]), reads=[pp], writes=[u])
                fw.op("act", lambda e, wt=wt, pp=pp: e.copy(wt[:], pp[:, 128:256]), reads=[pp], writes=[wt])

        def step(hq, q_, si):
            w = si % NW
            first = si == 0
            lastst = si == NTILE - 1
            info = []
            for ch in range(NCH):
                hv2, d = ch // 2, ch % 2
                hv = hq * 2 + hv2
                j = orders[d][si]
                info.append((ch, hv2, d, d * 32 + hv, j, PS[6 + ch % 2]))
            for (ch, hv2, d, col, j, pz) in info:
                if first:
                    fw.op("dve", lambda e, ch=ch, w=w: e.tensor_copy(vn[ch][:], U[ch][w][:]), reads=[U[ch][w]], writes=[vn[ch]])
                else:
                    c0 = (ch // 2) * 256
                    mm(pz[:, c0:c0 + 128], WT[ch][w][:], Sb[ch][:], True, True, [WT[ch][w], Sb[ch]], pz)
                    mm(pz[:, c0 + 128:c0 + 256], q_[:, j * 128:(j + 1) * 128], Sb[ch][:], True, True, [q_, Sb[ch]], pz)
            for (ch, hv2, d, col, j, pz) in info:
                if not first:
                    c0 = (ch // 2) * 256
                    fw.op("dve", lambda e, ch=ch, w=w, pz=pz, c0=c0: e.tensor_tensor(out=vn[ch][:], in0=U[ch][w][:], in1=pz[:, c0:c0 + 128], op=ALU.subtract),
                          reads=[U[ch][w], pz], writes=[vn[ch]])
            for (ch, hv2, d, col, j, pz) in info:
                osl = oacc[j][:, hv2 * 128:(hv2 + 1) * 128]
                if not first:
                    c0 = (ch // 2) * 256
                    fw.op("dve", lambda e, osl=osl, pz=pz, c0=c0, j=j, col=col: e.scalar_tensor_tensor(
                        out=osl, in0=pz[:, c0 + 128:c0 + 256], scalar=EGB[:, j, col:col + 1], in1=osl, op0=ALU.mult, op1=ALU.add),
                        reads=[pz, EGB, oacc[j]], writes=[oacc[j]])
            for (ch, hv2, d, col, j, pz) in info:
                pq = PS[4 + ch % 2]
                c0 = (ch // 2) * 256
                mm(pq[:, c0:c0 + 128], ATT[ch][w][:], vn[ch][:], True, True, [ATT[ch][w], vn[ch]], pq)
                if not lastst:
                    mm(pq[:, c0 + 128:c0 + 256], KP[ch][w][:], vn[ch][:], True, True, [KP[ch][w], vn[ch]], pq)
            for (ch, hv2, d, col, j, pz) in info:
                pq = PS[4 + ch % 2]
                c0 = (ch // 2) * 256
                osl = oacc[j][:, hv2 * 128:(hv2 + 1) * 128]
                fw.op("dve", lambda e, osl=osl, pq=pq, c0=c0: e.tensor_tensor(out=osl, in0=pq[:, c0:c0 + 128], in1=osl, op=ALU.add),
                      reads=[pq, oacc[j]], writes=[oacc[j]])
                if not lastst:
                    if first:
                        fw.op("dve", lambda e, ch=ch, pq=pq, c0=c0: e.tensor_copy(S[ch][:], pq[:, c0 + 128:c0 + 256]), reads=[pq], writes=[S[ch]])
                    else:
                        fw.op("dve", lambda e, ch=ch, pq=pq, c0=c0, j=j, col=col: e.scalar_tensor_tensor(
                            out=S[ch][:], in0=S[ch][:], scalar=EGL[:, j, col:col + 1], in1=pq[:, c0 + 128:c0 + 256], op0=ALU.mult, op1=ALU.add),
                            reads=[S[ch], EGL, pq], writes=[S[ch]])
                    fw.op("pool", lambda e, ch=ch: e.tensor_copy(Sb[ch][:], S[ch][:]), reads=[S[ch]], writes=[Sb[ch]])

        for hq in range(16):
            q_, k_, v_, z_ = st.get(hq)
            for j in range(NTILE):
                fw.op("pool", lambda e, j=j: e.memset(oacc[j][:], 0.0), writes=[oacc[j]])
            for si in range(NTILE + 1):
                if si < NTILE:
                    prep(hq, q_, k_, v_, si)
                if si >= 1:
                    step(hq, q_, si - 1)
            for j in range(NTILE):
                for hv2 in range(2):
                    jj = jk[(j * 2 + hv2) % 2]
                    fw.op("act", lambda e, j=j, hv2=hv2, jj=jj: e.activation(out=jj[:], in_=oacc[j][:, hv2 * 128:(hv2 + 1) * 128], func=AF.Square),
                          reads=[oacc[j]], writes=[jj])
                    fw.op("dve", lambda e, j=j, hv2=hv2, jj=jj: e.tensor_reduce(out=ss[:, j * 2 + hv2:j * 2 + hv2 + 1], in_=jj[:], axis=AX.X, op=ALU.add),
                          reads=[jj], writes=[ss])
            rsqrt(rstd[:], rstd, ss[:], ss, 1.0 / 128)
            for j in range(NTILE):
                y = ynt[j % 2]
                tn = tmpn[j % 2]
                for hv2 in range(2):
                    fw.op("dve", lambda e, j=j, hv2=hv2, tn=tn: e.scalar_tensor_tensor(
                        out=tn[:, hv2 * 128:(hv2 + 1) * 128], in0=oacc[j][:, hv2 * 128:(hv2 + 1) * 128], scalar=rstd[:, j * 2 + hv2:j * 2 + hv2 + 1],
                        in1=nwb[:], op0=ALU.mult, op1=ALU.mult), reads=[oacc[j], rstd, nwb], writes=[tn])
                fw.op("pool", lambda e, j=j, y=y, tn=tn, z_=z_: e.tensor_tensor(out=y[:], in0=tn[:], in1=z_[:, j, :], op=ALU.mult),
                      reads=[tn, z_], writes=[y])
                pt = PS[6 + j % 2]
                ptv = pt[:].bitcast(BF16)
                for ec in range(2):
                    tr(ptv[:, ec * 128:(ec + 1) * 128], y[:, ec * 128:(ec + 1) * 128], identb[:], [y, identb], pt)
                fw.op("act", lambda e, j=j, ptv=ptv: e.copy(yst[:, :, j * 128:(j + 1) * 128],
                                                           ptv[:, 0:256].rearrange("p (c t) -> p c t", c=2)), reads=[pt], writes=[yst])
            fw.dma("sp", ynT[b, hq * 256:(hq + 1) * 256, :].rearrange("(c p) t -> p c t", p=128), yst[:], src=yst)


    def na_inproj(b, hT):
        wb = [fw.sb("wi%d" % i, [128, KC, 512], BF16) for i in range(2)]
        qst = [fw.sb("nqst%d" % i, [128, 512], BF16) for i in range(3)]

        def load(i, cb):
            buf = wb[i % 2]
            fw.dma("pool", buf[:], na_w_in[:, cb * 512:(cb + 1) * 512].rearrange("(k p) n -> p k n", p=128), dst=buf)
            return buf
        st = Stream(list(range(12)), load, 2)
        tblocks = [(0, CTX)] + [(CTX + 512 * j, 512) for j in range(4)]
        npb = 0
        nq = 0
        for cb in range(12):
            buf = st.get(cb)
            if cb < 8:
                dst_s = qT_s if cb < 4 else kT_s
                for jj in range(4):
                    r0 = (cb % 4) * 512 + jj * 128
                    for (t0, nt) in tblocks:
                        pb = PS[npb % 8]
                        npb += 1
                        for kc in range(KC):
                            mm(pb[:, 0:nt], buf[:, kc, jj * 128:(jj + 1) * 128], hT[:, kc, t0:t0 + nt], kc == 0, kc == KC - 1, [buf, hT], pb)
                        qs = qst[nq % 3]
                        nq += 1
                        if cb < 4:
                            fw.op("act", lambda e, qs=qs, pb=pb, nt=nt: e.activation(out=qs[:, 0:nt], in_=pb[:, 0:nt], func=AF.Identity, scale=128 ** -0.5),
                                  reads=[pb], writes=[qs])
                        else:
                            fw.op("dve", lambda e, qs=qs, pb=pb, nt=nt: e.tensor_copy(qs[:, 0:nt], pb[:, 0:nt]), reads=[pb], writes=[qs])
                        fw.dma("sp", dst_s[r0:r0 + 128, t0:t0 + nt], qs[:, 0:nt], src=qs)
            else:
                col0 = (cb - 8) * 512
                for tt in range(NTILE):
                    pb = PS[npb % 8]
                    npb += 1
                    for kc in range(KC):
                        mm(pb[:], hT[:, kc, tt * 128:(tt + 1) * 128], buf[:, kc, :], kc == 0, kc == KC - 1, [buf, hT], pb)
                    qs = qst[nq % 3]
                    nq += 1
                    if nq % 2 == 0:
                        fw.op("act", lambda e, qs=qs, pb=pb: e.copy(qs[:], pb[:]), reads=[pb], writes=[qs])
                    else:
                        fw.op("dve", lambda e, qs=qs, pb=pb: e.tensor_copy(qs[:], pb[:]), reads=[pb], writes=[qs])
                    fw.dma("sp", v_s[tt * 128:(tt + 1) * 128, col0:col0 + 512], qs[:], src=qs)

    def na_attn(b):
        qh = [fw.sb("nq%d" % i, [128, T], BF16) for i in range(2)]
        kh = [fw.sb("nk%d" % i, [128, T], BF16) for i in range(2)]
        vh = [fw.sb("nv%d" % i, [128, NTILE, 128], BF16) for i in range(2)]
        bs = [fw.sb("nb%d" % i, [128, 5, 640], F32) for i in range(2)]
        yst = [fw.sb("nyst%d" % i, [128, T], BF16) for i in range(2)]
        Ssb = [fw.sb("nS%d" % i, [128, 896], F32) for i in range(2)]
        Pb = [fw.sb("nP%d" % i, [128, 896], BF16) for i in range(2)]
        PT = [fw.sb("nPT%d" % i, [128, 896], BF16) for i in range(2)]
        mx = [fw.sb("nmx%d" % i, [128, 2], F32) for i in range(2)]
        sm = [fw.sb("nsm%d" % i, [128, 2], F32) for i in range(2)]
        on = [fw.sb("non%d" % i, [128, 128], BF16) for i in range(2)]

        def load(i, h):
            fw.dma("sp", qh[i % 2][:], qT_s[h * 128:(h + 1) * 128, :], dst=qh[i % 2])
            fw.dma("sp", kh[i % 2][:], kT_s[h * 128:(h + 1) * 128, :], dst=kh[i % 2])
            fw.dma("sp", vh[i % 2][:], v_s[:, h * 128:(h + 1) * 128].rearrange("(j p) e -> p j e", p=128), dst=vh[i % 2])
            fw.dma("sp", bs[i % 2][:], na_bias[h * 640:(h + 1) * 640, :].rearrange("(i p) n -> p i n", p=128), dst=bs[i % 2])
            return (qh[i % 2], kh[i % 2], vh[i % 2], bs[i % 2], yst[i % 2])
        st = Stream(list(range(16)), load, 2)
        nblk = 0
        for h in range(16):
            q_, k_, v_, b_, y_ = st.get(h)
            for qt in range(NTILE):
                kx = nblk % 2
                nblk += 1
                SA, SB, PTp, OP = PS[2 * kx], PS[2 * kx + 1], PS[4 + kx], PS[6 + kx]
                S_, P_, PT_, mx_, sm_, on_ = Ssb[kx], Pb[kx], PT[kx], mx[kx], sm[kx], on[kx]
                qsl = q_[:, qt * 128:(qt + 1) * 128]
                if qt < 2:
                    nk = 256
                    ktiles = [0, 1]
                    mm(SB[:, 128:384], qsl, k_[:, 0:256], True, True, [q_, k_], SB)
                    fw.op("act", lambda e, S_=S_, SB=SB: e.copy(S_[:, 0:256], SB[:, 128:384]), reads=[SB], writes=[S_])
                else:
                    i = qt - 2
                    lo = min(max(i - 2, 0), 11)
                    pat = 0 if i == 0 else 1 if i == 1 else 3 if i == 14 else 4 if i == 15 else 2
                    nk = 896
                    ktiles = [2 + lo + t for t in range(5)] + [0, 1]
                    kc0 = CTX + lo * 128
                    mm(SA[:, 0:512], qsl, k_[:, kc0:kc0 + 512], True, True, [q_, k_], SA)
                    mm(SB[:, 0:128], qsl, k_[:, kc0 + 512:kc0 + 640], True, True, [q_, k_], SB)
                    mm(SB[:, 128:384], qsl, k_[:, 0:256], True, True, [q_, k_], SB)
                    fw.op("dve", lambda e, S_=S_, SA=SA, b_=b_, pat=pat: e.tensor_tensor(out=S_[:, 0:512], in0=SA[:, 0:512], in1=b_[:, pat, 0:512], op=ALU.add),
                          reads=[SA, b_], writes=[S_])
                    fw.op("dve", lambda e, S_=S_, SB=SB, b_=b_, pat=pat: e.tensor_tensor(out=S_[:, 512:640], in0=SB[:, 0:128], in1=b_[:, pat, 512:640], op=ALU.add),
                          reads=[SB, b_], writes=[S_])
                    fw.op("act", lambda e, S_=S_, SB=SB: e.copy(S_[:, 640:896], SB[:, 128:384]), reads=[SB], writes=[S_])
                fw.op("dve", lambda e, S_=S_, mx_=mx_, nk=nk: e.tensor_reduce(out=mx_[:, 0:1], in_=S_[:, 0:nk], axis=AX.X, op=ALU.max),
                      reads=[S_], writes=[mx_])
                fw.op("dve", lambda e, mx_=mx_: e.tensor_scalar(out=mx_[:, 1:2], in0=mx_[:, 0:1], scalar1=-1.0, scalar2=None, op0=ALU.mult),
                      reads=[mx_], writes=[mx_])
                fw.op("act", lambda e, S_=S_, P_=P_, mx_=mx_, nk=nk: e.activation(out=P_[:, 0:nk], in_=S_[:, 0:nk], func=AF.Exp, bias=mx_[:, 1:2]),
                      reads=[S_, mx_], writes=[P_])
                fw.op("dve", lambda e, P_=P_, sm_=sm_, nk=nk: e.tensor_reduce(out=sm_[:, 0:1], in_=P_[:, 0:nk], axis=AX.X, op=ALU.add),
                      reads=[P_], writes=[sm_])
                fw.op("dve", lambda e, sm_=sm_: e.reciprocal(sm_[:, 1:2], sm_[:, 0:1]), reads=[sm_], writes=[sm_])
                ptv = PTp[:].bitcast(BF16)
                nkt = len(ktiles)
                for t in range(nkt):
                    tr(ptv[:, t * 128:(t + 1) * 128], P_[:, t * 128:(t + 1) * 128], identb[:], [P_, identb], PTp)
                h1 = (nkt + 1) // 2 * 128
                fw.op("act", lambda e, PT_=PT_, ptv=ptv, h1=h1: e.copy(PT_[:, 0:h1], ptv[:, 0:h1]), reads=[PTp], writes=[PT_])
                if nkt * 128 > h1:
                    fw.op("dve", lambda e, PT_=PT_, ptv=ptv, h1=h1, nk=nk: e.tensor_copy(PT_[:, h1:nk], ptv[:, h1:nk]), reads=[PTp], writes=[PT_])
                for t, kt in enumerate(ktiles):
                    mm(OP[:, 0:128], PT_[:, t * 128:(t + 1) * 128], v_[:, kt, :], t == 0, t == nkt - 1, [PT_, v_], OP)
                fw.op("act", lambda e, on_=on_, OP=OP, sm_=sm_: e.activation(out=on_[:], in_=OP[:, 0:128], func=AF.Identity, scale=sm_[:, 1:2]),
                      reads=[OP, sm_], writes=[on_])
                opv = OP[:].bitcast(BF16)
                tr(opv[:, 512:640], on_[:], identb[:], [on_, identb], OP)
                fw.op("act", lambda e, y_=y_, opv=opv, qt=qt: e.copy(y_[:, qt * 128:(qt + 1) * 128], opv[:, 512:640]), reads=[OP], writes=[y_])
            fw.dma("sp", ynT[b, h * 128:(h + 1) * 128, :], y_[:], src=y_)

    def token_blocks(last):
        blks = []
        if not last:
            blks.append(("ctx", None, 2))
        for b in range(BPC):
            for j in range(4):
                blks.append(("lat", b, b))
        out = []
        jj = {0: 0, 1: 0}
        for kind, b, v in blks:
            if kind == "ctx":
                out.append(([(0, 0, CTX), (1, 0, CTX)], 2))
            else:
                out.append(([(b, CTX + 512 * jj[b], 512)], v))
                jj[b] += 1
        return out

    def outproj(li, last, KO):
        ynb = [fw.sb("ynb%d" % i, [128, KO, 512], BF16) for i in range(2)]
        wob = [fw.sb("wob%d" % i, [128, KO, 512], BF16) for i in range(2)]
        ybuf = fw.sb("ybuf", [128, KC, 512], F32)
        sqc = [fw.sb("sqc%d" % i, [128, 512], BF16) for i in range(2)]
        rstd = fw.sb("rstdo", [128, 512], F32)
        xc = [fw.sb("xc%d" % i, [128, 512], F32) for i in range(4)]
        tm = [fw.sb("tm%d" % i, [128, 512], F32) for i in range(2)]
        blks = token_blocks(last)

        def load_y(i, blk):
            buf = ynb[i % 2]
            segs, v = blk
            o = 0
            for (b, t0, nt) in segs:
                fw.dma("sp", buf[:, :, o:o + nt], ynT[b, 0:KO * 128, t0:t0 + nt].rearrange("(c p) t -> p c t", p=128), dst=buf)
                o += nt
            return buf
        sty = Stream(blks, load_y, 2)
        witems = [(bi, g) for bi in range(len(blks)) for g in range(4)]

        def load_w(i, it):
            buf = wob[i % 2]
            g = it[1]
            fw.dma("sp", buf[:], wo_b[0:KO * 128, g * 512:(g + 1) * 512].rearrange("(c p) n -> p c n", p=128), dst=buf)
            return buf
        stw = Stream(witems, load_w, 2)
        npb = 0
        nx = 0
        for bi, (segs, v) in enumerate(blks):
            yb = sty.get(bi)
            pss = PS[7]
            for g in range(4):
                wbuf = stw.get(bi * 4 + g)
                for j in range(4):
                    fc = g * 4 + j
                    pb = PS[npb % 6]
                    npb += 1
                    for kc in range(KO):
                        mm(pb[:], wbuf[:, kc, j * 128:(j + 1) * 128], yb[:, kc, :], kc == 0, kc == KO - 1, [wbuf, yb], pb)
                    fw.op("act", lambda e, fc=fc, pb=pb: e.copy(ybuf[:, fc, :], pb[:]), reads=[pb], writes=[ybuf])
                    sq = sqc[fc % 2]
                    fw.op("pool", lambda e, fc=fc, sq=sq: e.tensor_tensor(out=sq[:], in0=ybuf[:, fc, :], in1=ybuf[:, fc, :], op=ALU.mult),
                          reads=[ybuf], writes=[sq])
                    mm(pss[:], onesb[:], sq[:], fc == 0, fc == KC - 1, [onesb, sq], pss)
            rsqrt(rstd[:], rstd, pss[:], pss, 1.0)
            for fc in range(KC):
                x_ = xc[nx % 4]
                t_ = tm[nx % 2]
                nx += 1
                o = 0
                for (b, t0, nt) in segs:
                    fw.dma("sp", x_[:, o:o + nt], xT[b, fc * 128:(fc + 1) * 128, t0:t0 + nt], dst=x_)
                    o += nt
                fw.op("pool", lambda e, fc=fc, t_=t_: e.tensor_tensor(out=t_[:], in0=ybuf[:, fc, :], in1=rstd[:], op=ALU.mult),
                      reads=[ybuf, rstd], writes=[t_])
                fw.op("dve", lambda e, fc=fc, t_=t_, x_=x_, v=v: e.scalar_tensor_tensor(
                    out=x_[:], in0=t_[:], scalar=G1[:, fc, v:v + 1], in1=x_[:], op0=ALU.mult, op1=ALU.add),
                    reads=[t_, G1, x_], writes=[x_])
                o = 0
                for (b, t0, nt) in segs:
                    fw.dma("sp", xT[b, fc * 128:(fc + 1) * 128, t0:t0 + nt], x_[:, o:o + nt], src=x_)
                    o += nt

    def mlp(li, last):
        xb = fw.sb("xb", [128, KC, 512], F32)
        h2 = fw.sb("h2", [128, KC, 512], BF16)
        hid = fw.sb("hid", [128, 64, 512], BF16)
        w1b = [fw.sb("w1b%d" % i, [128, KC, 512], BF16) for i in range(2)]
        w2b = [fw.sb("w2b%d" % i, [128, 64, 128], BF16) for i in range(2)]
        sqb = fw.sb("sqb", [128, 2, 512], BF16)
        tmpb = fw.sb("tmpb", [128, 2, 512], F32)
        rstd = fw.sb("rstdm", [128, 512], F32)
        xc = [fw.sb("xcm%d" % i, [128, 512], F32) for i in range(3)]
        blks = token_blocks(last)
        w1items = [(bi, hb) for bi in range(len(blks)) for hb in range(16)]
        w2items = [(bi, fc) for bi in range(len(blks)) for fc in range(16)]

        def load_w1(i, it):
            buf = w1b[i % 2]
            hb = it[1]
            fw.dma("sp", buf[:], w1_b[:, hb * 512:(hb + 1) * 512].rearrange("(c p) n -> p c n", p=128), dst=buf)
            return buf

        def load_w2(i, it):
            buf = w2b[i % 2]
            fw.dma("sp", buf[:], w2_b[it[1]], dst=buf)
            return buf
        st1 = Stream(w1items, load_w1, 2)
        st2 = Stream(w2items, load_w2, 2)
        npb = 0
        nx = 0
        for bi, (segs, v) in enumerate(blks):
            o = 0
            for (b, t0, nt) in segs:
                fw.dma("sp", xb[:, :, o:o + nt], xT[b, :, t0:t0 + nt].rearrange("(c p) t -> p c t", p=128), dst=xb)
                o += nt
            norm_block(xb, 512, A2, 48, v, lambda c: (h2[:, c, :], h2), sqb, tmpb, rstd, PS[7])
            for hb in range(16):
                wbuf = st1.get(bi * 16 + hb)
                for j in range(4):
                    pb = PS[npb % 6]
                    npb += 1
                    for kc in range(KC):
                        mm(pb[:], wbuf[:, kc, j * 128:(j + 1) * 128], h2[:, kc, :], kc == 0, kc == KC - 1, [wbuf, h2], pb)
                    hc = hb * 4 + j
                    rl = tmpb
                    fw.op("act", lambda e, hc=hc, pb=pb: e.activation(out=tmpb[:, hc % 2, :], in_=pb[:], func=AF.Relu), reads=[pb], writes=[tmpb])
                    fw.op("dve", lambda e, hc=hc, pb=pb: e.tensor_tensor(out=hid[:, hc, :], in0=pb[:], in1=tmpb[:, hc % 2, :], op=ALU.mult),
                          reads=[pb, tmpb], writes=[hid])
            pss = PS[7]
            for fc in range(KC):
                wbuf = st2.get(bi * 16 + fc)
                pb = PS[npb % 6]
                npb += 1
                for kc in range(64):
                    mm(pb[:], wbuf[:, kc, :], hid[:, kc, :], kc == 0, kc == 63, [wbuf, hid], pb)
                fw.op("act", lambda e, fc=fc, pb=pb: e.copy(xb[:, fc, :], pb[:]), reads=[pb], writes=[xb])
                fw.op("pool", lambda e, fc=fc: e.tensor_tensor(out=sqb[:, fc % 2, :], in0=xb[:, fc, :], in1=xb[:, fc, :], op=ALU.mult),
                      reads=[xb], writes=[sqb])
                mm(pss[:], onesb[:], sqb[:, fc % 2, :], fc == 0, fc == KC - 1, [onesb, sqb], pss)
            rsqrt(rstd[:], rstd, pss[:], pss, 1.0)
            for fc in range(KC):
                x_ = xc[nx % 3]
                nx += 1
                o = 0
                for (b, t0, nt) in segs:
                    fw.dma("sp", x_[:, o:o + nt], xT[b, fc * 128:(fc + 1) * 128, t0:t0 + nt], dst=x_)
                    o += nt
                fw.op("pool", lambda e, fc=fc: e.tensor_tensor(out=tmpb[:, fc % 2, :], in0=xb[:, fc, :], in1=rstd[:], op=ALU.mult),
                      reads=[xb, rstd], writes=[tmpb])
                fw.op("dve", lambda e, fc=fc, x_=x_, v=v: e.scalar_tensor_tensor(
                    out=x_[:], in0=tmpb[:, fc % 2, :], scalar=G2[:, fc, v:v + 1], in1=x_[:], op0=ALU.mult, op1=ALU.add),
                    reads=[tmpb, G2, x_], writes=[x_])
                o = 0
                for (b, t0, nt) in segs:
                    fw.dma("sp", xT[b, fc * 128:(fc + 1) * 128, t0:t0 + nt], x_[:, o:o + nt], src=x_)
                    o += nt

    def norm1_to_hT(b, hT):
        xl = [fw.sb("xl%d" % i, [128, KC, 256], F32) for i in range(2)]
        sqb = fw.sb("sqn", [128, 2, 256], BF16)
        tmpb = fw.sb("tmpn", [128, 2, 256], F32)
        rstd = fw.sb("rstdn", [128, 256], F32)
        for i in range(T // 256):
            t0 = i * 256
            xb = xl[i % 2]
            fw.dma("sp", xb[:], xT[b, :, t0:t0 + 256].rearrange("(c p) t -> p c t", p=128), dst=xb)
            v = 2 if t0 < CTX else b
            norm_block(xb, 256, A1, 0, v, lambda c, t0=t0: (hT[:, c, t0:t0 + 256], hT), sqb, tmpb, rstd, PS[6 + i % 2])

    for li in layers:
        last = li == DEPTH - 1
        kind = li % 3
        adaln(li)
        if kind == 0:
            iret = li // 3
            W_in = ret_w_in[iret * D:(iret + 1) * D, :]
            W_out = ret_w_out[iret * 2 * D:(iret + 1) * 2 * D, :]
            KO = 32
        elif kind == 1:
            W_out = gdn_w_out
            KO = 32
        else:
            W_out = na_w_out
            KO = 16
        if not os.environ.get("KDBG_SKIP"):
            dmy = precast(li, W_out, KO)
        for b in range(BPC):
            if not os.environ.get("KDBG_SKIP"):
                hT = fw.sb("hT", [128, KC, T], BF16)
                m0 = fw.mark()
                norm1_to_hT(b, hT)
                fw.release_to(m0)
                if kind == 0:
                    ret_inproj(b, W_in, hT, last)
                elif kind == 1:
                    gdn_inproj(b, hT)
                else:
                    na_inproj(b, hT)
            fw.phase()
            if stop == "inproj":
                fw.emit()
                return nc, fw
            if kind == 0:
                ret_scan(b, last)
            elif kind == 2:
                na_attn(b)
            elif kind == 1:
                gdn_scan(b)
                if stop in ("gates", "g1", "g2", "g3"):
                    fw.emit()
                    return nc, fw
            fw.phase()
        outproj(li, last, KO)
        fw.phase()
        mlp(li, last)
        fw.phase()

    xo = [fw.sb("xo%d" % i, [128, KC, 128], F32) for i in range(2)]
    yo = [fw.sb("yo%d" % i, [128, D], F32) for i in range(2)]
    n = 0
    for b in range(BPC):
        for tt in range(SEQ // 128):
            xi_ = xo[n % 2]
            yo_ = yo[n % 2]
            t0 = CTX + tt * 128
            fw.dma("sp", xi_[:], xT[b, :, t0:t0 + 128].rearrange("(c p) t -> p c t", p=128), dst=xi_)
            for g4 in range(4):
                psb = PS[(n * 4 + g4) % 8]
                for j in range(4):
                    kc = g4 * 4 + j
                    tr(psb[:, j * 128:(j + 1) * 128], xi_[:, kc, :], ident[:], [xi_, ident], psb)
                if g4 % 2 == 0:
                    fw.op("act", lambda e, yo_=yo_, psb=psb, g4=g4: e.copy(yo_[:, g4 * 512:(g4 + 1) * 512], psb[:]), reads=[psb], writes=[yo_])
                else:
                    fw.op("dve", lambda e, yo_=yo_, psb=psb, g4=g4: e.tensor_copy(yo_[:, g4 * 512:(g4 + 1) * 512], psb[:]), reads=[psb], writes=[yo_])
            fw.dma("sp", y_out[b * SEQ + tt * 128: b * SEQ + (tt + 1) * 128, :], yo_[:], src=yo_)
            n += 1
    fw.barrier()
    fw.emit()
    return nc, fw


_CACHE = {}


def _consts():
    cosT, sinT = _rope_tables()
    maskT, xi, zeta, _ = _ret_tables()
    return {
        "ident": np.eye(128, dtype=np.float32),
        "ropecos": cosT, "ropesin": sinT,
        "ret_maskT": np.ascontiguousarray(maskT.reshape(16 * 128, 128)),
        "ret_xi": xi, "ret_zeta": zeta,
        "gdn_masks": _gdn_masks(),
        "gdn_lvmasks": _gdn_lvmasks(),
    }


def make_in_maps(inputs, cores):
    f = lambda a: np.ascontiguousarray(np.asarray(a, dtype=np.float32))
    shared = {
        "ada_w": f(inputs["ada_w"]).reshape(DEPTH * D, 6 * D),
        "ada_b": f(inputs["ada_b"]).reshape(DEPTH * 96, 128),
        "norm_g": f(inputs["norm_g"]).reshape(DEPTH * 64, 128),
        "mlp_w1": f(inputs["mlp_w1"]).reshape(DEPTH * D, 4 * D),
        "mlp_w2": f(inputs["mlp_w2"]).reshape(DEPTH * 4 * D, D),
        "ret_w_in": f(inputs["ret_w_in"]).reshape(2 * D, 6 * D),
        "ret_w_out": f(inputs["ret_w_out"]).reshape(2 * 2 * D, D),
        "gdn_w_in": f(inputs["gdn_w_in"]).reshape(D, 6 * D),
        "gdn_conv_w": f(inputs["gdn_conv_w"]).reshape(5 * 64, 128),
        "gdn_w_ab": f(inputs["gdn_w_ab"]).reshape(2 * D, 64),
        "gdn_a_log": f(inputs["gdn_a_log"]).reshape(1, 64),
        "gdn_dt_bias": f(inputs["gdn_dt_bias"]).reshape(1, 64),
        "gdn_norm_w": f(inputs["gdn_norm_w"]).reshape(1, 128),
        "gdn_w_out": f(inputs["gdn_w_out"]).reshape(2 * D, D),
        "na_w_in": f(inputs["na_w_in"]).reshape(D, 3 * D),
        "na_w_out": f(inputs["na_w_out"]).reshape(D, D),
        "na_bias": _na_bias(f(inputs["na_rpb"]).reshape(16, 15, 31)),
    }
    shared.update(_consts())
    x = f(inputs["x"])
    ctx = f(inputs["ctx"])
    c = f(inputs["c"])
    c_ctx = f(inputs["c_ctx"])
    maps = []
    for core in cores:
        b0 = core * BPC
        m = dict(shared)
        m["x"] = x[b0:b0 + BPC].reshape(BPC * SEQ, D)
        m["ctx"] = ctx[b0:b0 + BPC].reshape(BPC * CTX, D)
        m["cs"] = np.stack([c[b0], c[b0 + 1], c_ctx])
        maps.append(m)
    return maps


def kernel(**inputs):
    if "nc" not in _CACHE:
        _CACHE["nc"] = build_program()[0]
    nc = _CACHE["nc"]
    maps = make_in_maps(inputs, list(range(NCORE)))
    res = run_bass_kernel_spmd(nc, maps, core_ids=list(range(NCORE)))
    out = np.concatenate([r["y"].reshape(BPC, SEQ, D) for r in res.results], axis=0)
    return out.astype(np.float32)
```

```python
import math
import os
import numpy as np
import concourse.bass as bass
import concourse.mybir as mybir
from concourse.bass_utils import run_bass_kernel_spmd

F32 = mybir.dt.float32
BF16 = mybir.dt.bfloat16
AF = mybir.ActivationFunctionType
ALU = mybir.AluOpType
AX = mybir.AxisListType

COMPUTE = ("pe", "act", "dve", "pool")
ALLQ = ("pe", "act", "dve", "pool", "sp")
SB_LO = 16512
SB_HI = 229344

D = 2048
KC = 16
NCORE = 8
BPC = 2
CTX = 256
SEQ = 2048
T = CTX + SEQ
NTILE = T // 128
DEPTH = 4
EPS = 1e-6


def _dsize(dt):
    return 4 if dt == F32 else 2


class DSem:
    __slots__ = ("h", "cnt")

    def __init__(self, h):
        self.h = h
        self.cnt = 0


class Buf:
    __slots__ = ("t", "w", "r", "dsem", "name", "wd", "rd", "persist", "excl")

    def __init__(self, t, name, persist=False, excl=False):
        self.excl = excl
        self.t = t
        self.name = name
        self.w = {}
        self.r = {}
        self.dsem = {}
        self.wd = False
        self.rd = False
        self.persist = persist

    def __getitem__(self, k):
        return self.t[k]


class Ins:
    __slots__ = ("fn", "waits", "dwaits", "inc", "dinc")

    def __init__(self, fn):
        self.fn = fn
        self.waits = []
        self.dwaits = []
        self.inc = False
        self.dinc = None


class FW:
    def __init__(self, nc):
        self.nc = nc
        self.q = {e: [] for e in ALLQ}
        self.seen = {e: {} for e in ALLQ}
        self.dseen = {e: {} for e in ALLQ}
        self.bufs = []
        self.free_dsems = {"hw": [], "sw": []}
        self.all_dsems = []
        self.sb_ptr = SB_LO
        self.sb_persist = SB_LO
        self.uid = 0
        self.ps = []
        for i in range(8):
            t = nc.alloc_psum_tensor("psb%d" % i, [128, 512], F32)
            b = Buf(t, "psb%d" % i, persist=True, excl=True)
            self.bufs.append(b)
            self.ps.append(b)

    def sb(self, name, shape, dtype, persist=False):
        n = 1
        for s in shape[1:]:
            n *= s
        nbytes = (n * _dsize(dtype) + 63) // 64 * 64
        off = self.sb_ptr
        self.sb_ptr += nbytes
        assert self.sb_ptr <= SB_HI, "SBUF overflow at %s: %d" % (name, self.sb_ptr - SB_LO)
        self.uid += 1
        t = self.nc.alloc_sbuf_tensor_at("%s_%d" % (name, self.uid), list(shape), dtype, offset=off)
        b = Buf(t, name, persist=persist)
        self.bufs.append(b)
        return b

    def persist_done(self):
        self.sb_persist = self.sb_ptr

    def view(self, ap, name):
        b = Buf(ap, name)
        self.bufs.append(b)
        return b

    def phase(self):
        self.barrier()
        keep = []
        for b in self.bufs:
            if b.persist:
                keep.append(b)
            else:
                for kind, ds in b.dsem.items():
                    self.free_dsems[kind].append(ds)
                b.dsem = {}
        self.bufs = keep
        self.sb_ptr = self.sb_persist

    def mark(self):
        return (self.sb_ptr, len(self.bufs))

    def release_to(self, m):
        self.barrier()
        ptr, nb = m
        for b in self.bufs[nb:]:
            assert not b.persist
            for kind, ds in b.dsem.items():
                self.free_dsems[kind].append(ds)
            b.dsem = {}
        self.bufs = self.bufs[:nb]
        self.sb_ptr = ptr

    def _get_dsem(self, b, kind):
        if kind not in b.dsem:
            if self.free_dsems[kind]:
                b.dsem[kind] = self.free_dsems[kind].pop()
            else:
                h = self.nc.alloc_semaphore("dq%s%d" % (kind, len(self.all_dsems)))
                b.dsem[kind] = DSem(h)
                self.all_dsems.append(b.dsem[kind])
        return b.dsem[kind]

    def _need(self, e, ins, oe, oidx):
        if self.seen[e].get(oe, -1) >= oidx:
            return
        self.seen[e][oe] = oidx
        ins.waits.append((oe, oidx))
        tgt = self.q[oe][oidx]
        assert tgt.dinc is None and tgt.fn is not None
        tgt.inc = True

    def _dneed(self, e, ins, b):
        for ds in b.dsem.values():
            if ds.cnt == 0:
                continue
            key = id(ds)
            val = 16 * ds.cnt
            if self.dseen[e].get(key, 0) >= val:
                continue
            self.dseen[e][key] = val
            ins.dwaits.append((ds.h, val))

    def _track(self, e, ins, idx, reads, writes, is_dma=False):
        for b in reads:
            if b is None:
                continue
            for oe, oidx in b.w.items():
                if oe == e and e == "pe" and not is_dma:
                    continue
                self._need(e, ins, oe, oidx)
            if b.excl:
                for oe, oidx in b.r.items():
                    if oe != e:
                        self._need(e, ins, oe, oidx)
            if b.wd:
                self._dneed(e, ins, b)
        for b in writes:
            if b is None:
                continue
            for oe, oidx in b.r.items():
                if oe == e and not is_dma:
                    continue
                self._need(e, ins, oe, oidx)
            for oe, oidx in b.w.items():
                if oe == e and not is_dma:
                    continue
                self._need(e, ins, oe, oidx)
            if is_dma:
                if b.rd:
                    self._dneed(e, ins, b)
            elif b.wd or b.rd:
                self._dneed(e, ins, b)
        if not is_dma:
            for b in reads:
                if b is None:
                    continue
                if b.r.get(e, -1) < idx:
                    b.r[e] = idx
            for b in writes:
                if b is None:
                    continue
                if b.r:
                    b.r = {}
                    b.w = {e: idx}
                else:
                    b.w[e] = idx
                if b.wd or b.rd:
                    b.wd = False
                    b.rd = False

    def op(self, e, fn, reads=(), writes=()):
        ins = Ins(fn)
        idx = len(self.q[e])
        self._track(e, ins, idx, reads, writes)
        self.q[e].append(ins)
        return ins

    def dma(self, e, out, in_, src=None, dst=None, **kw):
        def fn(eng):
            return eng.dma_start(out=out, in_=in_, **kw)
        ins = Ins(fn)
        idx = len(self.q[e])
        self._track(e, ins, idx, [src], [dst], is_dma=True)
        assert not (src is not None and dst is not None)
        b = dst if dst is not None else src
        ds = self._get_dsem(b, "sw" if e == "pool" else "hw")
        ds.cnt += 1
        ins.dinc = ds.h
        if dst is not None:
            dst.wd = True
            if dst.r:
                dst.r = {}
                dst.w = {}
        if src is not None:
            src.rd = True
        self.q[e].append(ins)
        return ins

    def barrier(self):
        lasts = {}
        for e in COMPUTE:
            for i in range(len(self.q[e]) - 1, -1, -1):
                if self.q[e][i].fn is not None and self.q[e][i].dinc is None:
                    lasts[e] = i
                    break
        for e in ALLQ:
            ins = Ins(None)
            for oe, oidx in lasts.items():
                self._need(e, ins, oe, oidx)
            for b in self.bufs:
                if b.wd or b.rd:
                    self._dneed(e, ins, b)
            self.q[e].append(ins)
        for b in self.bufs:
            b.wd = False
            b.rd = False
            b.w = {}
            b.r = {}

    def emit(self):
        nc = self.nc
        esem = {e: nc.alloc_semaphore("es_" + e) for e in COMPUTE}
        val = {}
        for e in COMPUTE:
            c = 0
            arr = []
            for ins in self.q[e]:
                if ins.inc:
                    c += 1
                arr.append(c)
            val[e] = arr
        q = self.q

        def run(e):
            def body(engine):
                for ins in q[e]:
                    for oe, oidx in ins.waits:
                        engine.wait_ge(esem[oe], val[oe][oidx])
                    for s, v in ins.dwaits:
                        engine.wait_ge(s, v)
                    if ins.fn is None:
                        continue
                    bi = ins.fn(engine)
                    if ins.dinc is not None:
                        bi.then_inc(ins.dinc, 16)
                    elif ins.inc:
                        bi.then_inc(esem[e], 1)
            return body

        with nc.Block() as block:
            block.sync(run("sp"))
            block.tensor(run("pe"))
            block.scalar(run("act"))
            block.vector(run("dve"))
            block.gpsimd(run("pool"))


RET_HEADS = 8
RET_DK = 256
RET_DV = 512
GRID_W = 64


def _rope_tables():
    t = np.arange(SEQ)
    row = (t // GRID_W).astype(np.float32)
    col = (t % GRID_W).astype(np.float32)
    n_pairs = RET_DK // 2
    inv = (10000.0 ** (-np.arange(0, n_pairs, 2, dtype=np.float32) / n_pairs)).astype(np.float32)
    ang = np.concatenate([row[:, None] * inv, col[:, None] * inv], axis=-1).astype(np.float32)
    return np.ascontiguousarray(np.cos(ang).T.astype(np.float32)), np.ascontiguousarray(np.sin(ang).T.astype(np.float32))


def _ret_tables():
    fwd = np.log(1.0 - 2.0 ** (-5.0 - np.arange(RET_HEADS, dtype=np.float64)))
    lg = np.stack([fwd, fwd[::-1]])
    C = 128
    pos = np.arange(C, dtype=np.float64)
    maskT = np.zeros((RET_HEADS * 2, C, C), np.float32)
    xi = np.zeros((C, RET_HEADS * 2), np.float32)
    zeta = np.zeros((C, RET_HEADS * 2), np.float32)
    cdec = np.zeros((RET_HEADS * 2,), np.float64)
    sc = RET_DK ** -0.5
    for h in range(RET_HEADS):
        for d in range(2):
            g = lg[d, h]
            i = h * 2 + d
            m = pos[:, None]
            c = pos[None, :]
            if d == 0:
                maskT[i] = np.where(c >= m, np.exp(g * np.maximum(c - m, 0)), 0.0) * sc
                xi[:, i] = np.exp(g * (pos + 1))
                zeta[:, i] = np.exp(g * (C - 1 - pos)) * sc
            else:
                maskT[i] = np.where(m >= c, np.exp(g * np.maximum(m - c, 0)), 0.0) * sc
                xi[:, i] = np.exp(g * (C - pos))
                zeta[:, i] = np.exp(g * pos) * sc
            cdec[i] = np.exp(g * C)
    return maskT, xi, zeta, cdec


def _gdn_masks():
    c = np.arange(128)[:, None]
    m = np.arange(128)[None, :]
    tiles = [(m < c), (m <= c), (m > c), (m >= c)]
    return np.ascontiguousarray(np.concatenate([t.astype(np.float32) for t in tiles], axis=1))


def _na_bias(rpb):
    out = np.empty((16, 5, 128, 640), np.float32)
    lq = np.arange(128)
    lk = np.arange(640)
    q_col = lq % 64
    k_col = lk % 64
    col_start = np.clip(q_col - 8, 0, 64 - 16)
    col_ok = (k_col[None] >= col_start[:, None]) & (k_col[None] < col_start[:, None] + 16)
    dc = np.clip(k_col[None] - q_col[:, None] + 15, 0, 30)
    for p, i in enumerate((0, 1, 2, 14, 15)):
        lo = min(max(i - 2, 0), 11)
        q_row = 2 * i + lq // 64
        k_row = 2 * lo + lk // 64
        row_start = np.clip(q_row - 4, 0, 32 - 8)
        row_ok = (k_row[None] >= row_start[:, None]) & (k_row[None] < row_start[:, None] + 8)
        dr = np.clip(k_row[None] - q_row[:, None] + 7, 0, 14)
        ok = row_ok & col_ok
        out[:, p] = np.where(ok[None], rpb[:, dr, dc], np.float32(-30000.0))
    return np.ascontiguousarray(out.reshape(16 * 5 * 128, 640))


def _gdn_lvmasks():
    c = np.arange(128)[:, None]
    m = np.arange(128)[None, :]
    out = np.zeros((2, 7, 128, 256), np.float32)
    for lv in range(7):
        b = 1 << lv
        low = ((c // (2 * b)) == (m // (2 * b))) & ((c % (2 * b)) >= b) & ((m % (2 * b)) < b)
        low = -low.astype(np.float32)
        out[0, lv, :, 0:128] = low
        out[0, lv, :, 128:256] = low.T
        out[1, lv, :, 0:128] = low.T
        out[1, lv, :, 128:256] = low
    return np.ascontiguousarray(out.reshape(14 * 128, 256))


class Stream:
    def __init__(self, items, load_fn, depth):
        self.items = items
        self.load_fn = load_fn
        self.depth = depth
        self.loaded = 0
        self.res = {}

    def get(self, i):
        while self.loaded < len(self.items) and self.loaded < i + self.depth:
            self.res[self.loaded] = self.load_fn(self.loaded, self.items[self.loaded])
            self.loaded += 1
        r = self.res.pop(i)
        return r


def build_program(layers=(0, 1, 2, 3), dbg=False, stop=None):
    nc = bass.Bass("TRN2", target_bir_lowering=False)
    fw = FW(nc)
    PS = fw.ps

    def din(name, shape, dt=F32):
        return nc.dram_tensor(name, list(shape), dt, kind="ExternalInput").ap()

    def dscr(name, shape, dt):
        return nc.dram_tensor(name, list(shape), dt).ap()

    x_in = din("x", [BPC * SEQ, D])
    ctx_in = din("ctx", [BPC * CTX, D])
    cs_in = din("cs", [3, D])
    ada_w = din("ada_w", [DEPTH * D, 6 * D])
    ada_b = din("ada_b", [DEPTH * 96, 128])
    norm_g = din("norm_g", [DEPTH * 64, 128])
    mlp_w1 = din("mlp_w1", [DEPTH * D, 4 * D])
    mlp_w2 = din("mlp_w2", [DEPTH * 4 * D, D])
    ret_w_in = din("ret_w_in", [2 * D, 6 * D])
    ret_w_out = din("ret_w_out", [2 * 2 * D, D])
    gdn_w_in = din("gdn_w_in", [D, 6 * D])
    gdn_conv_w = din("gdn_conv_w", [5 * 64, 128])
    gdn_w_ab = din("gdn_w_ab", [2 * D, 64])
    gdn_a_log = din("gdn_a_log", [1, 64])
    gdn_dt_bias = din("gdn_dt_bias", [1, 64])
    gdn_norm_w = din("gdn_norm_w", [1, 128])
    gdn_w_out = din("gdn_w_out", [2 * D, D])
    na_w_in = din("na_w_in", [D, 3 * D])
    na_w_out = din("na_w_out", [D, D])
    na_bias = din("na_bias", [16 * 5 * 128, 640])
    gmask_in = din("gdn_masks", [128, 4 * 128])
    glv_in = din("gdn_lvmasks", [14 * 128, 256])
    ident_in = din("ident", [128, 128])
    cos_in = din("ropecos", [128, SEQ])
    sin_in = din("ropesin", [128, SEQ])
    rmask_in = din("ret_maskT", [16 * 128, 128])
    rxi_in = din("ret_xi", [128, 16])
    rzeta_in = din("ret_zeta", [128, 16])
    y_out = nc.dram_tensor("y", [BPC * SEQ, D], F32, kind="ExternalOutput").ap()

    xT = (nc.dram_tensor("xT", [BPC, D, T], F32, kind="ExternalOutput").ap() if dbg else dscr("xT", [BPC, D, T], F32))
    def dscr_dbg0(name, shape, dt):
        if dbg:
            return nc.dram_tensor(name, list(shape), dt, kind="ExternalOutput").ap()
        return dscr(name, shape, dt)
    qT_s = dscr_dbg0("qT_s", [D, T], BF16)
    kT_s = dscr_dbg0("kT_s", [D, T], BF16)
    v_s = dscr("v_s", [T, 2 * D], BF16)
    g_s = dscr("g_s", [T, 2 * D], BF16)
    def dscr_dbg(name, shape, dt):
        if dbg:
            return nc.dram_tensor(name, list(shape), dt, kind="ExternalOutput").ap()
        return dscr(name, shape, dt)
    vT_s = dscr_dbg("vT_s", [2 * D, T], BF16)
    ynT = dscr("ynT", [BPC, 2 * D, T], BF16)
    wo_b = dscr("wo_b", [2 * D, D], BF16)
    w1_b = dscr("w1_b", [D, 4 * D], BF16)
    w2_b = dscr("w2_b", [16, 128, 64, 128], BF16)

    _, _, _, ret_cdec = _ret_tables()

    ident = fw.sb("ident", [128, 128], F32, persist=True)
    identb = fw.sb("identb", [128, 128], BF16, persist=True)
    onesb = fw.sb("onesb", [128, 128], BF16, persist=True)
    ones512 = fw.sb("ones512", [128, 128], BF16, persist=True)
    bvec = fw.sb("bvec", [128, DEPTH * 96], F32, persist=True)
    gvec = fw.sb("gvec", [128, DEPTH * 64], F32, persist=True)
    sT = fw.sb("sT", [128, KC, 3], BF16, persist=True)
    modv = fw.sb("modv", [128, 96, 3], F32, persist=True)
    A1 = fw.sb("A1", [128, KC, 3], F32, persist=True)
    G1 = fw.sb("G1", [128, KC, 3], F32, persist=True)
    A2 = fw.sb("A2", [128, KC, 3], F32, persist=True)
    G2 = fw.sb("G2", [128, KC, 3], F32, persist=True)
    epsc = fw.sb("epsc", [128, 1], F32, persist=True)
    ones1b = fw.sb("ones1b", [128, 128], BF16, persist=True)
    ones1f = fw.sb("ones1f", [128, 128], F32, persist=True)
    fw.persist_done()

    def mm(ps_ap, lhsT, rhs, start, stop, reads, psb):
        fw.op("pe", lambda e: e.matmul(ps_ap, lhsT, rhs, start=start, stop=stop), reads=reads, writes=[psb])

    def rsqrt(o_ap, o_buf, i_ap, i_buf, scale):
        fw.op("act", lambda e: e.activation(out=o_ap, in_=i_ap, func=AF.Sqrt, bias=epsc[:, 0:1], scale=scale),
              reads=[i_buf, epsc], writes=[o_buf])
        fw.op("dve", lambda e: e.reciprocal(o_ap, o_ap), reads=[o_buf], writes=[o_buf])

    def tr(ps_ap, in_ap, id_ap, reads, psb):
        fw.op("pe", lambda e: e.transpose(ps_ap, in_ap, id_ap), reads=reads, writes=[psb])

    fw.dma("sp", ident[:], ident_in, dst=ident)
    fw.op("dve", lambda e: e.tensor_copy(identb[:], ident[:]), reads=[ident], writes=[identb])
    fw.op("dve", lambda e: e.memset(onesb[:], 1.0 / D), writes=[onesb])
    fw.op("dve", lambda e: e.memset(ones512[:], 1.0 / 512), writes=[ones512])
    fw.op("dve", lambda e: e.memset(epsc[:], EPS), writes=[epsc])
    fw.op("dve", lambda e: e.memset(ones1b[:], 1.0), writes=[ones1b])
    fw.op("dve", lambda e: e.memset(ones1f[:], 1.0), writes=[ones1f])
    tmpv = fw.sb("tmpv", [128, 5, 128], F32)
    for i in range(3):
        fw.dma("sp", tmpv[:, i, :], ada_b[i * 128:(i + 1) * 128, :], dst=tmpv)
    for i in range(2):
        fw.dma("sp", tmpv[:, 3 + i, :], norm_g[i * 128:(i + 1) * 128, :], dst=tmpv)
    for i in range(5):
        tr(PS[0][:, i * 128:(i + 1) * 128] if i < 4 else PS[1][:, 0:128], tmpv[:, i, :], ident[:],
           [tmpv, ident], PS[0] if i < 4 else PS[1])
    fw.op("dve", lambda e: e.tensor_copy(bvec[:], PS[0][:, 0:384]), reads=[PS[0]], writes=[bvec])
    fw.op("dve", lambda e: e.tensor_copy(gvec[:, 0:128], PS[0][:, 384:512]), reads=[PS[0]], writes=[gvec])
    fw.op("dve", lambda e: e.tensor_copy(gvec[:, 128:256], PS[1][:, 0:128]), reads=[PS[1]], writes=[gvec])
    cs_sb = fw.sb("cs_sb", [3, D], F32)
    cs_sg = fw.sb("cs_sg", [3, D], F32)
    fw.dma("sp", cs_sb[:], cs_in, dst=cs_sb)
    fw.op("act", lambda e: e.activation(out=cs_sg[:], in_=cs_sb[:], func=AF.Sigmoid), reads=[cs_sb], writes=[cs_sg])
    fw.op("dve", lambda e: e.tensor_tensor(out=cs_sb[:], in0=cs_sb[:], in1=cs_sg[:], op=ALU.mult),
          reads=[cs_sb, cs_sg], writes=[cs_sb])
    for kc in range(KC):
        tr(PS[2][:, kc * 3:kc * 3 + 3], cs_sb[:, kc * 128:(kc + 1) * 128], ident[0:3, 0:3], [cs_sb, ident], PS[2])
    fw.op("dve", lambda e: e.tensor_copy(sT[:].rearrange("p k v -> p (k v)"), PS[2][:, 0:48]), reads=[PS[2]], writes=[sT])
    fw.phase()

    xin_b = [fw.sb("xin%d" % i, [128, D], F32) for i in range(2)]
    xst_b = [fw.sb("xst%d" % i, [128, KC, 128], F32) for i in range(2)]
    n = 0
    for b in range(BPC):
        for tt in range(NTILE):
            xin = xin_b[n % 2]
            xst = xst_b[n % 2]
            src = ctx_in[b * CTX + tt * 128: b * CTX + (tt + 1) * 128, :] if tt < 2 else \
                x_in[b * SEQ + (tt - 2) * 128: b * SEQ + (tt - 1) * 128, :]
            fw.dma("sp", xin[:], src, dst=xin)
            for g4 in range(4):
                psb = PS[(n * 4 + g4) % 8]
                for j in range(4):
                    kc = g4 * 4 + j
                    tr(psb[:, j * 128:(j + 1) * 128], xin[:, kc * 128:(kc + 1) * 128], ident[:], [xin, ident], psb)
                eng = "act" if g4 % 2 == 0 else "dve"
                dst_ap = xst[:, g4 * 4:(g4 + 1) * 4, :].rearrange("p k t -> p (k t)")
                if eng == "act":
                    fw.op("act", lambda e, o=dst_ap, p=psb: e.copy(o, p[:]), reads=[psb], writes=[xst])
                else:
                    fw.op("dve", lambda e, o=dst_ap, p=psb: e.tensor_copy(o, p[:]), reads=[psb], writes=[xst])
            fw.dma("sp", xT[b, :, tt * 128:(tt + 1) * 128].rearrange("(k p) t -> p k t", p=128), xst[:], src=xst)
            n += 1
    fw.phase()

    def adaln(li):
        wb = [fw.sb("adw%d" % i, [128, KC, 512], BF16) for i in range(3)]
        W = ada_w[li * D:(li + 1) * D, :]

        def load(i, cb):
            buf = wb[i % 3]
            fw.dma("pool", buf[:], W[:, cb * 512:(cb + 1) * 512].rearrange("(k p) n -> p k n", p=128), dst=buf)
            return buf
        st = Stream(list(range(24)), load, 3)
        for cb in range(24):
            buf = st.get(cb)
            psb = PS[cb % 2]
            for j in range(4):
                for kc in range(KC):
                    mm(psb[:, j * 4:j * 4 + 3], buf[:, kc, j * 128:(j + 1) * 128], sT[:, kc, :], kc == 0, kc == KC - 1,
                       [buf, sT], psb)
            for j in range(4):
                fc = cb * 4 + j
                fw.op("dve", lambda e, fc=fc, j=j, psb=psb: e.tensor_scalar(
                    out=modv[:, fc, :], in0=psb[:, j * 4:j * 4 + 3], scalar1=bvec[:, li * 96 + fc:li * 96 + fc + 1],
                    scalar2=None, op0=ALU.add), reads=[psb, bvec], writes=[modv])
        g0 = li * 64
        for c in range(KC):
            fw.op("dve", lambda e, c=c: e.tensor_scalar(out=A1[:, c, :], in0=modv[:, 16 + c, :], scalar1=1.0,
                                                         scalar2=gvec[:, g0 + c:g0 + c + 1], op0=ALU.add, op1=ALU.mult),
                  reads=[modv, gvec], writes=[A1])
            fw.op("dve", lambda e, c=c: e.tensor_scalar(out=G1[:, c, :], in0=modv[:, 32 + c, :],
                                                         scalar1=gvec[:, g0 + 16 + c:g0 + 16 + c + 1], scalar2=None, op0=ALU.mult),
                  reads=[modv, gvec], writes=[G1])
            fw.op("dve", lambda e, c=c: e.tensor_scalar(out=A2[:, c, :], in0=modv[:, 64 + c, :], scalar1=1.0,
                                                         scalar2=gvec[:, g0 + 32 + c:g0 + 32 + c + 1], op0=ALU.add, op1=ALU.mult),
                  reads=[modv, gvec], writes=[A2])
            fw.op("dve", lambda e, c=c: e.tensor_scalar(out=G2[:, c, :], in0=modv[:, 80 + c, :],
                                                         scalar1=gvec[:, g0 + 48 + c:g0 + 48 + c + 1], scalar2=None, op0=ALU.mult),
                  reads=[modv, gvec], writes=[G2])
        fw.phase()

    def precast(li, w_out_ap, KO):
        dmy = fw.sb("pcdummy", [128, 8], F32)
        rr_ = KO * 128 // 4
        for i in range(4):
            fw.dma("pool", wo_b[i * rr_:(i + 1) * rr_, :], w_out_ap[i * rr_:(i + 1) * rr_, :], dst=dmy)
        W1 = mlp_w1[li * D:(li + 1) * D, :]
        for i in range(4):
            fw.dma("pool", w1_b[i * 512:(i + 1) * 512, :], W1[i * 512:(i + 1) * 512, :], dst=dmy)
        W2 = mlp_w2[li * 4 * D:(li + 1) * 4 * D, :]
        for fc in range(16):
            fw.dma("pool", w2_b[fc], W2[:, fc * 128:(fc + 1) * 128].rearrange("(k p) c -> p k c", p=128), dst=dmy)
        return dmy

    def norm_block(xblk, nt, Avec, Bcol_base, v, out_fn, sqb, tmpb, rstd, psb):
        for kc in range(KC):
            fw.op("act", lambda e, kc=kc: e.activation(out=sqb[:, kc % 2, 0:nt], in_=xblk[:, kc, 0:nt], func=AF.Square),
                  reads=[xblk], writes=[sqb])
            mm(psb[:, 0:nt], onesb[:], sqb[:, kc % 2, 0:nt], kc == 0, kc == KC - 1, [onesb, sqb], psb)
        rsqrt(rstd[:, 0:nt], rstd, psb[:, 0:nt], psb, 1.0)
        for c in range(KC):
            fw.op("dve", lambda e, c=c: e.tensor_tensor(out=tmpb[:, c % 2, 0:nt], in0=xblk[:, c, 0:nt], in1=rstd[:, 0:nt],
                                                        op=ALU.mult), reads=[xblk, rstd], writes=[tmpb])
            o_ap, o_buf = out_fn(c)
            fw.op("act", lambda e, c=c, o_ap=o_ap: e.activation(
                out=o_ap, in_=tmpb[:, c % 2, 0:nt], func=AF.Identity,
                bias=modv[:, Bcol_base + c, v:v + 1], scale=Avec[:, c, v:v + 1]),
                reads=[tmpb, modv, Avec], writes=[o_buf])

    def ret_inproj(b, W, hT, last):
        wb = [fw.sb("wi%d" % i, [128, KC, 512], BF16) for i in range(2)]
        cosT = fw.sb("cosT", [128, SEQ], F32)
        sinT = fw.sb("sinT", [128, SEQ], F32)
        fw.dma("sp", cosT[:], cos_in, dst=cosT)
        fw.dma("sp", sinT[:], sin_in, dst=sinT)
        rt = [fw.sb("rt%d" % i, [128, 512], F32) for i in range(4)]
        qst = [fw.sb("qst%d" % i, [128, 2, 512], BF16) for i in range(2)]
        vst = [fw.sb("vst%d" % i, [128, 512], BF16) for i in range(3)]

        def load(i, cb):
            buf = wb[i % 2]
            fw.dma("pool", buf[:], W[:, cb * 512:(cb + 1) * 512].rearrange("(k p) n -> p k n", p=128), dst=buf)
            return buf
        st = Stream(list(range(24)), load, 2)
        tblocks = [(0, CTX)] + [(CTX + 512 * j, 512) for j in range(4)]
        nq = 0
        nv = 0
        npb = 0
        for cb in range(24):
            buf = st.get(cb)
            if cb < 8:
                dst_s = qT_s if cb < 4 else kT_s
                for hh in range(2):
                    row0 = (cb % 4) * 512 + hh * 256
                    for (t0, nt) in tblocks:
                        p1 = PS[npb % 8]
                        p2 = PS[(npb + 1) % 8]
                        npb += 2
                        for half, pb in ((0, p1), (1, p2)):
                            c0 = hh * 256 + half * 128
                            for kc in range(KC):
                                mm(pb[:, 0:nt], buf[:, kc, c0:c0 + 128], hT[:, kc, t0:t0 + nt], kc == 0, kc == KC - 1,
                                   [buf, hT], pb)
                        qs = qst[nq % 2]
                        nq += 1
                        if t0 < CTX:
                            fw.op("act", lambda e, qs=qs, p1=p1, nt=nt: e.copy(qs[:, 0, 0:nt], p1[:, 0:nt]), reads=[p1], writes=[qs])
                            fw.op("act", lambda e, qs=qs, p2=p2, nt=nt: e.copy(qs[:, 1, 0:nt], p2[:, 0:nt]), reads=[p2], writes=[qs])
                        else:
                            l0 = t0 - CTX
                            cs_ = cosT[:, l0:l0 + nt]
                            sn_ = sinT[:, l0:l0 + nt]
                            fw.op("dve", lambda e, p1=p1, cs_=cs_: e.tensor_tensor(out=rt[0][:], in0=p1[:], in1=cs_, op=ALU.mult),
                                  reads=[p1, cosT], writes=[rt[0]])
                            fw.op("dve", lambda e, p2=p2, sn_=sn_: e.tensor_tensor(out=rt[1][:], in0=p2[:], in1=sn_, op=ALU.mult),
                                  reads=[p2, sinT], writes=[rt[1]])
                            fw.op("dve", lambda e, p1=p1, sn_=sn_: e.tensor_tensor(out=rt[2][:], in0=p1[:], in1=sn_, op=ALU.mult),
                                  reads=[p1, sinT], writes=[rt[2]])
                            fw.op("dve", lambda e, p2=p2, cs_=cs_: e.tensor_tensor(out=rt[3][:], in0=p2[:], in1=cs_, op=ALU.mult),
                                  reads=[p2, cosT], writes=[rt[3]])
                            fw.op("pool", lambda e, qs=qs: e.tensor_tensor(out=qs[:, 0, :], in0=rt[0][:], in1=rt[1][:], op=ALU.subtract),
                                  reads=[rt[0], rt[1]], writes=[qs])
                            fw.op("pool", lambda e, qs=qs: e.tensor_tensor(out=qs[:, 1, :], in0=rt[2][:], in1=rt[3][:], op=ALU.add),
                                  reads=[rt[2], rt[3]], writes=[qs])
                        fw.dma("sp", dst_s[row0:row0 + 256, t0:t0 + nt].rearrange("(h p) t -> p h t", p=128),
                               qs[:, :, 0:nt], src=qs)
            else:
                dst_s = v_s if cb < 16 else g_s
                col0 = ((cb - 8) % 8) * 512
                for tt in range(NTILE):
                    if last and tt < 2 and cb >= 16:
                        continue
                    pb = PS[npb % 8]
                    npb += 1
                    for kc in range(KC):
                        mm(pb[:], hT[:, kc, tt * 128:(tt + 1) * 128], buf[:, kc, :], kc == 0, kc == KC - 1, [buf, hT], pb)
                    vs = vst[nv % 3]
                    nv += 1
                    if cb < 16:
                        if nv % 2 == 0:
                            fw.op("act", lambda e, vs=vs, pb=pb: e.copy(vs[:], pb[:]), reads=[pb], writes=[vs])
                        else:
                            fw.op("dve", lambda e, vs=vs, pb=pb: e.tensor_copy(vs[:], pb[:]), reads=[pb], writes=[vs])
                    else:
                        fw.op("act", lambda e, vs=vs, pb=pb: e.activation(out=vs[:], in_=pb[:], func=AF.Silu), reads=[pb], writes=[vs])
                    fw.dma("sp", dst_s[tt * 128:(tt + 1) * 128, col0:col0 + 512], vs[:], src=vs)

    def ret_scan(b, last):
        maskT = fw.sb("rmask", [128, 16, 128], F32)
        xi = fw.sb("rxi", [128, 16], F32)
        zeta = fw.sb("rzeta", [128, 16], F32)
        fw.dma("sp", maskT[:], rmask_in.rearrange("(i m) c -> m i c", m=128), dst=maskT)
        fw.dma("sp", xi[:], rxi_in, dst=xi)
        fw.dma("sp", zeta[:], rzeta_in, dst=zeta)
        qh = [fw.sb("qh%d" % i, [128, 2, T], BF16) for i in range(2)]
        kh = [fw.sb("kh%d" % i, [128, 2, T], BF16) for i in range(2)]
        vh = [fw.sb("vh%d" % i, [128, NTILE, 512], BF16) for i in range(2)]
        gh = [fw.sb("gh%d" % i, [128, NTILE, 512], BF16) for i in range(2)]
        oacc = fw.sb("oacc", [128, NTILE, 512], F32)
        yst = fw.sb("yst", [128, 4, T], BF16)
        R = fw.sb("R", [128, 2, 512], F32)
        Rb = fw.sb("Rb", [128, 2, 512], BF16)
        AT = [fw.sb("AT%d" % i, [128, 128], BF16) for i in range(2)]
        kz = [fw.sb("kz%d" % i, [128, 256], BF16) for i in range(2)]
        junk = [fw.sb("junk%d" % i, [128, 512], F32) for i in range(2)]
        ss = fw.sb("ss", [128, NTILE], F32)
        rstd = fw.sb("rstdh", [128, NTILE], F32)
        yn = [fw.sb("yn%d" % i, [128, 512], BF16) for i in range(2)]

        def load(i, h):
            fw.dma("sp", qh[i % 2][:], qT_s[h * 256:(h + 1) * 256, :].rearrange("(c p) t -> p c t", p=128), dst=qh[i % 2])
            fw.dma("sp", kh[i % 2][:], kT_s[h * 256:(h + 1) * 256, :].rearrange("(c p) t -> p c t", p=128), dst=kh[i % 2])
            fw.dma("sp", vh[i % 2][:], v_s[:, h * 512:(h + 1) * 512].rearrange("(j p) e -> p j e", p=128), dst=vh[i % 2])
            fw.dma("sp", gh[i % 2][:], g_s[:, h * 512:(h + 1) * 512].rearrange("(j p) e -> p j e", p=128), dst=gh[i % 2])
            return (qh[i % 2], kh[i % 2], vh[i % 2], gh[i % 2])
        st = Stream(list(range(RET_HEADS)), load, 2)
        nstep = 0
        for h in range(RET_HEADS):
            q_, k_, v_, g_ = st.get(h)
            for d in range(2):
                hd = h * 2 + d
                order = list(range(NTILE)) if d == 0 else [1, 0] + list(range(NTILE - 1, 1, -1))
                for si, j in enumerate(order):
                    tsl = slice(j * 128, (j + 1) * 128)
                    need_out = not (last and j < 2)
                    first = si == 0
                    lastst = si == len(order) - 1
                    p_s = PS[0 + nstep % 2]
                    p_o = PS[2 + nstep % 2]
                    p_i = PS[4 + nstep % 2]
                    p_r = (PS[6], PS[7])
                    p_k = PS[0 + nstep % 2]
                    at = AT[nstep % 2]
                    kzz = kz[nstep % 2]
                    nstep += 1
                    if need_out:
                        for dc in range(2):
                            mm(p_s[:, 0:128], k_[:, dc, tsl], q_[:, dc, tsl], dc == 0, dc == 1, [k_, q_], p_s)
                        fw.op("dve", lambda e, at=at, p_s=p_s, hd=hd: e.tensor_tensor(out=at[:], in0=p_s[:, 0:128], in1=maskT[:, hd, :], op=ALU.mult),
                              reads=[p_s, maskT], writes=[at])
                        mm(p_o[:], at[:], v_[:, j, :], True, True, [at, v_], p_o)
                        if not first:
                            for dc in range(2):
                                mm(p_i[:], q_[:, dc, tsl], Rb[:, dc, :], dc == 0, dc == 1, [q_, Rb], p_i)
                        if d == 0:
                            fw.op("act", lambda e, j=j, p_o=p_o: e.copy(oacc[:, j, :], p_o[:]), reads=[p_o], writes=[oacc])
                        else:
                            fw.op("dve", lambda e, j=j, p_o=p_o: e.tensor_tensor(out=oacc[:, j, :], in0=p_o[:], in1=oacc[:, j, :], op=ALU.add),
                                  reads=[p_o, oacc], writes=[oacc])
                        if not first:
                            fw.op("dve", lambda e, j=j, p_i=p_i, hd=hd: e.scalar_tensor_tensor(
                                out=oacc[:, j, :], in0=p_i[:], scalar=xi[:, hd:hd + 1], in1=oacc[:, j, :], op0=ALU.mult, op1=ALU.add),
                                reads=[p_i, xi, oacc], writes=[oacc])
                    if not lastst:
                        pkv = p_k[:].bitcast(BF16)
                        for dc in range(2):
                            tr(pkv[:, 512 + dc * 128:512 + (dc + 1) * 128], k_[:, dc, tsl], identb[:], [k_, identb], p_k)
                        fw.op("act", lambda e, kzz=kzz, pkv=pkv, hd=hd: e.activation(out=kzz[:], in_=pkv[:, 512:768], func=AF.Identity,
                                                                                     scale=zeta[:, hd:hd + 1]),
                              reads=[p_k, zeta], writes=[kzz])
                        for dc in range(2):
                            mm(p_r[dc][:], kzz[:, dc * 128:(dc + 1) * 128], v_[:, j, :], True, True, [kzz, v_], p_r[dc])
                        for dc in range(2):
                            if first:
                                fw.op("dve", lambda e, dc=dc: e.tensor_copy(R[:, dc, :], p_r[dc][:]), reads=[p_r[dc]], writes=[R])
                            else:
                                fw.op("dve", lambda e, dc=dc, hd=hd: e.scalar_tensor_tensor(
                                    out=R[:, dc, :], in0=R[:, dc, :], scalar=float(ret_cdec[hd]), in1=p_r[dc][:], op0=ALU.mult, op1=ALU.add),
                                    reads=[R, p_r[dc]], writes=[R])
                        fw.op("pool", lambda e: e.tensor_copy(Rb[:], R[:]), reads=[R], writes=[Rb])
            j0 = 2 if last else 0
            for j in range(j0, NTILE):
                jk = junk[j % 2]
                fw.op("act", lambda e, j=j, jk=jk: e.activation(out=jk[:], in_=oacc[:, j, :], func=AF.Square),
                      reads=[oacc], writes=[jk])
                fw.op("dve", lambda e, j=j, jk=jk: e.tensor_reduce(out=ss[:, j:j + 1], in_=jk[:], axis=AX.X, op=ALU.add),
                      reads=[jk], writes=[ss])
            rsqrt(rstd[:, j0:NTILE], rstd, ss[:, j0:NTILE], ss, 1.0 / RET_DV)
            for j in range(j0, NTILE):
                y = yn[j % 2]
                fw.op("dve", lambda e, j=j, y=y, g_=g_: e.scalar_tensor_tensor(out=y[:], in0=oacc[:, j, :], scalar=rstd[:, j:j + 1],
                                                                        in1=g_[:, j, :], op0=ALU.mult, op1=ALU.mult),
                      reads=[oacc, rstd, g_], writes=[y])
                pt = PS[4 + j % 2]
                ptv = pt[:].bitcast(BF16)
                for ec in range(4):
                    tr(ptv[:, ec * 128:(ec + 1) * 128], y[:, ec * 128:(ec + 1) * 128], identb[:], [y, identb], pt)
                fw.op("act", lambda e, j=j, ptv=ptv: e.copy(yst[:, :, j * 128:(j + 1) * 128],
                                                           ptv[:, 0:512].rearrange("p (c t) -> p c t", c=4)),
                      reads=[pt], writes=[yst])
            t0 = j0 * 128
            fw.dma("sp", ynT[b, h * 512:(h + 1) * 512, t0:T].rearrange("(c p) t -> p c t", p=128), yst[:, :, t0:T], src=yst)


    ZL = 2314
    ZN = 2310
    ab_s = dscr("ab_s", [T, 128], F32)

    def gdn_inproj(b, hT):
        W = gdn_w_in
        wb = [fw.sb("wi%d" % i, [128, KC, 512], BF16) for i in range(2)]
        wab = fw.sb("wab", [128, KC, 128], BF16)
        for d in range(2):
            fw.dma("pool", wab[:, :, d * 64:(d + 1) * 64], gdn_w_ab[d * D:(d + 1) * D, :].rearrange("(k p) n -> p k n", p=128), dst=wab)
        cw = fw.sb("cw", [128, 320], F32)
        tmpc = fw.sb("tmpc", [128, 3, 128], F32)
        for i in range(3):
            n = 128 if i < 2 else 64
            fw.dma("sp", tmpc[0:n, i, :], gdn_conv_w[i * 128:i * 128 + n, :], dst=tmpc)
        for i in range(3):
            n = 128 if i < 2 else 64
            tr(PS[0][:, i * 128:i * 128 + n], tmpc[0:n, i, :], ident[0:n, 0:n], [tmpc, ident], PS[0])
        fw.op("dve", lambda e: e.tensor_copy(cw[:], PS[0][:, 0:320]), reads=[PS[0]], writes=[cw])
        zc = [fw.sb("zc%d" % i, [128, ZL], F32) for i in range(2)]
        for z in zc:
            fw.op("pool", lambda e, z=z: e.memset(z[:], 0.0), writes=[z])
        acc = fw.sb("acc", [128, ZN], F32)
        sl = fw.sb("sl", [128, ZN], F32)
        sqq = fw.sb("sqq", [128, ZN], BF16)
        rs = fw.sb("rs", [128, ZN], F32)
        ob = [fw.sb("ob%d" % i, [128, ZN], BF16) for i in range(2)]
        vst = [fw.sb("vst%d" % i, [128, 512], BF16) for i in range(3)]
        abst = [fw.sb("abst%d" % i, [128, 128], F32) for i in range(2)]

        def load(i, cb):
            buf = wb[i % 2]
            fw.dma("pool", buf[:], W[:, cb * 512:(cb + 1) * 512].rearrange("(k p) n -> p k n", p=128), dst=buf)
            return buf
        st = Stream(list(range(24)), load, 2)
        tblocks = [(0, CTX, 2)] + [(CTX + 512 * j, 512, 262 + 512 * j) for j in range(4)]
        npb = 0
        nv = 0
        ncc = 0
        for tt in range(NTILE):
            pb = PS[npb % 8]
            npb += 1
            for kc in range(KC):
                mm(pb[:, 0:128], hT[:, kc, tt * 128:(tt + 1) * 128], wab[:, kc, :], kc == 0, kc == KC - 1, [wab, hT], pb)
            a_ = abst[tt % 2]
            fw.op("act", lambda e, a_=a_, pb=pb: e.copy(a_[:], pb[:, 0:128]), reads=[pb], writes=[a_])
            fw.dma("sp", ab_s[tt * 128:(tt + 1) * 128, :], a_[:], src=a_)
        for cb in range(24):
            buf = st.get(cb)
            if cb < 16:
                for jj in range(4):
                    cc = cb * 4 + jj
                    z = zc[ncc % 2]
                    o_ = ob[ncc % 2]
                    ncc += 1
                    for (t0, nt, zo) in tblocks:
                        pb = PS[npb % 8]
                        npb += 1
                        for kc in range(KC):
                            mm(pb[:, 0:nt], buf[:, kc, jj * 128:(jj + 1) * 128], hT[:, kc, t0:t0 + nt], kc == 0, kc == KC - 1,
                               [buf, hT], pb)
                        fw.op("act", lambda e, z=z, pb=pb, zo=zo, nt=nt: e.copy(z[:, zo:zo + nt], pb[:, 0:nt]), reads=[pb], writes=[z])
                    fw.op("dve", lambda e, z=z, cc=cc: e.tensor_scalar(out=acc[:], in0=z[:, 0:ZN], scalar1=cw[:, cc:cc + 1], scalar2=None,
                                                                       op0=ALU.mult), reads=[z, cw], writes=[acc])
                    for j in range(1, 5):
                        fw.op("dve", lambda e, z=z, cc=cc, j=j: e.scalar_tensor_tensor(
                            out=acc[:], in0=z[:, j:j + ZN], scalar=cw[:, j * 64 + cc:j * 64 + cc + 1], in1=acc[:], op0=ALU.mult, op1=ALU.add),
                            reads=[z, cw, acc], writes=[acc])
                    if cc >= 32:
                        fw.op("act", lambda e, o_=o_: e.activation(out=o_[:], in_=acc[:], func=AF.Silu), reads=[acc], writes=[o_])
                        dst_s, r0 = vT_s, (cc - 32) * 128
                    else:
                        fw.op("act", lambda e: e.activation(out=sl[:], in_=acc[:], func=AF.Silu), reads=[acc], writes=[sl])
                        fw.op("pool", lambda e: e.tensor_tensor(out=sqq[:], in0=sl[:], in1=sl[:], op=ALU.mult), reads=[sl], writes=[sqq])
                        for c0 in range(0, ZN, 512):
                            n = min(512, ZN - c0)
                            pb = PS[npb % 8]
                            npb += 1
                            mm(pb[:, 0:n], ones1b[:], sqq[:, c0:c0 + n], True, True, [ones1b, sqq], pb)
                            rsqrt(rs[:, c0:c0 + n], rs, pb[:, 0:n], pb, 1.0)
                        qs = 128 ** -0.5 if cc < 16 else 1.0
                        fw.op("dve", lambda e, o_=o_, qs=qs: e.scalar_tensor_tensor(out=o_[:], in0=sl[:], scalar=qs, in1=rs[:],
                                                                                   op0=ALU.mult, op1=ALU.mult), reads=[sl, rs], writes=[o_])
                        dst_s, r0 = (qT_s, cc * 128) if cc < 16 else (kT_s, (cc - 16) * 128)
                    fw.dma("sp", dst_s[r0:r0 + 128, 0:CTX], o_[:, 0:CTX], src=o_)
                    fw.dma("sp", dst_s[r0:r0 + 128, CTX:T], o_[:, 260:260 + SEQ], src=o_)
            else:
                col0 = (cb - 16) * 512
                for tt in range(NTILE):
                    pb = PS[npb % 8]
                    npb += 1
                    for kc in range(KC):
                        mm(pb[:], hT[:, kc, tt * 128:(tt + 1) * 128], buf[:, kc, :], kc == 0, kc == KC - 1, [buf, hT], pb)
                    vs = vst[nv % 3]
                    nv += 1
                    fw.op("act", lambda e, vs=vs, pb=pb: e.activation(out=vs[:], in_=pb[:], func=AF.Silu), reads=[pb], writes=[vs])
                    fw.dma("sp", g_s[tt * 128:(tt + 1) * 128, col0:col0 + 512], vs[:], src=vs)

    def gdn_scan(b):
        BETA = fw.sb("BETA", [128, NTILE, 64], F32)
        GB = fw.sb("GB", [128, NTILE, 64], F32)
        EGB = fw.sb("EGB", [128, NTILE, 64], F32)
        KDEC = fw.sb("KDEC", [128, NTILE, 64], F32)
        BG = fw.sb("BG", [128, NTILE, 64], F32)
        EGL = fw.sb("EGL", [128, NTILE, 64], F32)
        NGB = fw.sb("NGB", [128, NTILE, 64], F32)
        nwb = fw.sb("nwb", [128, 128], F32)
        gm = fw.sb("gm", [128, 4, 128], F32)
        rowt = fw.sb("rowt", [1, 256], F32)
        fw.dma("sp", rowt[0:1, 0:128], gdn_norm_w, dst=rowt)
        fw.dma("sp", rowt[0:1, 128:192], gdn_dt_bias, dst=rowt)
        fw.dma("sp", rowt[0:1, 192:256], gdn_a_log, dst=rowt)
        mm(PS[4][:, 0:256], ones1f[0:1, 0:128], rowt[0:1, :], True, True, [ones1f, rowt], PS[4])
        fw.op("dve", lambda e: e.tensor_copy(nwb[:], PS[4][:, 0:128]), reads=[PS[4]], writes=[nwb])
        fw.dma("sp", gm[:], gmask_in.rearrange("p (i m) -> p i m", i=4), dst=gm)
        m0 = fw.mark()
        abt = fw.sb("abt", [128, NTILE, 128], F32)
        fw.dma("sp", abt[:], ab_s.rearrange("(j p) n -> p j n", p=128), dst=abt)
        dtb = fw.sb("dtb", [128, 64], F32)
        negA = fw.sb("negA", [128, 64], F32)
        fw.op("dve", lambda e: e.tensor_copy(dtb[:], PS[4][:, 128:192]), reads=[PS[4]], writes=[dtb])
        fw.op("dve", lambda e: e.tensor_copy(negA[:], PS[4][:, 192:256]), reads=[PS[4]], writes=[negA])
        fw.op("act", lambda e: e.activation(out=negA[:], in_=negA[:], func=AF.Exp), reads=[negA], writes=[negA])
        fw.op("dve", lambda e: e.tensor_scalar(out=negA[:], in0=negA[:], scalar1=-1.0, scalar2=None, op0=ALU.mult), reads=[negA], writes=[negA])
        G = fw.sb("G", [128, NTILE, 64], F32)
        GS = fw.sb("GS", [128, NTILE, 64], F32)
        if stop == "g1":
            dd = nc.dram_tensor("dbg_abt", list(abt.t.shape[:1]) + [int(np.prod(abt.t.shape[1:]))], F32, kind="ExternalOutput").ap()
            fw.dma("sp", dd, abt[:] if len(abt.t.shape) == 2 else abt[:].rearrange("p j n -> p (j n)"), src=abt)
            dd = nc.dram_tensor("dbg_dtb", list(dtb.t.shape[:1]) + [int(np.prod(dtb.t.shape[1:]))], F32, kind="ExternalOutput").ap()
            fw.dma("sp", dd, dtb[:] if len(dtb.t.shape) == 2 else dtb[:].rearrange("p j n -> p (j n)"), src=dtb)
            dd = nc.dram_tensor("dbg_negA", list(negA.t.shape[:1]) + [int(np.prod(negA.t.shape[1:]))], F32, kind="ExternalOutput").ap()
            fw.dma("sp", dd, negA[:] if len(negA.t.shape) == 2 else negA[:].rearrange("p j n -> p (j n)"), src=negA)
            dd = nc.dram_tensor("dbg_nwb", list(nwb.t.shape[:1]) + [int(np.prod(nwb.t.shape[1:]))], F32, kind="ExternalOutput").ap()
            fw.dma("sp", dd, nwb[:] if len(nwb.t.shape) == 2 else nwb[:].rearrange("p j n -> p (j n)"), src=nwb)
            fw.barrier()
            return
        for tt in range(NTILE):
            for d in range(2):
                fw.op("dve", lambda e, tt=tt, d=d: e.tensor_tensor(out=G[:, tt, d * 32:(d + 1) * 32], in0=abt[:, tt, d * 64:d * 64 + 32],
                                                                   in1=dtb[:, d * 32:(d + 1) * 32], op=ALU.add), reads=[abt, dtb], writes=[G])
                fw.op("dve", lambda e, tt=tt, d=d: e.tensor_copy(BETA[:, tt, d * 32:(d + 1) * 32], abt[:, tt, d * 64 + 32:d * 64 + 64]),
                      reads=[abt], writes=[BETA])
        fw.op("act", lambda e: e.activation(out=G[:], in_=G[:], func=AF.Exp), reads=[G], writes=[G])
        fw.op("act", lambda e: e.activation(out=G[:], in_=G[:], func=AF.Ln, bias=ones1f[:, 0:1]), reads=[G, ones1f], writes=[G])
        fw.op("act", lambda e: e.activation(out=BETA[:], in_=BETA[:], func=AF.Sigmoid), reads=[BETA], writes=[BETA])
        for tt in range(NTILE):
            fw.op("dve", lambda e, tt=tt: e.tensor_tensor(out=G[:, tt, :], in0=G[:, tt, :], in1=negA[:], op=ALU.mult), reads=[G, negA], writes=[G])
        if stop == "g2":
            dd = nc.dram_tensor("dbg_G", list(G.t.shape[:1]) + [int(np.prod(G.t.shape[1:]))], F32, kind="ExternalOutput").ap()
            fw.dma("sp", dd, G[:] if len(G.t.shape) == 2 else G[:].rearrange("p j n -> p (j n)"), src=G)
            dd = nc.dram_tensor("dbg_BETA", list(BETA.t.shape[:1]) + [int(np.prod(BETA.t.shape[1:]))], F32, kind="ExternalOutput").ap()
            fw.dma("sp", dd, BETA[:] if len(BETA.t.shape) == 2 else BETA[:].rearrange("p j n -> p (j n)"), src=BETA)
            fw.barrier()
            return
        for tt in range(NTILE):
            pb = PS[tt % 4]
            for d in range(2):
                tri = gm[:, 3, :] if d == 0 else gm[:, 1, :]
                mm(pb[:, d * 32:(d + 1) * 32], tri, G[:, tt, d * 32:(d + 1) * 32], True, True, [gm, G], pb)
            mm(pb[:, 64:128], ones1f[:], G[:, tt, :], True, True, [ones1f, G], pb)
            fw.op("act", lambda e, tt=tt, pb=pb: e.copy(GB[:, tt, :], pb[:, 0:64]), reads=[pb], writes=[GB])
            fw.op("dve", lambda e, tt=tt, pb=pb: e.tensor_copy(GS[:, tt, :], pb[:, 64:128]), reads=[pb], writes=[GS])
        if stop == "g3":
            dd = nc.dram_tensor("dbg_GB", list(GB.t.shape[:1]) + [int(np.prod(GB.t.shape[1:]))], F32, kind="ExternalOutput").ap()
            fw.dma("sp", dd, GB[:] if len(GB.t.shape) == 2 else GB[:].rearrange("p j n -> p (j n)"), src=GB)
            dd = nc.dram_tensor("dbg_GS", list(GS.t.shape[:1]) + [int(np.prod(GS.t.shape[1:]))], F32, kind="ExternalOutput").ap()
            fw.dma("sp", dd, GS[:] if len(GS.t.shape) == 2 else GS[:].rearrange("p j n -> p (j n)"), src=GS)
            fw.barrier()
            return
        fw.op("dve", lambda e: e.tensor_scalar(out=NGB[:], in0=GB[:], scalar1=-1.0, scalar2=None, op0=ALU.mult), reads=[GB], writes=[NGB])
        fw.op("act", lambda e: e.activation(out=EGB[:], in_=GB[:], func=AF.Exp), reads=[GB], writes=[EGB])
        fw.op("act", lambda e: e.activation(out=EGL[:], in_=GS[:], func=AF.Exp), reads=[GS], writes=[EGL])
        fw.op("dve", lambda e: e.tensor_tensor(out=KDEC[:], in0=GS[:], in1=GB[:], op=ALU.subtract), reads=[GS, GB], writes=[KDEC])
        fw.op("act", lambda e: e.activation(out=KDEC[:], in_=KDEC[:], func=AF.Exp), reads=[KDEC], writes=[KDEC])
        fw.op("dve", lambda e: e.tensor_tensor(out=BG[:], in0=BETA[:], in1=EGB[:], op=ALU.mult), reads=[BETA, EGB], writes=[BG])
        fw.release_to(m0)
        if stop == "gates":
            for nm, t_ in (("GB", GB), ("BETA", BETA), ("EGL", EGL), ("KDEC", KDEC), ("BG", BG)):
                dd = nc.dram_tensor("dbg_" + nm, [128, NTILE * 64], F32, kind="ExternalOutput").ap()
                fw.dma("sp", dd, t_[:].rearrange("p j n -> p (j n)"), src=t_)
            fw.barrier()
            return

        qh = [fw.sb("gq%d" % i, [128, T], BF16) for i in range(2)]
        kh = [fw.sb("gk%d" % i, [128, T], BF16) for i in range(2)]
        vh = [fw.sb("gv%d" % i, [128, 2, T], BF16) for i in range(2)]
        zh = [fw.sb("gz%d" % i, [128, NTILE, 256], BF16) for i in range(2)]
        oacc = [fw.sb("goacc%d" % j, [128, 256], F32) for j in range(NTILE)]
        yst = fw.sb("gyst", [128, 2, T], BF16)
        NCH = 4
        NW = 4
        NCS = 8
        U = [[fw.sb("U%d_%d" % (c, w), [128, 128], BF16) for w in range(NW)] for c in range(NCH)]
        WT = [[fw.sb("WT%d_%d" % (c, w), [128, 128], BF16) for w in range(NW)] for c in range(NCH)]
        KP = [[fw.sb("KP%d_%d" % (c, w), [128, 128], BF16) for w in range(NW)] for c in range(NCH)]
        ATT = [[fw.sb("ATT%d_%d" % (c, w), [128, 128], BF16) for w in range(NW)] for c in range(NCH)]
        S = [fw.sb("S%d" % c, [128, 128], F32) for c in range(NCH)]
        Sb = [fw.sb("Sb%d" % c, [128, 128], BF16) for c in range(NCH)]
        kkS = [fw.sb("kkS%d" % i, [128, 128], F32) for i in range(4)]
        qkI = [fw.sb("qkI%d" % i, [128, 128], F32) for i in range(4)]
        dg = [fw.sb("dg%d" % i, [128, 128], F32) for i in range(NCS)]
        tq = [fw.sb("tq%d" % i, [128, 128], F32) for i in range(NCS)]
        Dm = [fw.sb("Dm%d" % i, [128, 128], F32) for i in range(NCS)]
        MM = [fw.sb("MM%d" % i, [128, 256], BF16) for i in range(NCS)]
        Am = [fw.sb("Am%d" % i, [128, 128], BF16) for i in range(NCS)]
        TT = [[fw.sb("TT%d_%d" % (i, k), [128, 256], BF16) for k in range(2)] for i in range(NCS)]
        YY = [fw.sb("YY%d" % i, [128, 256], BF16) for i in range(NCS)]
        rb = [fw.sb("rb%d" % i, [128, 256], BF16) for i in range(NCS)]
        lvm = fw.sb("lvm", [128, 14, 256], BF16)
        fw.dma("pool", lvm[:], glv_in.rearrange("(i p) n -> p i n", p=128), dst=lvm)
        ident2b = fw.sb("ident2b", [128, 256], BF16)
        fw.op("dve", lambda e: e.tensor_copy(ident2b[:, 0:128], identb[:]), reads=[identb], writes=[ident2b])
        fw.op("dve", lambda e: e.tensor_copy(ident2b[:, 128:256], identb[:]), reads=[identb], writes=[ident2b])
        vn = [fw.sb("vn%d" % i, [128, 128], BF16) for i in range(NCH)]
        jk = [fw.sb("gjk%d" % i, [128, 128], F32) for i in range(2)]
        ss = fw.sb("gss", [128, NTILE * 2], F32)
        rstd = fw.sb("grstd", [128, NTILE * 2], F32)
        ynt = [fw.sb("gyn%d" % i, [128, 256], BF16) for i in range(2)]
        tmpn = [fw.sb("gtn%d" % i, [128, 256], F32) for i in range(2)]

        def load(i, hq):
            fw.dma("sp", qh[i % 2][:], qT_s[hq * 128:(hq + 1) * 128, :], dst=qh[i % 2])
            fw.dma("sp", kh[i % 2][:], kT_s[hq * 128:(hq + 1) * 128, :], dst=kh[i % 2])
            fw.dma("sp", vh[i % 2][:], vT_s[hq * 256:(hq + 1) * 256, :].rearrange("(c p) t -> p c t", p=128), dst=vh[i % 2])
            fw.dma("sp", zh[i % 2][:], g_s[:, hq * 256:(hq + 1) * 256].rearrange("(j p) e -> p j e", p=128), dst=zh[i % 2])
            return (qh[i % 2], kh[i % 2], vh[i % 2], zh[i % 2])
        st = Stream(list(range(16)), load, 2)
        rr = [0]
        pbn = [0]
        nsh = [0]

        def nbank():
            pbn[0] += 1
            return PS[2 + pbn[0] % 4]
        pan = [0]

        orders = [list(range(NTILE)), [1, 0] + list(range(NTILE - 1, 1, -1))]

        def prep(hq, q_, k_, v_, sis):
            cx = []
            for sx, si in enumerate(sis):
                w = si % NW
                for d in range(2):
                    j = orders[d][si]
                    tsl = slice(j * 128, (j + 1) * 128)
                    pa = PS[pan[0] % 2]
                    pan[0] += 1
                    mm(pa[:, 0:128], k_[:, tsl], k_[:, tsl], True, True, [k_], pa)
                    mm(pa[:, 128:256], q_[:, tsl], k_[:, tsl], True, True, [q_, k_], pa)
                    pav = pa[:].bitcast(BF16)
                    tr(pav[:, 512:640], k_[:, tsl], identb[:], [k_, identb], pa)
                    for hv2 in range(2):
                        tr(pav[:, 640 + hv2 * 128:768 + hv2 * 128], v_[:, hv2, tsl], identb[:], [v_, identb], pa)
                    ks = kkS[sx * 2 + d]
                    qi = qkI[sx * 2 + d]
                    fw.op("dve", lambda e, ks=ks, pa=pa, d=d: e.tensor_tensor(out=ks[:], in0=pa[:, 0:128], in1=gm[:, 2 * d, :], op=ALU.mult),
                          reads=[pa, gm], writes=[ks])
                    fw.op("dve", lambda e, qi=qi, pa=pa, d=d: e.tensor_tensor(out=qi[:], in0=pa[:, 128:256], in1=gm[:, 2 * d + 1, :], op=ALU.mult),
                          reads=[pa, gm], writes=[qi])
                    for hv2 in range(2):
                        ch = hv2 * 2 + d
                        cs = sx * 4 + ch
                        col = d * 32 + hq * 2 + hv2
                        fw.op("act", lambda e, cs=cs, pav=pav, hv2=hv2, j=j, col=col: e.activation(
                            out=rb[cs][:, 0:128], in_=pav[:, 640 + hv2 * 128:768 + hv2 * 128], func=AF.Identity, scale=BETA[:, j, col:col + 1]),
                            reads=[pa, BETA], writes=[rb[cs]])
                        fw.op("act", lambda e, cs=cs, pav=pav, j=j, col=col: e.activation(
                            out=rb[cs][:, 128:256], in_=pav[:, 512:640], func=AF.Identity, scale=BG[:, j, col:col + 1]),
                            reads=[pa, BG], writes=[rb[cs]])
                        kp = KP[ch][w]
                        fw.op("act", lambda e, kp=kp, pav=pav, j=j, col=col: e.activation(out=kp[:], in_=pav[:, 512:640], func=AF.Identity,
                                                                                          scale=KDEC[:, j, col:col + 1]), reads=[pa, KDEC], writes=[kp])
                        cx.append((cs, ch, w, d, j, col, ks, qi, PS[2 + cs // 2], (cs % 2) * 256))
            for (cs, ch, w, d, j, col, ks, qi, pp, c0) in cx:
                gbc = GB[:, j, col:col + 1]
                fw.op("pool", lambda e, cs=cs, gbc=gbc: e.tensor_scalar(out=dg[cs][:], in0=ident[:], scalar1=gbc, scalar2=None, op0=ALU.mult),
                      reads=[ident, GB], writes=[dg[cs]])
            for (cs, ch, w, d, j, col, ks, qi, pp, c0) in cx:
                mm(pp[:, c0:c0 + 128], ones1f[:], dg[cs][:], True, True, [ones1f, dg[cs]], pp)
            for (cs, ch, w, d, j, col, ks, qi, pp, c0) in cx:
                fw.op("act", lambda e, cs=cs, pp=pp, c0=c0, j=j, col=col: e.activation(out=tq[cs][:], in_=pp[:, c0:c0 + 128], func=AF.Relu,
                                                                                      bias=NGB[:, j, col:col + 1]), reads=[pp, NGB], writes=[tq[cs]])
            for (cs, ch, w, d, j, col, ks, qi, pp, c0) in cx:
                fw.op("act", lambda e, cs=cs: e.activation(out=Dm[cs][:], in_=tq[cs][:], func=AF.Exp, scale=-1.0), reads=[tq[cs]], writes=[Dm[cs]])
            for (cs, ch, w, d, j, col, ks, qi, pp, c0) in cx:
                fw.op("dve", lambda e, cs=cs, ks=ks, j=j, col=col: e.scalar_tensor_tensor(
                    out=MM[cs][:, 0:128], in0=Dm[cs][:], scalar=BETA[:, j, col:col + 1], in1=ks[:], op0=ALU.mult, op1=ALU.mult),
                    reads=[Dm[cs], BETA, ks], writes=[MM[cs]])
                fw.op("pool", lambda e, cs=cs, qi=qi: e.tensor_tensor(out=Am[cs][:], in0=Dm[cs][:], in1=qi[:], op=ALU.mult),
                      reads=[Dm[cs], qi], writes=[Am[cs]])
            for (cs, ch, w, d, j, col, ks, qi, pp, c0) in cx:
                ppv = pp[:].bitcast(BF16)
                b0 = 2 * (c0 + 128)
                tr(ppv[:, b0:b0 + 128], MM[cs][:, 0:128], identb[:], [MM[cs], identb], pp)
                tr(ppv[:, b0 + 128:b0 + 256], Am[cs][:], identb[:], [Am[cs], identb], pp)
            for (cs, ch, w, d, j, col, ks, qi, pp, c0) in cx:
                ppv = pp[:].bitcast(BF16)
                b0 = 2 * (c0 + 128)
                fw.op("act", lambda e, cs=cs, ppv=ppv, b0=b0: e.copy(MM[cs][:, 128:256], ppv[:, b0:b0 + 128]), reads=[pp], writes=[MM[cs]])
                att = ATT[ch][w]
                fw.op("dve", lambda e, att=att, ppv=ppv, b0=b0: e.tensor_copy(att[:], ppv[:, b0 + 128:b0 + 256]), reads=[pp], writes=[att])
            for (cs, ch, w, d, j, col, ks, qi, pp, c0) in cx:
                fw.op("pool", lambda e, cs=cs, d=d: e.tensor_tensor(out=YY[cs][:], in0=MM[cs][:], in1=lvm[:, d * 7, :], op=ALU.mult),
                      reads=[MM[cs], lvm], writes=[YY[cs]])
                fw.op("pool", lambda e, cs=cs: e.tensor_tensor(out=TT[cs][1][:], in0=ident2b[:], in1=YY[cs][:], op=ALU.add),
                      reads=[ident2b, YY[cs]], writes=[TT[cs][1]])
            for lv in range(1, 7):
                for (cs, ch, w, d, j, col, ks, qi, pp, c0) in cx:
                    Tc = TT[cs][lv % 2]
                    mm(pp[:, c0:c0 + 128], MM[cs][:, 128:256], Tc[:, 0:128], True, True, [MM[cs], Tc], pp)
                    mm(pp[:, c0 + 128:c0 + 256], MM[cs][:, 0:128], Tc[:, 128:256], True, True, [MM[cs], Tc], pp)
                for (cs, ch, w, d, j, col, ks, qi, pp, c0) in cx:
                    fw.op("dve", lambda e, cs=cs, pp=pp, c0=c0, d=d, lv=lv: e.tensor_tensor(out=YY[cs][:], in0=pp[:, c0:c0 + 256], in1=lvm[:, d * 7 + lv, :], op=ALU.mult),
                          reads=[pp, lvm], writes=[YY[cs]])
                for (cs, ch, w, d, j, col, ks, qi, pp, c0) in cx:
                    Tc = TT[cs][lv % 2]
                    mm(pp[:, c0:c0 + 256], identb[:], Tc[:, 0:256], True, False, [identb, Tc], pp)
                    mm(pp[:, c0:c0 + 128], Tc[:, 128:256], YY[cs][:, 0:128], False, True, [Tc, YY[cs]], pp)
                    mm(pp[:, c0 + 128:c0 + 256], Tc[:, 0:128], YY[cs][:, 128:256], False, True, [Tc, YY[cs]], pp)
                for (cs, ch, w, d, j, col, ks, qi, pp, c0) in cx:
                    Tn = TT[cs][(lv + 1) % 2]
                    fw.op("act", lambda e, Tn=Tn, pp=pp, c0=c0: e.copy(Tn[:], pp[:, c0:c0 + 256]), reads=[pp], writes=[Tn])
            for (cs, ch, w, d, j, col, ks, qi, pp, c0) in cx:
                Tf = TT[cs][1]
                mm(pp[:, c0:c0 + 128], Tf[:, 128:256], rb[cs][:, 0:128], True, True, [Tf, rb[cs]], pp)
                mm(pp[:, c0 + 128:c0 + 256], rb[cs][:, 128:256], Tf[:, 128:256], True, True, [Tf, rb[cs]], pp)
            for (cs, ch, w, d, j, col, ks, qi, pp, c0) in cx:
                u = U[ch][w]
                wt = WT[ch][w]
                fw.op("act", lambda e, u=u, pp=pp, c0=c0: e.copy(u[:], pp[:, c0:c0 + 128]), reads=[pp], writes=[u])
                fw.op("dve", lambda e, wt=wt, pp=pp, c0=c0: e.tensor_copy(wt[:], pp[:, c0 + 128:c0 + 256]), reads=[pp], writes=[wt])

        def step(hq, q_, si):
            w = si % NW
            first = si == 0
            lastst = si == NTILE - 1
            info = []
            for ch in range(NCH):
                hv2, d = ch // 2, ch % 2
                hv = hq * 2 + hv2
                j = orders[d][si]
                info.append((ch, hv2, d, d * 32 + hv, j, PS[6 + ch % 2]))
            for (ch, hv2, d, col, j, pz) in info:
                if first:
                    fw.op("dve", lambda e, ch=ch, w=w: e.tensor_copy(vn[ch][:], U[ch][w][:]), reads=[U[ch][w]], writes=[vn[ch]])
                else:
                    c0 = (ch // 2) * 256
                    mm(pz[:, c0:c0 + 128], WT[ch][w][:], Sb[ch][:], True, True, [WT[ch][w], Sb[ch]], pz)
                    mm(pz[:, c0 + 128:c0 + 256], q_[:, j * 128:(j + 1) * 128], Sb[ch][:], True, True, [q_, Sb[ch]], pz)
            for (ch, hv2, d, col, j, pz) in info:
                if not first:
                    c0 = (ch // 2) * 256
                    fw.op("dve", lambda e, ch=ch, w=w, pz=pz, c0=c0: e.tensor_tensor(out=vn[ch][:], in0=U[ch][w][:], in1=pz[:, c0:c0 + 128], op=ALU.subtract),
                          reads=[U[ch][w], pz], writes=[vn[ch]])
            for (ch, hv2, d, col, j, pz) in info:
                osl = oacc[j][:, hv2 * 128:(hv2 + 1) * 128]
                if not first:
                    c0 = (ch // 2) * 256
                    fw.op("dve", lambda e, osl=osl, pz=pz, c0=c0, j=j, col=col: e.scalar_tensor_tensor(
                        out=osl, in0=pz[:, c0 + 128:c0 + 256], scalar=EGB[:, j, col:col + 1], in1=osl, op0=ALU.mult, op1=ALU.add),
                        reads=[pz, EGB, oacc[j]], writes=[oacc[j]])
            for (ch, hv2, d, col, j, pz) in info:
                pq = PS[6 + ch % 2]
                c0 = (ch // 2) * 256
                mm(pq[:, c0:c0 + 128], ATT[ch][w][:], vn[ch][:], True, True, [ATT[ch][w], vn[ch]], pq)
                if not lastst:
                    mm(pq[:, c0 + 128:c0 + 256], KP[ch][w][:], vn[ch][:], True, True, [KP[ch][w], vn[ch]], pq)
            for (ch, hv2, d, col, j, pz) in info:
                pq = PS[6 + ch % 2]
                c0 = (ch // 2) * 256
                osl = oacc[j][:, hv2 * 128:(hv2 + 1) * 128]
                fw.op("dve", lambda e, osl=osl, pq=pq, c0=c0: e.tensor_tensor(out=osl, in0=pq[:, c0:c0 + 128], in1=osl, op=ALU.add),
                      reads=[pq, oacc[j]], writes=[oacc[j]])
                if not lastst:
                    if first:
                        fw.op("dve", lambda e, ch=ch, pq=pq, c0=c0: e.tensor_copy(S[ch][:], pq[:, c0 + 128:c0 + 256]), reads=[pq], writes=[S[ch]])
                    else:
                        fw.op("dve", lambda e, ch=ch, pq=pq, c0=c0, j=j, col=col: e.scalar_tensor_tensor(
                            out=S[ch][:], in0=S[ch][:], scalar=EGL[:, j, col:col + 1], in1=pq[:, c0 + 128:c0 + 256], op0=ALU.mult, op1=ALU.add),
                            reads=[S[ch], EGL, pq], writes=[S[ch]])
                    fw.op("pool", lambda e, ch=ch: e.tensor_copy(Sb[ch][:], S[ch][:]), reads=[S[ch]], writes=[Sb[ch]])

        for hq in range(16):
            q_, k_, v_, z_ = st.get(hq)
            for j in range(NTILE):
                fw.op("pool", lambda e, j=j: e.memset(oacc[j][:], 0.0), writes=[oacc[j]])
            for sp in range(0, NTILE + 2, 2):
                if sp < NTILE:
                    prep(hq, q_, k_, v_, [sp, sp + 1])
                if sp >= 2:
                    step(hq, q_, sp - 2)
                    step(hq, q_, sp - 1)
            for j in range(NTILE):
                for hv2 in range(2):
                    jj = jk[(j * 2 + hv2) % 2]
                    fw.op("act", lambda e, j=j, hv2=hv2, jj=jj: e.activation(out=jj[:], in_=oacc[j][:, hv2 * 128:(hv2 + 1) * 128], func=AF.Square),
                          reads=[oacc[j]], writes=[jj])
                    fw.op("dve", lambda e, j=j, hv2=hv2, jj=jj: e.tensor_reduce(out=ss[:, j * 2 + hv2:j * 2 + hv2 + 1], in_=jj[:], axis=AX.X, op=ALU.add),
                          reads=[jj], writes=[ss])
            rsqrt(rstd[:], rstd, ss[:], ss, 1.0 / 128)
            for j in range(NTILE):
                y = ynt[j % 2]
                tn = tmpn[j % 2]
                for hv2 in range(2):
                    fw.op("dve", lambda e, j=j, hv2=hv2, tn=tn: e.scalar_tensor_tensor(
                        out=tn[:, hv2 * 128:(hv2 + 1) * 128], in0=oacc[j][:, hv2 * 128:(hv2 + 1) * 128], scalar=rstd[:, j * 2 + hv2:j * 2 + hv2 + 1],
                        in1=nwb[:], op0=ALU.mult, op1=ALU.mult), reads=[oacc[j], rstd, nwb], writes=[tn])
                fw.op("pool", lambda e, j=j, y=y, tn=tn, z_=z_: e.tensor_tensor(out=y[:], in0=tn[:], in1=z_[:, j, :], op=ALU.mult),
                      reads=[tn, z_], writes=[y])
                pt = PS[6 + j % 2]
                ptv = pt[:].bitcast(BF16)
                for ec in range(2):
                    tr(ptv[:, ec * 128:(ec + 1) * 128], y[:, ec * 128:(ec + 1) * 128], identb[:], [y, identb], pt)
                fw.op("act", lambda e, j=j, ptv=ptv: e.copy(yst[:, :, j * 128:(j + 1) * 128],
                                                           ptv[:, 0:256].rearrange("p (c t) -> p c t", c=2)), reads=[pt], writes=[yst])
            fw.dma("sp", ynT[b, hq * 256:(hq + 1) * 256, :].rearrange("(c p) t -> p c t", p=128), yst[:], src=yst)


    def na_inproj(b, hT):
        wb = [fw.sb("wi%d" % i, [128, KC, 512], BF16) for i in range(2)]
        qst = [fw.sb("nqst%d" % i, [128, 512], BF16) for i in range(3)]

        def load(i, cb):
            buf = wb[i % 2]
            fw.dma("pool", buf[:], na_w_in[:, cb * 512:(cb + 1) * 512].rearrange("(k p) n -> p k n", p=128), dst=buf)
            return buf
        st = Stream(list(range(12)), load, 2)
        tblocks = [(0, CTX)] + [(CTX + 512 * j, 512) for j in range(4)]
        npb = 0
        nq = 0
        for cb in range(12):
            buf = st.get(cb)
            if cb < 8:
                dst_s = qT_s if cb < 4 else kT_s
                for jj in range(4):
                    r0 = (cb % 4) * 512 + jj * 128
                    for (t0, nt) in tblocks:
                        pb = PS[npb % 8]
                        npb += 1
                        for kc in range(KC):
                            mm(pb[:, 0:nt], buf[:, kc, jj * 128:(jj + 1) * 128], hT[:, kc, t0:t0 + nt], kc == 0, kc == KC - 1, [buf, hT], pb)
                        qs = qst[nq % 3]
                        nq += 1
                        if cb < 4:
                            fw.op("act", lambda e, qs=qs, pb=pb, nt=nt: e.activation(out=qs[:, 0:nt], in_=pb[:, 0:nt], func=AF.Identity, scale=128 ** -0.5),
                                  reads=[pb], writes=[qs])
                        else:
                            fw.op("dve", lambda e, qs=qs, pb=pb, nt=nt: e.tensor_copy(qs[:, 0:nt], pb[:, 0:nt]), reads=[pb], writes=[qs])
                        fw.dma("sp", dst_s[r0:r0 + 128, t0:t0 + nt], qs[:, 0:nt], src=qs)
            else:
                col0 = (cb - 8) * 512
                for tt in range(NTILE):
                    pb = PS[npb % 8]
                    npb += 1
                    for kc in range(KC):
                        mm(pb[:], hT[:, kc, tt * 128:(tt + 1) * 128], buf[:, kc, :], kc == 0, kc == KC - 1, [buf, hT], pb)
                    qs = qst[nq % 3]
                    nq += 1
                    if nq % 2 == 0:
                        fw.op("act", lambda e, qs=qs, pb=pb: e.copy(qs[:], pb[:]), reads=[pb], writes=[qs])
                    else:
                        fw.op("dve", lambda e, qs=qs, pb=pb: e.tensor_copy(qs[:], pb[:]), reads=[pb], writes=[qs])
                    fw.dma("sp", v_s[tt * 128:(tt + 1) * 128, col0:col0 + 512], qs[:], src=qs)

    def na_attn(b):
        qh = [fw.sb("nq%d" % i, [128, T], BF16) for i in range(2)]
        kh = [fw.sb("nk%d" % i, [128, T], BF16) for i in range(2)]
        vh = [fw.sb("nv%d" % i, [128, NTILE, 128], BF16) for i in range(2)]
        bs = [fw.sb("nb%d" % i, [128, 5, 640], F32) for i in range(2)]
        yst = [fw.sb("nyst%d" % i, [128, T], BF16) for i in range(2)]
        Ssb = [fw.sb("nS%d" % i, [128, 896], F32) for i in range(2)]
        Pb = [fw.sb("nP%d" % i, [128, 896], BF16) for i in range(2)]
        PT = [fw.sb("nPT%d" % i, [128, 896], BF16) for i in range(2)]
        mx = [fw.sb("nmx%d" % i, [128, 2], F32) for i in range(2)]
        sm = [fw.sb("nsm%d" % i, [128, 2], F32) for i in range(2)]
        on = [fw.sb("non%d" % i, [128, 128], BF16) for i in range(2)]

        def load(i, h):
            fw.dma("sp", qh[i % 2][:], qT_s[h * 128:(h + 1) * 128, :], dst=qh[i % 2])
            fw.dma("sp", kh[i % 2][:], kT_s[h * 128:(h + 1) * 128, :], dst=kh[i % 2])
            fw.dma("sp", vh[i % 2][:], v_s[:, h * 128:(h + 1) * 128].rearrange("(j p) e -> p j e", p=128), dst=vh[i % 2])
            fw.dma("sp", bs[i % 2][:], na_bias[h * 640:(h + 1) * 640, :].rearrange("(i p) n -> p i n", p=128), dst=bs[i % 2])
            return (qh[i % 2], kh[i % 2], vh[i % 2], bs[i % 2], yst[i % 2])
        st = Stream(list(range(16)), load, 2)
        nblk = 0
        for h in range(16):
            q_, k_, v_, b_, y_ = st.get(h)
            for qt in range(NTILE):
                kx = nblk % 2
                nblk += 1
                SA, SB, PTp, OP = PS[2 * kx], PS[2 * kx + 1], PS[4 + kx], PS[6 + kx]
                S_, P_, PT_, mx_, sm_, on_ = Ssb[kx], Pb[kx], PT[kx], mx[kx], sm[kx], on[kx]
                qsl = q_[:, qt * 128:(qt + 1) * 128]
                if qt < 2:
                    nk = 256
                    ktiles = [0, 1]
                    mm(SB[:, 128:384], qsl, k_[:, 0:256], True, True, [q_, k_], SB)
                    fw.op("act", lambda e, S_=S_, SB=SB: e.copy(S_[:, 0:256], SB[:, 128:384]), reads=[SB], writes=[S_])
                else:
                    i = qt - 2
                    lo = min(max(i - 2, 0), 11)
                    pat = 0 if i == 0 else 1 if i == 1 else 3 if i == 14 else 4 if i == 15 else 2
                    nk = 896
                    ktiles = [2 + lo + t for t in range(5)] + [0, 1]
                    kc0 = CTX + lo * 128
                    mm(SA[:, 0:512], qsl, k_[:, kc0:kc0 + 512], True, True, [q_, k_], SA)
                    mm(SB[:, 0:128], qsl, k_[:, kc0 + 512:kc0 + 640], True, True, [q_, k_], SB)
                    mm(SB[:, 128:384], qsl, k_[:, 0:256], True, True, [q_, k_], SB)
                    fw.op("dve", lambda e, S_=S_, SA=SA, b_=b_, pat=pat: e.tensor_tensor(out=S_[:, 0:512], in0=SA[:, 0:512], in1=b_[:, pat, 0:512], op=ALU.add),
                          reads=[SA, b_], writes=[S_])
                    fw.op("dve", lambda e, S_=S_, SB=SB, b_=b_, pat=pat: e.tensor_tensor(out=S_[:, 512:640], in0=SB[:, 0:128], in1=b_[:, pat, 512:640], op=ALU.add),
                          reads=[SB, b_], writes=[S_])
                    fw.op("act", lambda e, S_=S_, SB=SB: e.copy(S_[:, 640:896], SB[:, 128:384]), reads=[SB], writes=[S_])
                fw.op("dve", lambda e, S_=S_, mx_=mx_, nk=nk: e.tensor_reduce(out=mx_[:, 0:1], in_=S_[:, 0:nk], axis=AX.X, op=ALU.max),
                      reads=[S_], writes=[mx_])
                fw.op("dve", lambda e, mx_=mx_: e.tensor_scalar(out=mx_[:, 1:2], in0=mx_[:, 0:1], scalar1=-1.0, scalar2=None, op0=ALU.mult),
                      reads=[mx_], writes=[mx_])
                fw.op("act", lambda e, S_=S_, P_=P_, mx_=mx_, nk=nk: e.activation(out=P_[:, 0:nk], in_=S_[:, 0:nk], func=AF.Exp, bias=mx_[:, 1:2]),
                      reads=[S_, mx_], writes=[P_])
                fw.op("dve", lambda e, P_=P_, sm_=sm_, nk=nk: e.tensor_reduce(out=sm_[:, 0:1], in_=P_[:, 0:nk], axis=AX.X, op=ALU.add),
                      reads=[P_], writes=[sm_])
                fw.op("dve", lambda e, sm_=sm_: e.reciprocal(sm_[:, 1:2], sm_[:, 0:1]), reads=[sm_], writes=[sm_])
                ptv = PTp[:].bitcast(BF16)
                nkt = len(ktiles)
                for t in range(nkt):
                    tr(ptv[:, t * 128:(t + 1) * 128], P_[:, t * 128:(t + 1) * 128], identb[:], [P_, identb], PTp)
                h1 = (nkt + 1) // 2 * 128
                fw.op("act", lambda e, PT_=PT_, ptv=ptv, h1=h1: e.copy(PT_[:, 0:h1], ptv[:, 0:h1]), reads=[PTp], writes=[PT_])
                if nkt * 128 > h1:
                    fw.op("dve", lambda e, PT_=PT_, ptv=ptv, h1=h1, nk=nk: e.tensor_copy(PT_[:, h1:nk], ptv[:, h1:nk]), reads=[PTp], writes=[PT_])
                for t, kt in enumerate(ktiles):
                    mm(OP[:, 0:128], PT_[:, t * 128:(t + 1) * 128], v_[:, kt, :], t == 0, t == nkt - 1, [PT_, v_], OP)
                fw.op("act", lambda e, on_=on_, OP=OP, sm_=sm_: e.activation(out=on_[:], in_=OP[:, 0:128], func=AF.Identity, scale=sm_[:, 1:2]),
                      reads=[OP, sm_], writes=[on_])
                opv = OP[:].bitcast(BF16)
                tr(opv[:, 512:640], on_[:], identb[:], [on_, identb], OP)
                fw.op("act", lambda e, y_=y_, opv=opv, qt=qt: e.copy(y_[:, qt * 128:(qt + 1) * 128], opv[:, 512:640]), reads=[OP], writes=[y_])
            fw.dma("sp", ynT[b, h * 128:(h + 1) * 128, :], y_[:], src=y_)

    def token_blocks(last):
        blks = []
        if not last:
            blks.append(("ctx", None, 2))
        for b in range(BPC):
            for j in range(4):
                blks.append(("lat", b, b))
        out = []
        jj = {0: 0, 1: 0}
        for kind, b, v in blks:
            if kind == "ctx":
                out.append(([(0, 0, CTX), (1, 0, CTX)], 2))
            else:
                out.append(([(b, CTX + 512 * jj[b], 512)], v))
                jj[b] += 1
        return out

    def outproj(li, last, KO):
        ynb = [fw.sb("ynb%d" % i, [128, KO, 512], BF16) for i in range(2)]
        wob = [fw.sb("wob%d" % i, [128, KO, 512], BF16) for i in range(2)]
        ybuf = fw.sb("ybuf", [128, KC, 512], F32)
        sqc = [fw.sb("sqc%d" % i, [128, 512], BF16) for i in range(2)]
        rstd = fw.sb("rstdo", [128, 512], F32)
        xc = [fw.sb("xc%d" % i, [128, 512], F32) for i in range(4)]
        tm = [fw.sb("tm%d" % i, [128, 512], F32) for i in range(2)]
        blks = token_blocks(last)

        def load_y(i, blk):
            buf = ynb[i % 2]
            segs, v = blk
            o = 0
            for (b, t0, nt) in segs:
                fw.dma("sp", buf[:, :, o:o + nt], ynT[b, 0:KO * 128, t0:t0 + nt].rearrange("(c p) t -> p c t", p=128), dst=buf)
                o += nt
            return buf
        sty = Stream(blks, load_y, 2)
        witems = [(bi, g) for bi in range(len(blks)) for g in range(4)]

        def load_w(i, it):
            buf = wob[i % 2]
            g = it[1]
            fw.dma("sp", buf[:], wo_b[0:KO * 128, g * 512:(g + 1) * 512].rearrange("(c p) n -> p c n", p=128), dst=buf)
            return buf
        stw = Stream(witems, load_w, 2)
        npb = 0
        nx = 0
        for bi, (segs, v) in enumerate(blks):
            yb = sty.get(bi)
            pss = PS[7]
            for g in range(4):
                wbuf = stw.get(bi * 4 + g)
                for j in range(4):
                    fc = g * 4 + j
                    pb = PS[npb % 6]
                    npb += 1
                    for kc in range(KO):
                        mm(pb[:], wbuf[:, kc, j * 128:(j + 1) * 128], yb[:, kc, :], kc == 0, kc == KO - 1, [wbuf, yb], pb)
                    fw.op("act", lambda e, fc=fc, pb=pb: e.copy(ybuf[:, fc, :], pb[:]), reads=[pb], writes=[ybuf])
                    sq = sqc[fc % 2]
                    fw.op("pool", lambda e, fc=fc, sq=sq: e.tensor_tensor(out=sq[:], in0=ybuf[:, fc, :], in1=ybuf[:, fc, :], op=ALU.mult),
                          reads=[ybuf], writes=[sq])
                    mm(pss[:], onesb[:], sq[:], fc == 0, fc == KC - 1, [onesb, sq], pss)
            rsqrt(rstd[:], rstd, pss[:], pss, 1.0)
            for fc in range(KC):
                x_ = xc[nx % 4]
                t_ = tm[nx % 2]
                nx += 1
                o = 0
                for (b, t0, nt) in segs:
                    fw.dma("sp", x_[:, o:o + nt], xT[b, fc * 128:(fc + 1) * 128, t0:t0 + nt], dst=x_)
                    o += nt
                fw.op("pool", lambda e, fc=fc, t_=t_: e.tensor_tensor(out=t_[:], in0=ybuf[:, fc, :], in1=rstd[:], op=ALU.mult),
                      reads=[ybuf, rstd], writes=[t_])
                fw.op("dve", lambda e, fc=fc, t_=t_, x_=x_, v=v: e.scalar_tensor_tensor(
                    out=x_[:], in0=t_[:], scalar=G1[:, fc, v:v + 1], in1=x_[:], op0=ALU.mult, op1=ALU.add),
                    reads=[t_, G1, x_], writes=[x_])
                o = 0
                for (b, t0, nt) in segs:
                    fw.dma("sp", xT[b, fc * 128:(fc + 1) * 128, t0:t0 + nt], x_[:, o:o + nt], src=x_)
                    o += nt

    def mlp(li, last):
        xb = fw.sb("xb", [128, KC, 512], F32)
        h2 = fw.sb("h2", [128, KC, 512], BF16)
        hid = fw.sb("hid", [128, 64, 512], BF16)
        w1b = [fw.sb("w1b%d" % i, [128, KC, 512], BF16) for i in range(2)]
        w2b = [fw.sb("w2b%d" % i, [128, 64, 128], BF16) for i in range(2)]
        sqb = fw.sb("sqb", [128, 2, 512], BF16)
        tmpb = fw.sb("tmpb", [128, 2, 512], F32)
        rstd = fw.sb("rstdm", [128, 512], F32)
        xc = [fw.sb("xcm%d" % i, [128, 512], F32) for i in range(3)]
        blks = token_blocks(last)
        w1items = [(bi, hb) for bi in range(len(blks)) for hb in range(16)]
        w2items = [(bi, fc) for bi in range(len(blks)) for fc in range(16)]

        def load_w1(i, it):
            buf = w1b[i % 2]
            hb = it[1]
            fw.dma("sp", buf[:], w1_b[:, hb * 512:(hb + 1) * 512].rearrange("(c p) n -> p c n", p=128), dst=buf)
            return buf

        def load_w2(i, it):
            buf = w2b[i % 2]
            fw.dma("sp", buf[:], w2_b[it[1]], dst=buf)
            return buf
        st1 = Stream(w1items, load_w1, 2)
        st2 = Stream(w2items, load_w2, 2)
        npb = 0
        nx = 0
        for bi, (segs, v) in enumerate(blks):
            o = 0
            for (b, t0, nt) in segs:
                fw.dma("sp", xb[:, :, o:o + nt], xT[b, :, t0:t0 + nt].rearrange("(c p) t -> p c t", p=128), dst=xb)
                o += nt
            norm_block(xb, 512, A2, 48, v, lambda c: (h2[:, c, :], h2), sqb, tmpb, rstd, PS[7])
            for hb in range(16):
                wbuf = st1.get(bi * 16 + hb)
                for j in range(4):
                    pb = PS[npb % 6]
                    npb += 1
                    for kc in range(KC):
                        mm(pb[:], wbuf[:, kc, j * 128:(j + 1) * 128], h2[:, kc, :], kc == 0, kc == KC - 1, [wbuf, h2], pb)
                    hc = hb * 4 + j
                    rl = tmpb
                    fw.op("act", lambda e, hc=hc, pb=pb: e.activation(out=tmpb[:, hc % 2, :], in_=pb[:], func=AF.Relu), reads=[pb], writes=[tmpb])
                    fw.op("dve", lambda e, hc=hc, pb=pb: e.tensor_tensor(out=hid[:, hc, :], in0=pb[:], in1=tmpb[:, hc % 2, :], op=ALU.mult),
                          reads=[pb, tmpb], writes=[hid])
            pss = PS[7]
            for fc in range(KC):
                wbuf = st2.get(bi * 16 + fc)
                pb = PS[npb % 6]
                npb += 1
                for kc in range(64):
                    mm(pb[:], wbuf[:, kc, :], hid[:, kc, :], kc == 0, kc == 63, [wbuf, hid], pb)
                fw.op("act", lambda e, fc=fc, pb=pb: e.copy(xb[:, fc, :], pb[:]), reads=[pb], writes=[xb])
                fw.op("pool", lambda e, fc=fc: e.tensor_tensor(out=sqb[:, fc % 2, :], in0=xb[:, fc, :], in1=xb[:, fc, :], op=ALU.mult),
                      reads=[xb], writes=[sqb])
                mm(pss[:], onesb[:], sqb[:, fc % 2, :], fc == 0, fc == KC - 1, [onesb, sqb], pss)
            rsqrt(rstd[:], rstd, pss[:], pss, 1.0)
            for fc in range(KC):
                x_ = xc[nx % 3]
                nx += 1
                o = 0
                for (b, t0, nt) in segs:
                    fw.dma("sp", x_[:, o:o + nt], xT[b, fc * 128:(fc + 1) * 128, t0:t0 + nt], dst=x_)
                    o += nt
                fw.op("pool", lambda e, fc=fc: e.tensor_tensor(out=tmpb[:, fc % 2, :], in0=xb[:, fc, :], in1=rstd[:], op=ALU.mult),
                      reads=[xb, rstd], writes=[tmpb])
                fw.op("dve", lambda e, fc=fc, x_=x_, v=v: e.scalar_tensor_tensor(
                    out=x_[:], in0=tmpb[:, fc % 2, :], scalar=G2[:, fc, v:v + 1], in1=x_[:], op0=ALU.mult, op1=ALU.add),
                    reads=[tmpb, G2, x_], writes=[x_])
                o = 0
                for (b, t0, nt) in segs:
                    fw.dma("sp", xT[b, fc * 128:(fc + 1) * 128, t0:t0 + nt], x_[:, o:o + nt], src=x_)
                    o += nt

    def norm1_to_hT(b, hT):
        xl = [fw.sb("xl%d" % i, [128, KC, 256], F32) for i in range(2)]
        sqb = fw.sb("sqn", [128, 2, 256], BF16)
        tmpb = fw.sb("tmpn", [128, 2, 256], F32)
        rstd = fw.sb("rstdn", [128, 256], F32)
        for i in range(T // 256):
            t0 = i * 256
            xb = xl[i % 2]
            fw.dma("sp", xb[:], xT[b, :, t0:t0 + 256].rearrange("(c p) t -> p c t", p=128), dst=xb)
            v = 2 if t0 < CTX else b
            norm_block(xb, 256, A1, 0, v, lambda c, t0=t0: (hT[:, c, t0:t0 + 256], hT), sqb, tmpb, rstd, PS[6 + i % 2])

    for li in layers:
        last = li == DEPTH - 1
        kind = li % 3
        adaln(li)
        if kind == 0:
            iret = li // 3
            W_in = ret_w_in[iret * D:(iret + 1) * D, :]
            W_out = ret_w_out[iret * 2 * D:(iret + 1) * 2 * D, :]
            KO = 32
        elif kind == 1:
            W_out = gdn_w_out
            KO = 32
        else:
            W_out = na_w_out
            KO = 16
        if not os.environ.get("KDBG_SKIP"):
            dmy = precast(li, W_out, KO)
        for b in range(BPC):
            if not os.environ.get("KDBG_SKIP"):
                hT = fw.sb("hT", [128, KC, T], BF16)
                m0 = fw.mark()
                norm1_to_hT(b, hT)
                fw.release_to(m0)
                if kind == 0:
                    ret_inproj(b, W_in, hT, last)
                elif kind == 1:
                    gdn_inproj(b, hT)
                else:
                    na_inproj(b, hT)
            fw.phase()
            if stop == "inproj":
                fw.emit()
                return nc, fw
            if kind == 0:
                ret_scan(b, last)
            elif kind == 2:
                na_attn(b)
            elif kind == 1:
                gdn_scan(b)
                if stop in ("gates", "g1", "g2", "g3"):
                    fw.emit()
                    return nc, fw
            fw.phase()
        outproj(li, last, KO)
        fw.phase()
        mlp(li, last)
        fw.phase()

    xo = [fw.sb("xo%d" % i, [128, KC, 128], F32) for i in range(2)]
    yo = [fw.sb("yo%d" % i, [128, D], F32) for i in range(2)]
    n = 0
    for b in range(BPC):
        for tt in range(SEQ // 128):
            xi_ = xo[n % 2]
            yo_ = yo[n % 2]
            t0 = CTX + tt * 128
            fw.dma("sp", xi_[:], xT[b, :, t0:t0 + 128].rearrange("(c p) t -> p c t", p=128), dst=xi_)
            for g4 in range(4):
                psb = PS[(n * 4 + g4) % 8]
                for j in range(4):
                    kc = g4 * 4 + j
                    tr(psb[:, j * 128:(j + 1) * 128], xi_[:, kc, :], ident[:], [xi_, ident], psb)
                if g4 % 2 == 0:
                    fw.op("act", lambda e, yo_=yo_, psb=psb, g4=g4: e.copy(yo_[:, g4 * 512:(g4 + 1) * 512], psb[:]), reads=[psb], writes=[yo_])
                else:
                    fw.op("dve", lambda e, yo_=yo_, psb=psb, g4=g4: e.tensor_copy(yo_[:, g4 * 512:(g4 + 1) * 512], psb[:]), reads=[psb], writes=[yo_])
            fw.dma("sp", y_out[b * SEQ + tt * 128: b * SEQ + (tt + 1) * 128, :], yo_[:], src=yo_)
            n += 1
    fw.barrier()
    fw.emit()
    return nc, fw


_CACHE = {}


def _consts():
    cosT, sinT = _rope_tables()
    maskT, xi, zeta, _ = _ret_tables()
    return {
        "ident": np.eye(128, dtype=np.float32),
        "ropecos": cosT, "ropesin": sinT,
        "ret_maskT": np.ascontiguousarray(maskT.reshape(16 * 128, 128)),
        "ret_xi": xi, "ret_zeta": zeta,
        "gdn_masks": _gdn_masks(),
        "gdn_lvmasks": _gdn_lvmasks(),
    }


def make_in_maps(inputs, cores):
    f = lambda a: np.ascontiguousarray(np.asarray(a, dtype=np.float32))
    shared = {
        "ada_w": f(inputs["ada_w"]).reshape(DEPTH * D, 6 * D),
        "ada_b": f(inputs["ada_b"]).reshape(DEPTH * 96, 128),
        "norm_g": f(inputs["norm_g"]).reshape(DEPTH * 64, 128),
        "mlp_w1": f(inputs["mlp_w1"]).reshape(DEPTH * D, 4 * D),
        "mlp_w2": f(inputs["mlp_w2"]).reshape(DEPTH * 4 * D, D),
        "ret_w_in": f(inputs["ret_w_in"]).reshape(2 * D, 6 * D),
        "ret_w_out": f(inputs["ret_w_out"]).reshape(2 * 2 * D, D),
        "gdn_w_in": f(inputs["gdn_w_in"]).reshape(D, 6 * D),
        "gdn_conv_w": f(inputs["gdn_conv_w"]).reshape(5 * 64, 128),
        "gdn_w_ab": f(inputs["gdn_w_ab"]).reshape(2 * D, 64),
        "gdn_a_log": f(inputs["gdn_a_log"]).reshape(1, 64),
        "gdn_dt_bias": f(inputs["gdn_dt_bias"]).reshape(1, 64),
        "gdn_norm_w": f(inputs["gdn_norm_w"]).reshape(1, 128),
        "gdn_w_out": f(inputs["gdn_w_out"]).reshape(2 * D, D),
        "na_w_in": f(inputs["na_w_in"]).reshape(D, 3 * D),
        "na_w_out": f(inputs["na_w_out"]).reshape(D, D),
        "na_bias": _na_bias(f(inputs["na_rpb"]).reshape(16, 15, 31)),
    }
    shared.update(_consts())
    x = f(inputs["x"])
    ctx = f(inputs["ctx"])
    c = f(inputs["c"])
    c_ctx = f(inputs["c_ctx"])
    maps = []
    for core in cores:
        b0 = core * BPC
        m = dict(shared)
        m["x"] = x[b0:b0 + BPC].reshape(BPC * SEQ, D)
        m["ctx"] = ctx[b0:b0 + BPC].reshape(BPC * CTX, D)
        m["cs"] = np.stack([c[b0], c[b0 + 1], c_ctx])
        maps.append(m)
    return maps


def kernel(**inputs):
    if "nc" not in _CACHE:
        _CACHE["nc"] = build_program()[0]
    nc = _CACHE["nc"]
    maps = make_in_maps(inputs, list(range(NCORE)))
    res = run_bass_kernel_spmd(nc, maps, core_ids=list(range(NCORE)))
    out = np.concatenate([r["y"].reshape(BPC, SEQ, D) for r in res.results], axis=0)
    return out.astype(np.float32)
```

```python
import math
import os
import numpy as np
import concourse.bass as bass
import concourse.mybir as mybir
from concourse.bass_utils import run_bass_kernel_spmd

F32 = mybir.dt.float32
BF16 = mybir.dt.bfloat16
AF = mybir.ActivationFunctionType
ALU = mybir.AluOpType
AX = mybir.AxisListType

COMPUTE = ("pe", "act", "dve", "pool")
ALLQ = ("pe", "act", "dve", "pool", "sp")
SB_LO = 16512
SB_HI = 229344

D = 2048
KC = 16
NCORE = 8
BPC = 2
CTX = 256
SEQ = 2048
T = CTX + SEQ
NTILE = T // 128
DEPTH = 4
EPS = 1e-6


def _dsize(dt):
    return 4 if dt == F32 else 2


class DSem:
    __slots__ = ("h", "cnt")

    def __init__(self, h):
        self.h = h
        self.cnt = 0


class Buf:
    __slots__ = ("t", "w", "r", "dsem", "name", "wd", "rd", "persist", "excl")

    def __init__(self, t, name, persist=False, excl=False):
        self.excl = excl
        self.t = t
        self.name = name
        self.w = {}
        self.r = {}
        self.dsem = {}
        self.wd = False
        self.rd = False
        self.persist = persist

    def __getitem__(self, k):
        return self.t[k]


class Ins:
    __slots__ = ("fn", "waits", "dwaits", "inc", "dinc")

    def __init__(self, fn):
        self.fn = fn
        self.waits = []
        self.dwaits = []
        self.inc = False
        self.dinc = None


class FW:
    def __init__(self, nc):
        self.nc = nc
        self.q = {e: [] for e in ALLQ}
        self.seen = {e: {} for e in ALLQ}
        self.dseen = {e: {} for e in ALLQ}
        self.bufs = []
        self.free_dsems = {"hw": [], "sw": []}
        self.all_dsems = []
        self.sb_ptr = SB_LO
        self.sb_persist = SB_LO
        self.uid = 0
        self.ps = []
        for i in range(8):
            t = nc.alloc_psum_tensor("psb%d" % i, [128, 512], F32)
            b = Buf(t, "psb%d" % i, persist=True, excl=True)
            self.bufs.append(b)
            self.ps.append(b)

    def sb(self, name, shape, dtype, persist=False):
        n = 1
        for s in shape[1:]:
            n *= s
        nbytes = (n * _dsize(dtype) + 63) // 64 * 64
        off = self.sb_ptr
        self.sb_ptr += nbytes
        assert self.sb_ptr <= SB_HI, "SBUF overflow at %s: %d" % (name, self.sb_ptr - SB_LO)
        self.uid += 1
        t = self.nc.alloc_sbuf_tensor_at("%s_%d" % (name, self.uid), list(shape), dtype, offset=off)
        b = Buf(t, name, persist=persist)
        self.bufs.append(b)
        return b

    def persist_done(self):
        self.sb_persist = self.sb_ptr

    def view(self, ap, name):
        b = Buf(ap, name)
        self.bufs.append(b)
        return b

    def phase(self):
        self.barrier()
        keep = []
        for b in self.bufs:
            if b.persist:
                keep.append(b)
            else:
                for kind, ds in b.dsem.items():
                    self.free_dsems[kind].append(ds)
                b.dsem = {}
        self.bufs = keep
        self.sb_ptr = self.sb_persist

    def mark(self):
        return (self.sb_ptr, len(self.bufs))

    def release_to(self, m):
        self.barrier()
        ptr, nb = m
        for b in self.bufs[nb:]:
            assert not b.persist
            for kind, ds in b.dsem.items():
                self.free_dsems[kind].append(ds)
            b.dsem = {}
        self.bufs = self.bufs[:nb]
        self.sb_ptr = ptr

    def _get_dsem(self, b, kind):
        if kind not in b.dsem:
            if self.free_dsems[kind]:
                b.dsem[kind] = self.free_dsems[kind].pop()
            else:
                h = self.nc.alloc_semaphore("dq%s%d" % (kind, len(self.all_dsems)))
                b.dsem[kind] = DSem(h)
                self.all_dsems.append(b.dsem[kind])
        return b.dsem[kind]

    def _need(self, e, ins, oe, oidx):
        if self.seen[e].get(oe, -1) >= oidx:
            return
        self.seen[e][oe] = oidx
        ins.waits.append((oe, oidx))
        tgt = self.q[oe][oidx]
        assert tgt.dinc is None and tgt.fn is not None
        tgt.inc = True

    def _dneed(self, e, ins, b):
        for ds in b.dsem.values():
            if ds.cnt == 0:
                continue
            key = id(ds)
            val = 16 * ds.cnt
            if self.dseen[e].get(key, 0) >= val:
                continue
            self.dseen[e][key] = val
            ins.dwaits.append((ds.h, val))

    def _track(self, e, ins, idx, reads, writes, is_dma=False):
        for b in reads:
            if b is None:
                continue
            for oe, oidx in b.w.items():
                if oe == e and e == "pe" and not is_dma:
                    continue
                self._need(e, ins, oe, oidx)
            if b.excl:
                for oe, oidx in b.r.items():
                    if oe != e:
                        self._need(e, ins, oe, oidx)
            if b.wd:
                self._dneed(e, ins, b)
        for b in writes:
            if b is None:
                continue
            for oe, oidx in b.r.items():
                if oe == e and not is_dma:
                    continue
                self._need(e, ins, oe, oidx)
            for oe, oidx in b.w.items():
                if oe == e and not is_dma:
                    continue
                self._need(e, ins, oe, oidx)
            if is_dma:
                if b.rd:
                    self._dneed(e, ins, b)
            elif b.wd or b.rd:
                self._dneed(e, ins, b)
        if not is_dma:
            for b in reads:
                if b is None:
                    continue
                if b.r.get(e, -1) < idx:
                    b.r[e] = idx
            for b in writes:
                if b is None:
                    continue
                if b.r:
                    b.r = {}
                    b.w = {e: idx}
                else:
                    b.w[e] = idx
                if b.wd or b.rd:
                    b.wd = False
                    b.rd = False

    def op(self, e, fn, reads=(), writes=()):
        ins = Ins(fn)
        idx = len(self.q[e])
        self._track(e, ins, idx, reads, writes)
        self.q[e].append(ins)
        return ins

    def dma(self, e, out, in_, src=None, dst=None, **kw):
        def fn(eng):
            return eng.dma_start(out=out, in_=in_, **kw)
        ins = Ins(fn)
        idx = len(self.q[e])
        self._track(e, ins, idx, [src], [dst], is_dma=True)
        assert not (src is not None and dst is not None)
        b = dst if dst is not None else src
        ds = self._get_dsem(b, "sw" if e == "pool" else "hw")
        ds.cnt += 1
        ins.dinc = ds.h
        if dst is not None:
            dst.wd = True
            if dst.r:
                dst.r = {}
                dst.w = {}
        if src is not None:
            src.rd = True
        self.q[e].append(ins)
        return ins

    def barrier(self):
        lasts = {}
        for e in COMPUTE:
            for i in range(len(self.q[e]) - 1, -1, -1):
                if self.q[e][i].fn is not None and self.q[e][i].dinc is None:
                    lasts[e] = i
                    break
        for e in ALLQ:
            ins = Ins(None)
            for oe, oidx in lasts.items():
                self._need(e, ins, oe, oidx)
            for b in self.bufs:
                if b.wd or b.rd:
                    self._dneed(e, ins, b)
            self.q[e].append(ins)
        for b in self.bufs:
            b.wd = False
            b.rd = False
            b.w = {}
            b.r = {}

    def emit(self):
        nc = self.nc
        esem = {e: nc.alloc_semaphore("es_" + e) for e in COMPUTE}
        val = {}
        for e in COMPUTE:
            c = 0
            arr = []
            for ins in self.q[e]:
                if ins.inc:
                    c += 1
                arr.append(c)
            val[e] = arr
        q = self.q

        def run(e):
            def body(engine):
                for ins in q[e]:
                    for oe, oidx in ins.waits:
                        engine.wait_ge(esem[oe], val[oe][oidx])
                    for s, v in ins.dwaits:
                        engine.wait_ge(s, v)
                    if ins.fn is None:
                        continue
                    bi = ins.fn(engine)
                    if ins.dinc is not None:
                        bi.then_inc(ins.dinc, 16)
                    elif ins.inc:
                        bi.then_inc(esem[e], 1)
            return body

        with nc.Block() as block:
            block.sync(run("sp"))
            block.tensor(run("pe"))
            block.scalar(run("act"))
            block.vector(run("dve"))
            block.gpsimd(run("pool"))


RET_HEADS = 8
RET_DK = 256
RET_DV = 512
GRID_W = 64


def _rope_tables():
    t = np.arange(SEQ)
    row = (t // GRID_W).astype(np.float32)
    col = (t % GRID_W).astype(np.float32)
    n_pairs = RET_DK // 2
    inv = (10000.0 ** (-np.arange(0, n_pairs, 2, dtype=np.float32) / n_pairs)).astype(np.float32)
    ang = np.concatenate([row[:, None] * inv, col[:, None] * inv], axis=-1).astype(np.float32)
    return np.ascontiguousarray(np.cos(ang).T.astype(np.float32)), np.ascontiguousarray(np.sin(ang).T.astype(np.float32))


def _ret_tables():
    fwd = np.log(1.0 - 2.0 ** (-5.0 - np.arange(RET_HEADS, dtype=np.float64)))
    lg = np.stack([fwd, fwd[::-1]])
    C = 128
    pos = np.arange(C, dtype=np.float64)
    maskT = np.zeros((RET_HEADS * 2, C, C), np.float32)
    xi = np.zeros((C, RET_HEADS * 2), np.float32)
    zeta = np.zeros((C, RET_HEADS * 2), np.float32)
    cdec = np.zeros((RET_HEADS * 2,), np.float64)
    sc = RET_DK ** -0.5
    for h in range(RET_HEADS):
        for d in range(2):
            g = lg[d, h]
            i = h * 2 + d
            m = pos[:, None]
            c = pos[None, :]
            if d == 0:
                maskT[i] = np.where(c >= m, np.exp(g * np.maximum(c - m, 0)), 0.0) * sc
                xi[:, i] = np.exp(g * (pos + 1))
                zeta[:, i] = np.exp(g * (C - 1 - pos)) * sc
            else:
                maskT[i] = np.where(m >= c, np.exp(g * np.maximum(m - c, 0)), 0.0) * sc
                xi[:, i] = np.exp(g * (C - pos))
                zeta[:, i] = np.exp(g * pos) * sc
            cdec[i] = np.exp(g * C)
    return maskT, xi, zeta, cdec


def _gdn_masks():
    c = np.arange(128)[:, None]
    m = np.arange(128)[None, :]
    tiles = [(m < c), (m <= c), (m > c), (m >= c)]
    return np.ascontiguousarray(np.concatenate([t.astype(np.float32) for t in tiles], axis=1))


def _na_bias(rpb):
    out = np.empty((16, 5, 128, 640), np.float32)
    lq = np.arange(128)
    lk = np.arange(640)
    q_col = lq % 64
    k_col = lk % 64
    col_start = np.clip(q_col - 8, 0, 64 - 16)
    col_ok = (k_col[None] >= col_start[:, None]) & (k_col[None] < col_start[:, None] + 16)
    dc = np.clip(k_col[None] - q_col[:, None] + 15, 0, 30)
    for p, i in enumerate((0, 1, 2, 14, 15)):
        lo = min(max(i - 2, 0), 11)
        q_row = 2 * i + lq // 64
        k_row = 2 * lo + lk // 64
        row_start = np.clip(q_row - 4, 0, 32 - 8)
        row_ok = (k_row[None] >= row_start[:, None]) & (k_row[None] < row_start[:, None] + 8)
        dr = np.clip(k_row[None] - q_row[:, None] + 7, 0, 14)
        ok = row_ok & col_ok
        out[:, p] = np.where(ok[None], rpb[:, dr, dc], np.float32(-30000.0))
    return np.ascontiguousarray(out.reshape(16 * 5 * 128, 640))


def _gdn_lvmasks():
    c = np.arange(128)[:, None]
    m = np.arange(128)[None, :]
    out = np.zeros((2, 7, 128, 256), np.float32)
    for lv in range(7):
        b = 1 << lv
        low = ((c // (2 * b)) == (m // (2 * b))) & ((c % (2 * b)) >= b) & ((m % (2 * b)) < b)
        low = -low.astype(np.float32)
        out[0, lv, :, 0:128] = low
        out[0, lv, :, 128:256] = low.T
        out[1, lv, :, 0:128] = low.T
        out[1, lv, :, 128:256] = low
    return np.ascontiguousarray(out.reshape(14 * 128, 256))


class Stream:
    def __init__(self, items, load_fn, depth):
        self.items = items
        self.load_fn = load_fn
        self.depth = depth
        self.loaded = 0
        self.res = {}

    def get(self, i):
        while self.loaded < len(self.items) and self.loaded < i + self.depth:
            self.res[self.loaded] = self.load_fn(self.loaded, self.items[self.loaded])
            self.loaded += 1
        r = self.res.pop(i)
        return r


def build_program(layers=(0, 1, 2, 3), dbg=False, stop=None):
    nc = bass.Bass("TRN2", target_bir_lowering=False)
    fw = FW(nc)
    PS = fw.ps

    def din(name, shape, dt=F32):
        return nc.dram_tensor(name, list(shape), dt, kind="ExternalInput").ap()

    def dscr(name, shape, dt):
        return nc.dram_tensor(name, list(shape), dt).ap()

    x_in = din("x", [BPC * SEQ, D])
    ctx_in = din("ctx", [BPC * CTX, D])
    cs_in = din("cs", [3, D])
    ada_w = din("ada_w", [DEPTH * D, 6 * D])
    ada_b = din("ada_b", [DEPTH * 96, 128])
    norm_g = din("norm_g", [DEPTH * 64, 128])
    mlp_w1 = din("mlp_w1", [DEPTH * D, 4 * D])
    mlp_w2 = din("mlp_w2", [DEPTH * 4 * D, D])
    ret_w_in = din("ret_w_in", [2 * D, 6 * D])
    ret_w_out = din("ret_w_out", [2 * 2 * D, D])
    gdn_w_in = din("gdn_w_in", [D, 6 * D])
    gdn_conv_w = din("gdn_conv_w", [5 * 64, 128])
    gdn_w_ab = din("gdn_w_ab", [2 * D, 64])
    gdn_a_log = din("gdn_a_log", [1, 64])
    gdn_dt_bias = din("gdn_dt_bias", [1, 64])
    gdn_norm_w = din("gdn_norm_w", [1, 128])
    gdn_w_out = din("gdn_w_out", [2 * D, D])
    na_w_in = din("na_w_in", [D, 3 * D])
    na_w_out = din("na_w_out", [D, D])
    na_bias = din("na_bias", [16 * 5 * 128, 640])
    gmask_in = din("gdn_masks", [128, 4 * 128])
    glv_in = din("gdn_lvmasks", [14 * 128, 256])
    ident_in = din("ident", [128, 128])
    cos_in = din("ropecos", [128, SEQ])
    sin_in = din("ropesin", [128, SEQ])
    rmask_in = din("ret_maskT", [16 * 128, 128])
    rxi_in = din("ret_xi", [128, 16])
    rzeta_in = din("ret_zeta", [128, 16])
    y_out = nc.dram_tensor("y", [BPC * SEQ, D], F32, kind="ExternalOutput").ap()

    xT = (nc.dram_tensor("xT", [BPC, D, T], F32, kind="ExternalOutput").ap() if dbg else dscr("xT", [BPC, D, T], F32))
    def dscr_dbg0(name, shape, dt):
        if dbg:
            return nc.dram_tensor(name, list(shape), dt, kind="ExternalOutput").ap()
        return dscr(name, shape, dt)
    qT_s = dscr_dbg0("qT_s", [D, T], BF16)
    kT_s = dscr_dbg0("kT_s", [D, T], BF16)
    v_s = dscr("v_s", [T, 2 * D], BF16)
    g_s = dscr("g_s", [T, 2 * D], BF16)
    def dscr_dbg(name, shape, dt):
        if dbg:
            return nc.dram_tensor(name, list(shape), dt, kind="ExternalOutput").ap()
        return dscr(name, shape, dt)
    vT_s = dscr_dbg("vT_s", [2 * D, T], BF16)
    ynT = dscr("ynT", [BPC, 2 * D, T], BF16)
    wo_b = dscr("wo_b", [2 * D, D], BF16)
    w1_b = dscr("w1_b", [D, 4 * D], BF16)
    w2_b = dscr("w2_b", [16, 128, 64, 128], BF16)

    _, _, _, ret_cdec = _ret_tables()

    ident = fw.sb("ident", [128, 128], F32, persist=True)
    identb = fw.sb("identb", [128, 128], BF16, persist=True)
    onesb = fw.sb("onesb", [128, 128], BF16, persist=True)
    ones512 = fw.sb("ones512", [128, 128], BF16, persist=True)
    bvec = fw.sb("bvec", [128, DEPTH * 96], F32, persist=True)
    gvec = fw.sb("gvec", [128, DEPTH * 64], F32, persist=True)
    sT = fw.sb("sT", [128, KC, 3], BF16, persist=True)
    modv = fw.sb("modv", [128, 96, 3], F32, persist=True)
    A1 = fw.sb("A1", [128, KC, 3], F32, persist=True)
    G1 = fw.sb("G1", [128, KC, 3], F32, persist=True)
    A2 = fw.sb("A2", [128, KC, 3], F32, persist=True)
    G2 = fw.sb("G2", [128, KC, 3], F32, persist=True)
    epsc = fw.sb("epsc", [128, 1], F32, persist=True)
    ones1b = fw.sb("ones1b", [128, 128], BF16, persist=True)
    ones1f = fw.sb("ones1f", [128, 128], F32, persist=True)
    fw.persist_done()

    def mm(ps_ap, lhsT, rhs, start, stop, reads, psb):
        fw.op("pe", lambda e: e.matmul(ps_ap, lhsT, rhs, start=start, stop=stop), reads=reads, writes=[psb])

    def rsqrt(o_ap, o_buf, i_ap, i_buf, scale):
        fw.op("act", lambda e: e.activation(out=o_ap, in_=i_ap, func=AF.Sqrt, bias=epsc[:, 0:1], scale=scale),
              reads=[i_buf, epsc], writes=[o_buf])
        fw.op("dve", lambda e: e.reciprocal(o_ap, o_ap), reads=[o_buf], writes=[o_buf])

    def tr(ps_ap, in_ap, id_ap, reads, psb):
        fw.op("pe", lambda e: e.transpose(ps_ap, in_ap, id_ap), reads=reads, writes=[psb])

    fw.dma("sp", ident[:], ident_in, dst=ident)
    fw.op("dve", lambda e: e.tensor_copy(identb[:], ident[:]), reads=[ident], writes=[identb])
    fw.op("dve", lambda e: e.memset(onesb[:], 1.0 / D), writes=[onesb])
    fw.op("dve", lambda e: e.memset(ones512[:], 1.0 / 512), writes=[ones512])
    fw.op("dve", lambda e: e.memset(epsc[:], EPS), writes=[epsc])
    fw.op("dve", lambda e: e.memset(ones1b[:], 1.0), writes=[ones1b])
    fw.op("dve", lambda e: e.memset(ones1f[:], 1.0), writes=[ones1f])
    tmpv = fw.sb("tmpv", [128, 5, 128], F32)
    for i in range(3):
        fw.dma("sp", tmpv[:, i, :], ada_b[i * 128:(i + 1) * 128, :], dst=tmpv)
    for i in range(2):
        fw.dma("sp", tmpv[:, 3 + i, :], norm_g[i * 128:(i + 1) * 128, :], dst=tmpv)
    for i in range(5):
        tr(PS[0][:, i * 128:(i + 1) * 128] if i < 4 else PS[1][:, 0:128], tmpv[:, i, :], ident[:],
           [tmpv, ident], PS[0] if i < 4 else PS[1])
    fw.op("dve", lambda e: e.tensor_copy(bvec[:], PS[0][:, 0:384]), reads=[PS[0]], writes=[bvec])
    fw.op("dve", lambda e: e.tensor_copy(gvec[:, 0:128], PS[0][:, 384:512]), reads=[PS[0]], writes=[gvec])
    fw.op("dve", lambda e: e.tensor_copy(gvec[:, 128:256], PS[1][:, 0:128]), reads=[PS[1]], writes=[gvec])
    cs_sb = fw.sb("cs_sb", [3, D], F32)
    cs_sg = fw.sb("cs_sg", [3, D], F32)
    fw.dma("sp", cs_sb[:], cs_in, dst=cs_sb)
    fw.op("act", lambda e: e.activation(out=cs_sg[:], in_=cs_sb[:], func=AF.Sigmoid), reads=[cs_sb], writes=[cs_sg])
    fw.op("dve", lambda e: e.tensor_tensor(out=cs_sb[:], in0=cs_sb[:], in1=cs_sg[:], op=ALU.mult),
          reads=[cs_sb, cs_sg], writes=[cs_sb])
    for kc in range(KC):
        tr(PS[2][:, kc * 3:kc * 3 + 3], cs_sb[:, kc * 128:(kc + 1) * 128], ident[0:3, 0:3], [cs_sb, ident], PS[2])
    fw.op("dve", lambda e: e.tensor_copy(sT[:].rearrange("p k v -> p (k v)"), PS[2][:, 0:48]), reads=[PS[2]], writes=[sT])
    fw.phase()

    xin_b = [fw.sb("xin%d" % i, [128, D], F32) for i in range(2)]
    xst_b = [fw.sb("xst%d" % i, [128, KC, 128], F32) for i in range(2)]
    n = 0
    for b in range(BPC):
        for tt in range(NTILE):
            xin = xin_b[n % 2]
            xst = xst_b[n % 2]
            src = ctx_in[b * CTX + tt * 128: b * CTX + (tt + 1) * 128, :] if tt < 2 else \
                x_in[b * SEQ + (tt - 2) * 128: b * SEQ + (tt - 1) * 128, :]
            fw.dma("sp", xin[:], src, dst=xin)
            for g4 in range(4):
                psb = PS[(n * 4 + g4) % 8]
                for j in range(4):
                    kc = g4 * 4 + j
                    tr(psb[:, j * 128:(j + 1) * 128], xin[:, kc * 128:(kc + 1) * 128], ident[:], [xin, ident], psb)
                eng = "act" if g4 % 2 == 0 else "dve"
                dst_ap = xst[:, g4 * 4:(g4 + 1) * 4, :].rearrange("p k t -> p (k t)")
                if eng == "act":
                    fw.op("act", lambda e, o=dst_ap, p=psb: e.copy(o, p[:]), reads=[psb], writes=[xst])
                else:
                    fw.op("dve", lambda e, o=dst_ap, p=psb: e.tensor_copy(o, p[:]), reads=[psb], writes=[xst])
            fw.dma("sp", xT[b, :, tt * 128:(tt + 1) * 128].rearrange("(k p) t -> p k t", p=128), xst[:], src=xst)
            n += 1
    fw.phase()

    def adaln(li):
        wb = [fw.sb("adw%d" % i, [128, KC, 512], BF16) for i in range(3)]
        W = ada_w[li * D:(li + 1) * D, :]

        def load(i, cb):
            buf = wb[i % 3]
            fw.dma("pool", buf[:], W[:, cb * 512:(cb + 1) * 512].rearrange("(k p) n -> p k n", p=128), dst=buf)
            return buf
        st = Stream(list(range(24)), load, 3)
        for cb in range(24):
            buf = st.get(cb)
            psb = PS[cb % 2]
            for j in range(4):
                for kc in range(KC):
                    mm(psb[:, j * 4:j * 4 + 3], buf[:, kc, j * 128:(j + 1) * 128], sT[:, kc, :], kc == 0, kc == KC - 1,
                       [buf, sT], psb)
            for j in range(4):
                fc = cb * 4 + j
                fw.op("dve", lambda e, fc=fc, j=j, psb=psb: e.tensor_scalar(
                    out=modv[:, fc, :], in0=psb[:, j * 4:j * 4 + 3], scalar1=bvec[:, li * 96 + fc:li * 96 + fc + 1],
                    scalar2=None, op0=ALU.add), reads=[psb, bvec], writes=[modv])
        g0 = li * 64
        for c in range(KC):
            fw.op("dve", lambda e, c=c: e.tensor_scalar(out=A1[:, c, :], in0=modv[:, 16 + c, :], scalar1=1.0,
                                                         scalar2=gvec[:, g0 + c:g0 + c + 1], op0=ALU.add, op1=ALU.mult),
                  reads=[modv, gvec], writes=[A1])
            fw.op("dve", lambda e, c=c: e.tensor_scalar(out=G1[:, c, :], in0=modv[:, 32 + c, :],
                                                         scalar1=gvec[:, g0 + 16 + c:g0 + 16 + c + 1], scalar2=None, op0=ALU.mult),
                  reads=[modv, gvec], writes=[G1])
            fw.op("dve", lambda e, c=c: e.tensor_scalar(out=A2[:, c, :], in0=modv[:, 64 + c, :], scalar1=1.0,
                                                         scalar2=gvec[:, g0 + 32 + c:g0 + 32 + c + 1], op0=ALU.add, op1=ALU.mult),
                  reads=[modv, gvec], writes=[A2])
            fw.op("dve", lambda e, c=c: e.tensor_scalar(out=G2[:, c, :], in0=modv[:, 80 + c, :],
                                                         scalar1=gvec[:, g0 + 48 + c:g0 + 48 + c + 1], scalar2=None, op0=ALU.mult),
                  reads=[modv, gvec], writes=[G2])
        fw.phase()

    def precast(li, w_out_ap, KO):
        dmy = fw.sb("pcdummy", [128, 8], F32)
        rr_ = KO * 128 // 4
        for i in range(4):
            fw.dma("pool", wo_b[i * rr_:(i + 1) * rr_, :], w_out_ap[i * rr_:(i + 1) * rr_, :], dst=dmy)
        W1 = mlp_w1[li * D:(li + 1) * D, :]
        for i in range(4):
            fw.dma("pool", w1_b[i * 512:(i + 1) * 512, :], W1[i * 512:(i + 1) * 512, :], dst=dmy)
        W2 = mlp_w2[li * 4 * D:(li + 1) * 4 * D, :]
        for fc in range(16):
            fw.dma("pool", w2_b[fc], W2[:, fc * 128:(fc + 1) * 128].rearrange("(k p) c -> p k c", p=128), dst=dmy)
        return dmy

    def norm_block(xblk, nt, Avec, Bcol_base, v, out_fn, sqb, tmpb, rstd, psb):
        for kc in range(KC):
            fw.op("act", lambda e, kc=kc: e.activation(out=sqb[:, kc % 2, 0:nt], in_=xblk[:, kc, 0:nt], func=AF.Square),
                  reads=[xblk], writes=[sqb])
            mm(psb[:, 0:nt], onesb[:], sqb[:, kc % 2, 0:nt], kc == 0, kc == KC - 1, [onesb, sqb], psb)
        rsqrt(rstd[:, 0:nt], rstd, psb[:, 0:nt], psb, 1.0)
        for c in range(KC):
            fw.op("dve", lambda e, c=c: e.tensor_tensor(out=tmpb[:, c % 2, 0:nt], in0=xblk[:, c, 0:nt], in1=rstd[:, 0:nt],
                                                        op=ALU.mult), reads=[xblk, rstd], writes=[tmpb])
            o_ap, o_buf = out_fn(c)
            fw.op("act", lambda e, c=c, o_ap=o_ap: e.activation(
                out=o_ap, in_=tmpb[:, c % 2, 0:nt], func=AF.Identity,
                bias=modv[:, Bcol_base + c, v:v + 1], scale=Avec[:, c, v:v + 1]),
                reads=[tmpb, modv, Avec], writes=[o_buf])

    def ret_inproj(b, W, hT, last):
        wb = [fw.sb("wi%d" % i, [128, KC, 512], BF16) for i in range(2)]
        cosT = fw.sb("cosT", [128, SEQ], F32)
        sinT = fw.sb("sinT", [128, SEQ], F32)
        fw.dma("sp", cosT[:], cos_in, dst=cosT)
        fw.dma("sp", sinT[:], sin_in, dst=sinT)
        rt = [fw.sb("rt%d" % i, [128, 512], F32) for i in range(4)]
        qst = [fw.sb("qst%d" % i, [128, 2, 512], BF16) for i in range(2)]
        vst = [fw.sb("vst%d" % i, [128, 512], BF16) for i in range(3)]

        def load(i, cb):
            buf = wb[i % 2]
            fw.dma("pool", buf[:], W[:, cb * 512:(cb + 1) * 512].rearrange("(k p) n -> p k n", p=128), dst=buf)
            return buf
        st = Stream(list(range(24)), load, 2)
        tblocks = [(0, CTX)] + [(CTX + 512 * j, 512) for j in range(4)]
        nq = 0
        nv = 0
        npb = 0
        for cb in range(24):
            buf = st.get(cb)
            if cb < 8:
                dst_s = qT_s if cb < 4 else kT_s
                for hh in range(2):
                    row0 = (cb % 4) * 512 + hh * 256
                    for (t0, nt) in tblocks:
                        p1 = PS[npb % 8]
                        p2 = PS[(npb + 1) % 8]
                        npb += 2
                        for half, pb in ((0, p1), (1, p2)):
                            c0 = hh * 256 + half * 128
                            for kc in range(KC):
                                mm(pb[:, 0:nt], buf[:, kc, c0:c0 + 128], hT[:, kc, t0:t0 + nt], kc == 0, kc == KC - 1,
                                   [buf, hT], pb)
                        qs = qst[nq % 2]
                        nq += 1
                        if t0 < CTX:
                            fw.op("act", lambda e, qs=qs, p1=p1, nt=nt: e.copy(qs[:, 0, 0:nt], p1[:, 0:nt]), reads=[p1], writes=[qs])
                            fw.op("act", lambda e, qs=qs, p2=p2, nt=nt: e.copy(qs[:, 1, 0:nt], p2[:, 0:nt]), reads=[p2], writes=[qs])
                        else:
                            l0 = t0 - CTX
                            cs_ = cosT[:, l0:l0 + nt]
                            sn_ = sinT[:, l0:l0 + nt]
                            fw.op("dve", lambda e, p1=p1, cs_=cs_: e.tensor_tensor(out=rt[0][:], in0=p1[:], in1=cs_, op=ALU.mult),
                                  reads=[p1, cosT], writes=[rt[0]])
                            fw.op("dve", lambda e, p2=p2, sn_=sn_: e.tensor_tensor(out=rt[1][:], in0=p2[:], in1=sn_, op=ALU.mult),
                                  reads=[p2, sinT], writes=[rt[1]])
                            fw.op("dve", lambda e, p1=p1, sn_=sn_: e.tensor_tensor(out=rt[2][:], in0=p1[:], in1=sn_, op=ALU.mult),
                                  reads=[p1, sinT], writes=[rt[2]])
                            fw.op("dve", lambda e, p2=p2, cs_=cs_: e.tensor_tensor(out=rt[3][:], in0=p2[:], in1=cs_, op=ALU.mult),
                                  reads=[p2, cosT], writes=[rt[3]])
                            fw.op("pool", lambda e, qs=qs: e.tensor_tensor(out=qs[:, 0, :], in0=rt[0][:], in1=rt[1][:], op=ALU.subtract),
                                  reads=[rt[0], rt[1]], writes=[qs])
                            fw.op("pool", lambda e, qs=qs: e.tensor_tensor(out=qs[:, 1, :], in0=rt[2][:], in1=rt[3][:], op=ALU.add),
                                  reads=[rt[2], rt[3]], writes=[qs])
                        fw.dma("sp", dst_s[row0:row0 + 256, t0:t0 + nt].rearrange("(h p) t -> p h t", p=128),
                               qs[:, :, 0:nt], src=qs)
            else:
                dst_s = v_s if cb < 16 else g_s
                col0 = ((cb - 8) % 8) * 512
                for tt in range(NTILE):
                    if last and tt < 2 and cb >= 16:
                        continue
                    pb = PS[npb % 8]
                    npb += 1
                    for kc in range(KC):
                        mm(pb[:], hT[:, kc, tt * 128:(tt + 1) * 128], buf[:, kc, :], kc == 0, kc == KC - 1, [buf, hT], pb)
                    vs = vst[nv % 3]
                    nv += 1
                    if cb < 16:
                        if nv % 2 == 0:
                            fw.op("act", lambda e, vs=vs, pb=pb: e.copy(vs[:], pb[:]), reads=[pb], writes=[vs])
                        else:
                            fw.op("dve", lambda e, vs=vs, pb=pb: e.tensor_copy(vs[:], pb[:]), reads=[pb], writes=[vs])
                    else:
                        fw.op("act", lambda e, vs=vs, pb=pb: e.activation(out=vs[:], in_=pb[:], func=AF.Silu), reads=[pb], writes=[vs])
                    fw.dma("sp", dst_s[tt * 128:(tt + 1) * 128, col0:col0 + 512], vs[:], src=vs)

    def ret_scan(b, last):
        maskT = fw.sb("rmask", [128, 16, 128], F32)
        xi = fw.sb("rxi", [128, 16], F32)
        zeta = fw.sb("rzeta", [128, 16], F32)
        fw.dma("sp", maskT[:], rmask_in.rearrange("(i m) c -> m i c", m=128), dst=maskT)
        fw.dma("sp", xi[:], rxi_in, dst=xi)
        fw.dma("sp", zeta[:], rzeta_in, dst=zeta)
        qh = [fw.sb("qh%d" % i, [128, 2, T], BF16) for i in range(2)]
        kh = [fw.sb("kh%d" % i, [128, 2, T], BF16) for i in range(2)]
        vh = [fw.sb("vh%d" % i, [128, NTILE, 512], BF16) for i in range(2)]
        gh = [fw.sb("gh%d" % i, [128, NTILE, 512], BF16) for i in range(2)]
        oacc = fw.sb("oacc", [128, NTILE, 512], F32)
        yst = fw.sb("yst", [128, 4, T], BF16)
        R = fw.sb("R", [128, 2, 512], F32)
        Rb = fw.sb("Rb", [128, 2, 512], BF16)
        AT = [fw.sb("AT%d" % i, [128, 128], BF16) for i in range(2)]
        kz = [fw.sb("kz%d" % i, [128, 256], BF16) for i in range(2)]
        junk = [fw.sb("junk%d" % i, [128, 512], F32) for i in range(2)]
        ss = fw.sb("ss", [128, NTILE], F32)
        rstd = fw.sb("rstdh", [128, NTILE], F32)
        yn = [fw.sb("yn%d" % i, [128, 512], BF16) for i in range(2)]

        def load(i, h):
            fw.dma("sp", qh[i % 2][:], qT_s[h * 256:(h + 1) * 256, :].rearrange("(c p) t -> p c t", p=128), dst=qh[i % 2])
            fw.dma("sp", kh[i % 2][:], kT_s[h * 256:(h + 1) * 256, :].rearrange("(c p) t -> p c t", p=128), dst=kh[i % 2])
            fw.dma("sp", vh[i % 2][:], v_s[:, h * 512:(h + 1) * 512].rearrange("(j p) e -> p j e", p=128), dst=vh[i % 2])
            r0 = CTX if last else 0
            fw.dma("sp", gh[i % 2][:, r0 // 128:NTILE, :], g_s[r0:T, h * 512:(h + 1) * 512].rearrange("(j p) e -> p j e", p=128), dst=gh[i % 2])
            return (qh[i % 2], kh[i % 2], vh[i % 2], gh[i % 2])
        st = Stream(list(range(RET_HEADS)), load, 2)
        nstep = 0
        for h in range(RET_HEADS):
            q_, k_, v_, g_ = st.get(h)
            for d in range(2):
                hd = h * 2 + d
                order = list(range(NTILE)) if d == 0 else [1, 0] + list(range(NTILE - 1, 1, -1))
                for si, j in enumerate(order):
                    tsl = slice(j * 128, (j + 1) * 128)
                    need_out = not (last and j < 2)
                    first = si == 0
                    lastst = si == len(order) - 1
                    p_s = PS[0 + nstep % 2]
                    p_o = PS[2 + nstep % 2]
                    p_i = PS[4 + nstep % 2]
                    p_r = (PS[6], PS[7])
                    p_k = PS[0 + nstep % 2]
                    at = AT[nstep % 2]
                    kzz = kz[nstep % 2]
                    nstep += 1
                    if need_out:
                        for dc in range(2):
                            mm(p_s[:, 0:128], k_[:, dc, tsl], q_[:, dc, tsl], dc == 0, dc == 1, [k_, q_], p_s)
                        fw.op("dve", lambda e, at=at, p_s=p_s, hd=hd: e.tensor_tensor(out=at[:], in0=p_s[:, 0:128], in1=maskT[:, hd, :], op=ALU.mult),
                              reads=[p_s, maskT], writes=[at])
                        mm(p_o[:], at[:], v_[:, j, :], True, True, [at, v_], p_o)
                        if not first:
                            for dc in range(2):
                                mm(p_i[:], q_[:, dc, tsl], Rb[:, dc, :], dc == 0, dc == 1, [q_, Rb], p_i)
                        if d == 0:
                            fw.op("act", lambda e, j=j, p_o=p_o: e.copy(oacc[:, j, :], p_o[:]), reads=[p_o], writes=[oacc])
                        else:
                            fw.op("dve", lambda e, j=j, p_o=p_o: e.tensor_tensor(out=oacc[:, j, :], in0=p_o[:], in1=oacc[:, j, :], op=ALU.add),
                                  reads=[p_o, oacc], writes=[oacc])
                        if not first:
                            fw.op("dve", lambda e, j=j, p_i=p_i, hd=hd: e.scalar_tensor_tensor(
                                out=oacc[:, j, :], in0=p_i[:], scalar=xi[:, hd:hd + 1], in1=oacc[:, j, :], op0=ALU.mult, op1=ALU.add),
                                reads=[p_i, xi, oacc], writes=[oacc])
                    if not lastst:
                        pkv = p_k[:].bitcast(BF16)
                        for dc in range(2):
                            tr(pkv[:, 512 + dc * 128:512 + (dc + 1) * 128], k_[:, dc, tsl], identb[:], [k_, identb], p_k)
                        fw.op("act", lambda e, kzz=kzz, pkv=pkv, hd=hd: e.activation(out=kzz[:], in_=pkv[:, 512:768], func=AF.Identity,
                                                                                     scale=zeta[:, hd:hd + 1]),
                              reads=[p_k, zeta], writes=[kzz])
                        for dc in range(2):
                            mm(p_r[dc][:], kzz[:, dc * 128:(dc + 1) * 128], v_[:, j, :], True, True, [kzz, v_], p_r[dc])
                        for dc in range(2):
                            if first:
                                fw.op("dve", lambda e, dc=dc: e.tensor_copy(R[:, dc, :], p_r[dc][:]), reads=[p_r[dc]], writes=[R])
                            else:
                                fw.op("dve", lambda e, dc=dc, hd=hd: e.scalar_tensor_tensor(
                                    out=R[:, dc, :], in0=R[:, dc, :], scalar=float(ret_cdec[hd]), in1=p_r[dc][:], op0=ALU.mult, op1=ALU.add),
                                    reads=[R, p_r[dc]], writes=[R])
                        fw.op("pool", lambda e: e.tensor_copy(Rb[:], R[:]), reads=[R], writes=[Rb])
            j0 = 2 if last else 0
            for j in range(j0, NTILE):
                jk = junk[j % 2]
                fw.op("act", lambda e, j=j, jk=jk: e.activation(out=jk[:], in_=oacc[:, j, :], func=AF.Square),
                      reads=[oacc], writes=[jk])
                fw.op("dve", lambda e, j=j, jk=jk: e.tensor_reduce(out=ss[:, j:j + 1], in_=jk[:], axis=AX.X, op=ALU.add),
                      reads=[jk], writes=[ss])
            rsqrt(rstd[:, j0:NTILE], rstd, ss[:, j0:NTILE], ss, 1.0 / RET_DV)
            for j in range(j0, NTILE):
                y = yn[j % 2]
                fw.op("dve", lambda e, j=j, y=y, g_=g_: e.scalar_tensor_tensor(out=y[:], in0=oacc[:, j, :], scalar=rstd[:, j:j + 1],
                                                                        in1=g_[:, j, :], op0=ALU.mult, op1=ALU.mult),
                      reads=[oacc, rstd, g_], writes=[y])
                pt = PS[4 + j % 2]
                ptv = pt[:].bitcast(BF16)
                for ec in range(4):
                    tr(ptv[:, ec * 128:(ec + 1) * 128], y[:, ec * 128:(ec + 1) * 128], identb[:], [y, identb], pt)
                fw.op("act", lambda e, j=j, ptv=ptv: e.copy(yst[:, :, j * 128:(j + 1) * 128],
                                                           ptv[:, 0:512].rearrange("p (c t) -> p c t", c=4)),
                      reads=[pt], writes=[yst])
            t0 = j0 * 128
            fw.dma("sp", ynT[b, h * 512:(h + 1) * 512, t0:T].rearrange("(c p) t -> p c t", p=128), yst[:, :, t0:T], src=yst)


    ZL = 2314
    ZN = 2310
    ab_s = dscr("ab_s", [T, 128], F32)

    def gdn_inproj(b, hT):
        W = gdn_w_in
        wb = [fw.sb("wi%d" % i, [128, KC, 512], BF16) for i in range(2)]
        wab = fw.sb("wab", [128, KC, 128], BF16)
        for d in range(2):
            fw.dma("pool", wab[:, :, d * 64:(d + 1) * 64], gdn_w_ab[d * D:(d + 1) * D, :].rearrange("(k p) n -> p k n", p=128), dst=wab)
        cw = fw.sb("cw", [128, 320], F32)
        tmpc = fw.sb("tmpc", [128, 3, 128], F32)
        for i in range(3):
            n = 128 if i < 2 else 64
            fw.dma("sp", tmpc[0:n, i, :], gdn_conv_w[i * 128:i * 128 + n, :], dst=tmpc)
        for i in range(3):
            n = 128 if i < 2 else 64
            tr(PS[0][:, i * 128:i * 128 + n], tmpc[0:n, i, :], ident[0:n, 0:n], [tmpc, ident], PS[0])
        fw.op("dve", lambda e: e.tensor_copy(cw[:], PS[0][:, 0:320]), reads=[PS[0]], writes=[cw])
        zc = [fw.sb("zc%d" % i, [128, ZL], F32) for i in range(2)]
        for z in zc:
            fw.op("pool", lambda e, z=z: e.memset(z[:], 0.0), writes=[z])
        acc = fw.sb("acc", [128, ZN], F32)
        sl = fw.sb("sl", [128, ZN], F32)
        sqq = fw.sb("sqq", [128, ZN], BF16)
        rs = fw.sb("rs", [128, ZN], F32)
        ob = [fw.sb("ob%d" % i, [128, ZN], BF16) for i in range(2)]
        vst = [fw.sb("vst%d" % i, [128, 512], BF16) for i in range(3)]
        abst = [fw.sb("abst%d" % i, [128, 128], F32) for i in range(2)]

        def load(i, cb):
            buf = wb[i % 2]
            fw.dma("pool", buf[:], W[:, cb * 512:(cb + 1) * 512].rearrange("(k p) n -> p k n", p=128), dst=buf)
            return buf
        st = Stream(list(range(24)), load, 2)
        tblocks = [(0, CTX, 2)] + [(CTX + 512 * j, 512, 262 + 512 * j) for j in range(4)]
        npb = 0
        nv = 0
        ncc = 0
        for tt in range(NTILE):
            pb = PS[npb % 8]
            npb += 1
            for kc in range(KC):
                mm(pb[:, 0:128], hT[:, kc, tt * 128:(tt + 1) * 128], wab[:, kc, :], kc == 0, kc == KC - 1, [wab, hT], pb)
            a_ = abst[tt % 2]
            fw.op("act", lambda e, a_=a_, pb=pb: e.copy(a_[:], pb[:, 0:128]), reads=[pb], writes=[a_])
            fw.dma("sp", ab_s[tt * 128:(tt + 1) * 128, :], a_[:], src=a_)
        for cb in range(24):
            buf = st.get(cb)
            if cb < 16:
                for jj in range(4):
                    cc = cb * 4 + jj
                    z = zc[ncc % 2]
                    o_ = ob[ncc % 2]
                    ncc += 1
                    for (t0, nt, zo) in tblocks:
                        pb = PS[npb % 8]
                        npb += 1
                        for kc in range(KC):
                            mm(pb[:, 0:nt], buf[:, kc, jj * 128:(jj + 1) * 128], hT[:, kc, t0:t0 + nt], kc == 0, kc == KC - 1,
                               [buf, hT], pb)
                        fw.op("act", lambda e, z=z, pb=pb, zo=zo, nt=nt: e.copy(z[:, zo:zo + nt], pb[:, 0:nt]), reads=[pb], writes=[z])
                    fw.op("dve", lambda e, z=z, cc=cc: e.tensor_scalar(out=acc[:], in0=z[:, 0:ZN], scalar1=cw[:, cc:cc + 1], scalar2=None,
                                                                       op0=ALU.mult), reads=[z, cw], writes=[acc])
                    for j in range(1, 5):
                        fw.op("dve", lambda e, z=z, cc=cc, j=j: e.scalar_tensor_tensor(
                            out=acc[:], in0=z[:, j:j + ZN], scalar=cw[:, j * 64 + cc:j * 64 + cc + 1], in1=acc[:], op0=ALU.mult, op1=ALU.add),
                            reads=[z, cw, acc], writes=[acc])
                    if cc >= 32:
                        fw.op("act", lambda e, o_=o_: e.activation(out=o_[:], in_=acc[:], func=AF.Silu), reads=[acc], writes=[o_])
                        dst_s, r0 = vT_s, (cc - 32) * 128
                    else:
                        fw.op("act", lambda e: e.activation(out=sl[:], in_=acc[:], func=AF.Silu), reads=[acc], writes=[sl])
                        fw.op("pool", lambda e: e.tensor_tensor(out=sqq[:], in0=sl[:], in1=sl[:], op=ALU.mult), reads=[sl], writes=[sqq])
                        for c0 in range(0, ZN, 512):
                            n = min(512, ZN - c0)
                            pb = PS[npb % 8]
                            npb += 1
                            mm(pb[:, 0:n], ones1b[:], sqq[:, c0:c0 + n], True, True, [ones1b, sqq], pb)
                            rsqrt(rs[:, c0:c0 + n], rs, pb[:, 0:n], pb, 1.0)
                        qs = 128 ** -0.5 if cc < 16 else 1.0
                        fw.op("dve", lambda e, o_=o_, qs=qs: e.scalar_tensor_tensor(out=o_[:], in0=sl[:], scalar=qs, in1=rs[:],
                                                                                   op0=ALU.mult, op1=ALU.mult), reads=[sl, rs], writes=[o_])
                        dst_s, r0 = (qT_s, cc * 128) if cc < 16 else (kT_s, (cc - 16) * 128)
                    fw.dma("sp", dst_s[r0:r0 + 128, 0:CTX], o_[:, 0:CTX], src=o_)
                    fw.dma("sp", dst_s[r0:r0 + 128, CTX:T], o_[:, 260:260 + SEQ], src=o_)
            else:
                col0 = (cb - 16) * 512
                for tt in range(NTILE):
                    pb = PS[npb % 8]
                    npb += 1
                    for kc in range(KC):
                        mm(pb[:], hT[:, kc, tt * 128:(tt + 1) * 128], buf[:, kc, :], kc == 0, kc == KC - 1, [buf, hT], pb)
                    vs = vst[nv % 3]
                    nv += 1
                    fw.op("act", lambda e, vs=vs, pb=pb: e.activation(out=vs[:], in_=pb[:], func=AF.Silu), reads=[pb], writes=[vs])
                    fw.dma("sp", g_s[tt * 128:(tt + 1) * 128, col0:col0 + 512], vs[:], src=vs)

    def gdn_scan(b):
        BETA = fw.sb("BETA", [128, NTILE, 64], F32)
        GB = fw.sb("GB", [128, NTILE, 64], F32)
        EGB = fw.sb("EGB", [128, NTILE, 64], F32)
        KDEC = fw.sb("KDEC", [128, NTILE, 64], F32)
        BG = fw.sb("BG", [128, NTILE, 64], F32)
        EGL = fw.sb("EGL", [128, NTILE, 64], F32)
        NGB = fw.sb("NGB", [128, NTILE, 64], F32)
        nwb = fw.sb("nwb", [128, 128], F32)
        gm = fw.sb("gm", [128, 4, 128], F32)
        rowt = fw.sb("rowt", [1, 256], F32)
        fw.dma("sp", rowt[0:1, 0:128], gdn_norm_w, dst=rowt)
        fw.dma("sp", rowt[0:1, 128:192], gdn_dt_bias, dst=rowt)
        fw.dma("sp", rowt[0:1, 192:256], gdn_a_log, dst=rowt)
        mm(PS[4][:, 0:256], ones1f[0:1, 0:128], rowt[0:1, :], True, True, [ones1f, rowt], PS[4])
        fw.op("dve", lambda e: e.tensor_copy(nwb[:], PS[4][:, 0:128]), reads=[PS[4]], writes=[nwb])
        fw.dma("sp", gm[:], gmask_in.rearrange("p (i m) -> p i m", i=4), dst=gm)
        m0 = fw.mark()
        abt = fw.sb("abt", [128, NTILE, 128], F32)
        fw.dma("sp", abt[:], ab_s.rearrange("(j p) n -> p j n", p=128), dst=abt)
        dtb = fw.sb("dtb", [128, 64], F32)
        negA = fw.sb("negA", [128, 64], F32)
        fw.op("dve", lambda e: e.tensor_copy(dtb[:], PS[4][:, 128:192]), reads=[PS[4]], writes=[dtb])
        fw.op("dve", lambda e: e.tensor_copy(negA[:], PS[4][:, 192:256]), reads=[PS[4]], writes=[negA])
        fw.op("act", lambda e: e.activation(out=negA[:], in_=negA[:], func=AF.Exp), reads=[negA], writes=[negA])
        fw.op("dve", lambda e: e.tensor_scalar(out=negA[:], in0=negA[:], scalar1=-1.0, scalar2=None, op0=ALU.mult), reads=[negA], writes=[negA])
        G = fw.sb("G", [128, NTILE, 64], F32)
        GS = fw.sb("GS", [128, NTILE, 64], F32)
        if stop == "g1":
            dd = nc.dram_tensor("dbg_abt", list(abt.t.shape[:1]) + [int(np.prod(abt.t.shape[1:]))], F32, kind="ExternalOutput").ap()
            fw.dma("sp", dd, abt[:] if len(abt.t.shape) == 2 else abt[:].rearrange("p j n -> p (j n)"), src=abt)
            dd = nc.dram_tensor("dbg_dtb", list(dtb.t.shape[:1]) + [int(np.prod(dtb.t.shape[1:]))], F32, kind="ExternalOutput").ap()
            fw.dma("sp", dd, dtb[:] if len(dtb.t.shape) == 2 else dtb[:].rearrange("p j n -> p (j n)"), src=dtb)
            dd = nc.dram_tensor("dbg_negA", list(negA.t.shape[:1]) + [int(np.prod(negA.t.shape[1:]))], F32, kind="ExternalOutput").ap()
            fw.dma("sp", dd, negA[:] if len(negA.t.shape) == 2 else negA[:].rearrange("p j n -> p (j n)"), src=negA)
            dd = nc.dram_tensor("dbg_nwb", list(nwb.t.shape[:1]) + [int(np.prod(nwb.t.shape[1:]))], F32, kind="ExternalOutput").ap()
            fw.dma("sp", dd, nwb[:] if len(nwb.t.shape) == 2 else nwb[:].rearrange("p j n -> p (j n)"), src=nwb)
            fw.barrier()
            return
        for tt in range(NTILE):
            for d in range(2):
                fw.op("dve", lambda e, tt=tt, d=d: e.tensor_tensor(out=G[:, tt, d * 32:(d + 1) * 32], in0=abt[:, tt, d * 64:d * 64 + 32],
                                                                   in1=dtb[:, d * 32:(d + 1) * 32], op=ALU.add), reads=[abt, dtb], writes=[G])
                fw.op("dve", lambda e, tt=tt, d=d: e.tensor_copy(BETA[:, tt, d * 32:(d + 1) * 32], abt[:, tt, d * 64 + 32:d * 64 + 64]),
                      reads=[abt], writes=[BETA])
        fw.op("act", lambda e: e.activation(out=G[:], in_=G[:], func=AF.Exp), reads=[G], writes=[G])
        fw.op("act", lambda e: e.activation(out=G[:], in_=G[:], func=AF.Ln, bias=ones1f[:, 0:1]), reads=[G, ones1f], writes=[G])
        fw.op("act", lambda e: e.activation(out=BETA[:], in_=BETA[:], func=AF.Sigmoid), reads=[BETA], writes=[BETA])
        for tt in range(NTILE):
            fw.op("dve", lambda e, tt=tt: e.tensor_tensor(out=G[:, tt, :], in0=G[:, tt, :], in1=negA[:], op=ALU.mult), reads=[G, negA], writes=[G])
        if stop == "g2":
            dd = nc.dram_tensor("dbg_G", list(G.t.shape[:1]) + [int(np.prod(G.t.shape[1:]))], F32, kind="ExternalOutput").ap()
            fw.dma("sp", dd, G[:] if len(G.t.shape) == 2 else G[:].rearrange("p j n -> p (j n)"), src=G)
            dd = nc.dram_tensor("dbg_BETA", list(BETA.t.shape[:1]) + [int(np.prod(BETA.t.shape[1:]))], F32, kind="ExternalOutput").ap()
            fw.dma("sp", dd, BETA[:] if len(BETA.t.shape) == 2 else BETA[:].rearrange("p j n -> p (j n)"), src=BETA)
            fw.barrier()
            return
        for tt in range(NTILE):
            pb = PS[tt % 4]
            for d in range(2):
                tri = gm[:, 3, :] if d == 0 else gm[:, 1, :]
                mm(pb[:, d * 32:(d + 1) * 32], tri, G[:, tt, d * 32:(d + 1) * 32], True, True, [gm, G], pb)
            mm(pb[:, 64:128], ones1f[:], G[:, tt, :], True, True, [ones1f, G], pb)
            fw.op("act", lambda e, tt=tt, pb=pb: e.copy(GB[:, tt, :], pb[:, 0:64]), reads=[pb], writes=[GB])
            fw.op("dve", lambda e, tt=tt, pb=pb: e.tensor_copy(GS[:, tt, :], pb[:, 64:128]), reads=[pb], writes=[GS])
        if stop == "g3":
            dd = nc.dram_tensor("dbg_GB", list(GB.t.shape[:1]) + [int(np.prod(GB.t.shape[1:]))], F32, kind="ExternalOutput").ap()
            fw.dma("sp", dd, GB[:] if len(GB.t.shape) == 2 else GB[:].rearrange("p j n -> p (j n)"), src=GB)
            dd = nc.dram_tensor("dbg_GS", list(GS.t.shape[:1]) + [int(np.prod(GS.t.shape[1:]))], F32, kind="ExternalOutput").ap()
            fw.dma("sp", dd, GS[:] if len(GS.t.shape) == 2 else GS[:].rearrange("p j n -> p (j n)"), src=GS)
            fw.barrier()
            return
        fw.op("dve", lambda e: e.tensor_scalar(out=NGB[:], in0=GB[:], scalar1=-1.0, scalar2=None, op0=ALU.mult), reads=[GB], writes=[NGB])
        fw.op("act", lambda e: e.activation(out=EGB[:], in_=GB[:], func=AF.Exp), reads=[GB], writes=[EGB])
        fw.op("act", lambda e: e.activation(out=EGL[:], in_=GS[:], func=AF.Exp), reads=[GS], writes=[EGL])
        fw.op("dve", lambda e: e.tensor_tensor(out=KDEC[:], in0=GS[:], in1=GB[:], op=ALU.subtract), reads=[GS, GB], writes=[KDEC])
        fw.op("act", lambda e: e.activation(out=KDEC[:], in_=KDEC[:], func=AF.Exp), reads=[KDEC], writes=[KDEC])
        fw.op("dve", lambda e: e.tensor_tensor(out=BG[:], in0=BETA[:], in1=EGB[:], op=ALU.mult), reads=[BETA, EGB], writes=[BG])
        fw.release_to(m0)
        if stop == "gates":
            for nm, t_ in (("GB", GB), ("BETA", BETA), ("EGL", EGL), ("KDEC", KDEC), ("BG", BG)):
                dd = nc.dram_tensor("dbg_" + nm, [128, NTILE * 64], F32, kind="ExternalOutput").ap()
                fw.dma("sp", dd, t_[:].rearrange("p j n -> p (j n)"), src=t_)
            fw.barrier()
            return

        qh = [fw.sb("gq%d" % i, [128, T], BF16) for i in range(2)]
        kh = [fw.sb("gk%d" % i, [128, T], BF16) for i in range(2)]
        vh = [fw.sb("gv%d" % i, [128, 2, T], BF16) for i in range(2)]
        zh = [fw.sb("gz%d" % i, [128, NTILE, 256], BF16) for i in range(2)]
        oacc = [fw.sb("goacc%d" % j, [128, 256], F32) for j in range(NTILE)]
        yst = fw.sb("gyst", [128, 2, T], BF16)
        NCH = 4
        NW = 4
        NCS = 8
        U = [[fw.sb("U%d_%d" % (c, w), [128, 128], BF16) for w in range(NW)] for c in range(NCH)]
        WT = [[fw.sb("WT%d_%d" % (c, w), [128, 128], BF16) for w in range(NW)] for c in range(NCH)]
        KP = [[fw.sb("KP%d_%d" % (c, w), [128, 128], BF16) for w in range(NW)] for c in range(NCH)]
        ATT = [[fw.sb("ATT%d_%d" % (c, w), [128, 128], BF16) for w in range(NW)] for c in range(NCH)]
        S = [fw.sb("S%d" % c, [128, 128], F32) for c in range(NCH)]
        Sb = [fw.sb("Sb%d" % c, [128, 128], BF16) for c in range(NCH)]
        kkS = [fw.sb("kkS%d" % i, [128, 128], F32) for i in range(4)]
        qkI = [fw.sb("qkI%d" % i, [128, 128], F32) for i in range(4)]
        dg = [fw.sb("dg%d" % i, [128, 128], F32) for i in range(NCS)]
        tq = [fw.sb("tq%d" % i, [128, 128], F32) for i in range(NCS)]
        Dm = [fw.sb("Dm%d" % i, [128, 128], F32) for i in range(NCS)]
        MM = [fw.sb("MM%d" % i, [128, 256], BF16) for i in range(NCS)]
        Am = [fw.sb("Am%d" % i, [128, 128], BF16) for i in range(NCS)]
        TT = [[fw.sb("TT%d_%d" % (i, k), [128, 256], BF16) for k in range(2)] for i in range(NCS)]
        YY = [fw.sb("YY%d" % i, [128, 256], BF16) for i in range(NCS)]
        rb = [fw.sb("rb%d" % i, [128, 256], BF16) for i in range(NCS)]
        lvm = fw.sb("lvm", [128, 14, 256], BF16)
        fw.dma("pool", lvm[:], glv_in.rearrange("(i p) n -> p i n", p=128), dst=lvm)
        ident2b = fw.sb("ident2b", [128, 256], BF16)
        fw.op("dve", lambda e: e.tensor_copy(ident2b[:, 0:128], identb[:]), reads=[identb], writes=[ident2b])
        fw.op("dve", lambda e: e.tensor_copy(ident2b[:, 128:256], identb[:]), reads=[identb], writes=[ident2b])
        vn = [fw.sb("vn%d" % i, [128, 128], BF16) for i in range(NCH)]
        jk = [fw.sb("gjk%d" % i, [128, 128], F32) for i in range(2)]
        ss = fw.sb("gss", [128, NTILE * 2], F32)
        rstd = fw.sb("grstd", [128, NTILE * 2], F32)
        ynt = [fw.sb("gyn%d" % i, [128, 256], BF16) for i in range(2)]
        tmpn = [fw.sb("gtn%d" % i, [128, 256], F32) for i in range(2)]

        def load(i, hq):
            fw.dma("sp", qh[i % 2][:], qT_s[hq * 128:(hq + 1) * 128, :], dst=qh[i % 2])
            fw.dma("sp", kh[i % 2][:], kT_s[hq * 128:(hq + 1) * 128, :], dst=kh[i % 2])
            fw.dma("sp", vh[i % 2][:], vT_s[hq * 256:(hq + 1) * 256, :].rearrange("(c p) t -> p c t", p=128), dst=vh[i % 2])
            fw.dma("sp", zh[i % 2][:], g_s[:, hq * 256:(hq + 1) * 256].rearrange("(j p) e -> p j e", p=128), dst=zh[i % 2])
            return (qh[i % 2], kh[i % 2], vh[i % 2], zh[i % 2])
        st = Stream(list(range(16)), load, 2)
        rr = [0]
        pbn = [0]
        nsh = [0]

        def nbank():
            pbn[0] += 1
            return PS[2 + pbn[0] % 4]
        pan = [0]

        orders = [list(range(NTILE)), [1, 0] + list(range(NTILE - 1, 1, -1))]

        def prep(hq, q_, k_, v_, sis):
            cx = []
            for sx, si in enumerate(sis):
                w = si % NW
                for d in range(2):
                    j = orders[d][si]
                    tsl = slice(j * 128, (j + 1) * 128)
                    pa = PS[pan[0] % 2]
                    pan[0] += 1
                    mm(pa[:, 0:128], k_[:, tsl], k_[:, tsl], True, True, [k_], pa)
                    mm(pa[:, 128:256], q_[:, tsl], k_[:, tsl], True, True, [q_, k_], pa)
                    pav = pa[:].bitcast(BF16)
                    tr(pav[:, 512:640], k_[:, tsl], identb[:], [k_, identb], pa)
                    for hv2 in range(2):
                        tr(pav[:, 640 + hv2 * 128:768 + hv2 * 128], v_[:, hv2, tsl], identb[:], [v_, identb], pa)
                    ks = kkS[sx * 2 + d]
                    qi = qkI[sx * 2 + d]
                    fw.op("dve", lambda e, ks=ks, pa=pa, d=d: e.tensor_tensor(out=ks[:], in0=pa[:, 0:128], in1=gm[:, 2 * d, :], op=ALU.mult),
                          reads=[pa, gm], writes=[ks])
                    fw.op("dve", lambda e, qi=qi, pa=pa, d=d: e.tensor_tensor(out=qi[:], in0=pa[:, 128:256], in1=gm[:, 2 * d + 1, :], op=ALU.mult),
                          reads=[pa, gm], writes=[qi])
                    for hv2 in range(2):
                        ch = hv2 * 2 + d
                        cs = sx * 4 + ch
                        col = d * 32 + hq * 2 + hv2
                        fw.op("act", lambda e, cs=cs, pav=pav, hv2=hv2, j=j, col=col: e.activation(
                            out=rb[cs][:, 0:128], in_=pav[:, 640 + hv2 * 128:768 + hv2 * 128], func=AF.Identity, scale=BETA[:, j, col:col + 1]),
                            reads=[pa, BETA], writes=[rb[cs]])
                        fw.op("act", lambda e, cs=cs, pav=pav, j=j, col=col: e.activation(
                            out=rb[cs][:, 128:256], in_=pav[:, 512:640], func=AF.Identity, scale=BG[:, j, col:col + 1]),
                            reads=[pa, BG], writes=[rb[cs]])
                        kp = KP[ch][w]
                        fw.op("act", lambda e, kp=kp, pav=pav, j=j, col=col: e.activation(out=kp[:], in_=pav[:, 512:640], func=AF.Identity,
                                                                                          scale=KDEC[:, j, col:col + 1]), reads=[pa, KDEC], writes=[kp])
                        cx.append((cs, ch, w, d, j, col, ks, qi, PS[2 + cs // 2], (cs % 2) * 256))
            for (cs, ch, w, d, j, col, ks, qi, pp, c0) in cx:
                gbc = GB[:, j, col:col + 1]
                fw.op("pool", lambda e, cs=cs, gbc=gbc: e.tensor_scalar(out=dg[cs][:], in0=ident[:], scalar1=gbc, scalar2=None, op0=ALU.mult),
                      reads=[ident, GB], writes=[dg[cs]])
            for (cs, ch, w, d, j, col, ks, qi, pp, c0) in cx:
                mm(pp[:, c0:c0 + 128], ones1f[:], dg[cs][:], True, True, [ones1f, dg[cs]], pp)
            for (cs, ch, w, d, j, col, ks, qi, pp, c0) in cx:
                fw.op("act", lambda e, cs=cs, pp=pp, c0=c0, j=j, col=col: e.activation(out=tq[cs][:], in_=pp[:, c0:c0 + 128], func=AF.Relu,
                                                                                      bias=NGB[:, j, col:col + 1]), reads=[pp, NGB], writes=[tq[cs]])
            for (cs, ch, w, d, j, col, ks, qi, pp, c0) in cx:
                fw.op("act", lambda e, cs=cs: e.activation(out=Dm[cs][:], in_=tq[cs][:], func=AF.Exp, scale=-1.0), reads=[tq[cs]], writes=[Dm[cs]])
            for (cs, ch, w, d, j, col, ks, qi, pp, c0) in cx:
                fw.op("dve", lambda e, cs=cs, ks=ks, j=j, col=col: e.scalar_tensor_tensor(
                    out=MM[cs][:, 0:128], in0=Dm[cs][:], scalar=BETA[:, j, col:col + 1], in1=ks[:], op0=ALU.mult, op1=ALU.mult),
                    reads=[Dm[cs], BETA, ks], writes=[MM[cs]])
                fw.op("pool", lambda e, cs=cs, qi=qi: e.tensor_tensor(out=Am[cs][:], in0=Dm[cs][:], in1=qi[:], op=ALU.mult),
                      reads=[Dm[cs], qi], writes=[Am[cs]])
            for (cs, ch, w, d, j, col, ks, qi, pp, c0) in cx:
                ppv = pp[:].bitcast(BF16)
                b0 = 2 * (c0 + 128)
                tr(ppv[:, b0:b0 + 128], MM[cs][:, 0:128], identb[:], [MM[cs], identb], pp)
                tr(ppv[:, b0 + 128:b0 + 256], Am[cs][:], identb[:], [Am[cs], identb], pp)
            for (cs, ch, w, d, j, col, ks, qi, pp, c0) in cx:
                ppv = pp[:].bitcast(BF16)
                b0 = 2 * (c0 + 128)
                fw.op("act", lambda e, cs=cs, ppv=ppv, b0=b0: e.copy(MM[cs][:, 128:256], ppv[:, b0:b0 + 128]), reads=[pp], writes=[MM[cs]])
                att = ATT[ch][w]
                fw.op("dve", lambda e, att=att, ppv=ppv, b0=b0: e.tensor_copy(att[:], ppv[:, b0 + 128:b0 + 256]), reads=[pp], writes=[att])
            for (cs, ch, w, d, j, col, ks, qi, pp, c0) in cx:
                fw.op("pool", lambda e, cs=cs, d=d: e.tensor_tensor(out=YY[cs][:], in0=MM[cs][:], in1=lvm[:, d * 7, :], op=ALU.mult),
                      reads=[MM[cs], lvm], writes=[YY[cs]])
                fw.op("pool", lambda e, cs=cs: e.tensor_tensor(out=TT[cs][1][:], in0=ident2b[:], in1=YY[cs][:], op=ALU.add),
                      reads=[ident2b, YY[cs]], writes=[TT[cs][1]])
            for lv in range(1, 7):
                for (cs, ch, w, d, j, col, ks, qi, pp, c0) in cx:
                    Tc = TT[cs][lv % 2]
                    mm(pp[:, c0:c0 + 128], MM[cs][:, 128:256], Tc[:, 0:128], True, True, [MM[cs], Tc], pp)
                    mm(pp[:, c0 + 128:c0 + 256], MM[cs][:, 0:128], Tc[:, 128:256], True, True, [MM[cs], Tc], pp)
                for (cs, ch, w, d, j, col, ks, qi, pp, c0) in cx:
                    fw.op("dve", lambda e, cs=cs, pp=pp, c0=c0, d=d, lv=lv: e.tensor_tensor(out=YY[cs][:], in0=pp[:, c0:c0 + 256], in1=lvm[:, d * 7 + lv, :], op=ALU.mult),
                          reads=[pp, lvm], writes=[YY[cs]])
                for (cs, ch, w, d, j, col, ks, qi, pp, c0) in cx:
                    Tc = TT[cs][lv % 2]
                    mm(pp[:, c0:c0 + 128], identb[:], Tc[:, 0:128], True, False, [identb, Tc], pp)
                    mm(pp[:, c0:c0 + 128], Tc[:, 128:256], YY[cs][:, 0:128], False, True, [Tc, YY[cs]], pp)
                    mm(pp[:, c0 + 128:c0 + 256], identb[:], Tc[:, 128:256], True, False, [identb, Tc], pp)
                    mm(pp[:, c0 + 128:c0 + 256], Tc[:, 0:128], YY[cs][:, 128:256], False, True, [Tc, YY[cs]], pp)
                for (cs, ch, w, d, j, col, ks, qi, pp, c0) in cx:
                    Tn = TT[cs][(lv + 1) % 2]
                    fw.op("act", lambda e, Tn=Tn, pp=pp, c0=c0: e.copy(Tn[:], pp[:, c0:c0 + 256]), reads=[pp], writes=[Tn])
            for (cs, ch, w, d, j, col, ks, qi, pp, c0) in cx:
                Tf = TT[cs][1]
                mm(pp[:, c0:c0 + 128], Tf[:, 128:256], rb[cs][:, 0:128], True, True, [Tf, rb[cs]], pp)
                mm(pp[:, c0 + 128:c0 + 256], rb[cs][:, 128:256], Tf[:, 128:256], True, True, [Tf, rb[cs]], pp)
            for (cs, ch, w, d, j, col, ks, qi, pp, c0) in cx:
                u = U[ch][w]
                wt = WT[ch][w]
                fw.op("act", lambda e, u=u, pp=pp, c0=c0: e.copy(u[:], pp[:, c0:c0 + 128]), reads=[pp], writes=[u])
                fw.op("dve", lambda e, wt=wt, pp=pp, c0=c0: e.tensor_copy(wt[:], pp[:, c0 + 128:c0 + 256]), reads=[pp], writes=[wt])

        def step(hq, q_, si):
            w = si % NW
            first = si == 0
            lastst = si == NTILE - 1
            info = []
            for ch in range(NCH):
                hv2, d = ch // 2, ch % 2
                hv = hq * 2 + hv2
                j = orders[d][si]
                info.append((ch, hv2, d, d * 32 + hv, j, PS[6 + ch % 2]))
            for (ch, hv2, d, col, j, pz) in info:
                if first:
                    fw.op("dve", lambda e, ch=ch, w=w: e.tensor_copy(vn[ch][:], U[ch][w][:]), reads=[U[ch][w]], writes=[vn[ch]])
                else:
                    c0 = (ch // 2) * 256
                    mm(pz[:, c0:c0 + 128], WT[ch][w][:], Sb[ch][:], True, True, [WT[ch][w], Sb[ch]], pz)
                    mm(pz[:, c0 + 128:c0 + 256], q_[:, j * 128:(j + 1) * 128], Sb[ch][:], True, True, [q_, Sb[ch]], pz)
            for (ch, hv2, d, col, j, pz) in info:
                if not first:
                    c0 = (ch // 2) * 256
                    fw.op("dve", lambda e, ch=ch, w=w, pz=pz, c0=c0: e.tensor_tensor(out=vn[ch][:], in0=U[ch][w][:], in1=pz[:, c0:c0 + 128], op=ALU.subtract),
                          reads=[U[ch][w], pz], writes=[vn[ch]])
            for (ch, hv2, d, col, j, pz) in info:
                osl = oacc[j][:, hv2 * 128:(hv2 + 1) * 128]
                if not first:
                    c0 = (ch // 2) * 256
                    fw.op("dve", lambda e, osl=osl, pz=pz, c0=c0, j=j, col=col: e.scalar_tensor_tensor(
                        out=osl, in0=pz[:, c0 + 128:c0 + 256], scalar=EGB[:, j, col:col + 1], in1=osl, op0=ALU.mult, op1=ALU.add),
                        reads=[pz, EGB, oacc[j]], writes=[oacc[j]])
            for (ch, hv2, d, col, j, pz) in info:
                pq = PS[6 + ch % 2]
                c0 = (ch // 2) * 256
                mm(pq[:, c0:c0 + 128], ATT[ch][w][:], vn[ch][:], True, True, [ATT[ch][w], vn[ch]], pq)
                if not lastst:
                    mm(pq[:, c0 + 128:c0 + 256], KP[ch][w][:], vn[ch][:], True, True, [KP[ch][w], vn[ch]], pq)
            for (ch, hv2, d, col, j, pz) in info:
                pq = PS[6 + ch % 2]
                c0 = (ch // 2) * 256
                osl = oacc[j][:, hv2 * 128:(hv2 + 1) * 128]
                fw.op("dve", lambda e, osl=osl, pq=pq, c0=c0: e.tensor_tensor(out=osl, in0=pq[:, c0:c0 + 128], in1=osl, op=ALU.add),
                      reads=[pq, oacc[j]], writes=[oacc[j]])
                if not lastst:
                    if first:
                        fw.op("dve", lambda e, ch=ch, pq=pq, c0=c0: e.tensor_copy(S[ch][:], pq[:, c0 + 128:c0 + 256]), reads=[pq], writes=[S[ch]])
                    else:
                        fw.op("dve", lambda e, ch=ch, pq=pq, c0=c0, j=j, col=col: e.scalar_tensor_tensor(
                            out=S[ch][:], in0=S[ch][:], scalar=EGL[:, j, col:col + 1], in1=pq[:, c0 + 128:c0 + 256], op0=ALU.mult, op1=ALU.add),
                            reads=[S[ch], EGL, pq], writes=[S[ch]])
                    fw.op("pool", lambda e, ch=ch: e.tensor_copy(Sb[ch][:], S[ch][:]), reads=[S[ch]], writes=[Sb[ch]])

        for hq in range(16):
            q_, k_, v_, z_ = st.get(hq)
            for j in range(NTILE):
                fw.op("pool", lambda e, j=j: e.memset(oacc[j][:], 0.0), writes=[oacc[j]])
            for sp in range(0, NTILE + 2, 2):
                if sp < NTILE:
                    prep(hq, q_, k_, v_, [sp, sp + 1])
                if sp >= 2:
                    step(hq, q_, sp - 2)
                    step(hq, q_, sp - 1)
            for j in range(NTILE):
                for hv2 in range(2):
                    jj = jk[(j * 2 + hv2) % 2]
                    fw.op("act", lambda e, j=j, hv2=hv2, jj=jj: e.activation(out=jj[:], in_=oacc[j][:, hv2 * 128:(hv2 + 1) * 128], func=AF.Square),
                          reads=[oacc[j]], writes=[jj])
                    fw.op("dve", lambda e, j=j, hv2=hv2, jj=jj: e.tensor_reduce(out=ss[:, j * 2 + hv2:j * 2 + hv2 + 1], in_=jj[:], axis=AX.X, op=ALU.add),
                          reads=[jj], writes=[ss])
            rsqrt(rstd[:], rstd, ss[:], ss, 1.0 / 128)
            for j in range(NTILE):
                y = ynt[j % 2]
                tn = tmpn[j % 2]
                for hv2 in range(2):
                    fw.op("dve", lambda e, j=j, hv2=hv2, tn=tn: e.scalar_tensor_tensor(
                        out=tn[:, hv2 * 128:(hv2 + 1) * 128], in0=oacc[j][:, hv2 * 128:(hv2 + 1) * 128], scalar=rstd[:, j * 2 + hv2:j * 2 + hv2 + 1],
                        in1=nwb[:], op0=ALU.mult, op1=ALU.mult), reads=[oacc[j], rstd, nwb], writes=[tn])
                fw.op("pool", lambda e, j=j, y=y, tn=tn, z_=z_: e.tensor_tensor(out=y[:], in0=tn[:], in1=z_[:, j, :], op=ALU.mult),
                      reads=[tn, z_], writes=[y])
                pt = PS[6 + j % 2]
                ptv = pt[:].bitcast(BF16)
                for ec in range(2):
                    tr(ptv[:, ec * 128:(ec + 1) * 128], y[:, ec * 128:(ec + 1) * 128], identb[:], [y, identb], pt)
                fw.op("act", lambda e, j=j, ptv=ptv: e.copy(yst[:, :, j * 128:(j + 1) * 128],
                                                           ptv[:, 0:256].rearrange("p (c t) -> p c t", c=2)), reads=[pt], writes=[yst])
            fw.dma("sp", ynT[b, hq * 256:(hq + 1) * 256, :].rearrange("(c p) t -> p c t", p=128), yst[:], src=yst)


    def na_inproj(b, hT):
        wb = [fw.sb("wi%d" % i, [128, KC, 512], BF16) for i in range(2)]
        qst = [fw.sb("nqst%d" % i, [128, 512], BF16) for i in range(3)]

        def load(i, cb):
            buf = wb[i % 2]
            fw.dma("pool", buf[:], na_w_in[:, cb * 512:(cb + 1) * 512].rearrange("(k p) n -> p k n", p=128), dst=buf)
            return buf
        st = Stream(list(range(12)), load, 2)
        tblocks = [(0, CTX)] + [(CTX + 512 * j, 512) for j in range(4)]
        npb = 0
        nq = 0
        for cb in range(12):
            buf = st.get(cb)
            if cb < 8:
                dst_s = qT_s if cb < 4 else kT_s
                for jj in range(4):
                    r0 = (cb % 4) * 512 + jj * 128
                    for (t0, nt) in tblocks:
                        pb = PS[npb % 8]
                        npb += 1
                        for kc in range(KC):
                            mm(pb[:, 0:nt], buf[:, kc, jj * 128:(jj + 1) * 128], hT[:, kc, t0:t0 + nt], kc == 0, kc == KC - 1, [buf, hT], pb)
                        qs = qst[nq % 3]
                        nq += 1
                        if cb < 4:
                            fw.op("act", lambda e, qs=qs, pb=pb, nt=nt: e.activation(out=qs[:, 0:nt], in_=pb[:, 0:nt], func=AF.Identity, scale=128 ** -0.5),
                                  reads=[pb], writes=[qs])
                        else:
                            fw.op("dve", lambda e, qs=qs, pb=pb, nt=nt: e.tensor_copy(qs[:, 0:nt], pb[:, 0:nt]), reads=[pb], writes=[qs])
                        fw.dma("sp", dst_s[r0:r0 + 128, t0:t0 + nt], qs[:, 0:nt], src=qs)
            else:
                col0 = (cb - 8) * 512
                for tt in range(NTILE):
                    pb = PS[npb % 8]
                    npb += 1
                    for kc in range(KC):
                        mm(pb[:], hT[:, kc, tt * 128:(tt + 1) * 128], buf[:, kc, :], kc == 0, kc == KC - 1, [buf, hT], pb)
                    qs = qst[nq % 3]
                    nq += 1
                    if nq % 2 == 0:
                        fw.op("act", lambda e, qs=qs, pb=pb: e.copy(qs[:], pb[:]), reads=[pb], writes=[qs])
                    else:
                        fw.op("dve", lambda e, qs=qs, pb=pb: e.tensor_copy(qs[:], pb[:]), reads=[pb], writes=[qs])
                    fw.dma("sp", v_s[tt * 128:(tt + 1) * 128, col0:col0 + 512], qs[:], src=qs)

    def na_attn(b):
        qh = [fw.sb("nq%d" % i, [128, T], BF16) for i in range(2)]
        kh = [fw.sb("nk%d" % i, [128, T], BF16) for i in range(2)]
        vh = [fw.sb("nv%d" % i, [128, NTILE, 128], BF16) for i in range(2)]
        bs = [fw.sb("nb%d" % i, [128, 5, 640], F32) for i in range(2)]
        yst = [fw.sb("nyst%d" % i, [128, T], BF16) for i in range(2)]
        Ssb = [fw.sb("nS%d" % i, [128, 896], F32) for i in range(2)]
        Pb = [fw.sb("nP%d" % i, [128, 896], BF16) for i in range(2)]
        PT = [fw.sb("nPT%d" % i, [128, 896], BF16) for i in range(2)]
        mx = [fw.sb("nmx%d" % i, [128, 2], F32) for i in range(2)]
        sm = [fw.sb("nsm%d" % i, [128, 2], F32) for i in range(2)]
        on = [fw.sb("non%d" % i, [128, 128], BF16) for i in range(2)]

        def load(i, h):
            fw.dma("sp", qh[i % 2][:], qT_s[h * 128:(h + 1) * 128, :], dst=qh[i % 2])
            fw.dma("sp", kh[i % 2][:], kT_s[h * 128:(h + 1) * 128, :], dst=kh[i % 2])
            fw.dma("sp", vh[i % 2][:], v_s[:, h * 128:(h + 1) * 128].rearrange("(j p) e -> p j e", p=128), dst=vh[i % 2])
            fw.dma("sp", bs[i % 2][:], na_bias[h * 640:(h + 1) * 640, :].rearrange("(i p) n -> p i n", p=128), dst=bs[i % 2])
            return (qh[i % 2], kh[i % 2], vh[i % 2], bs[i % 2], yst[i % 2])
        st = Stream(list(range(16)), load, 2)
        nblk = 0
        for h in range(16):
            q_, k_, v_, b_, y_ = st.get(h)
            for qt in range(NTILE):
                kx = nblk % 2
                nblk += 1
                SA, SB, PTp, OP = PS[2 * kx], PS[2 * kx + 1], PS[4 + kx], PS[6 + kx]
                S_, P_, PT_, mx_, sm_, on_ = Ssb[kx], Pb[kx], PT[kx], mx[kx], sm[kx], on[kx]
                qsl = q_[:, qt * 128:(qt + 1) * 128]
                if qt < 2:
                    nk = 256
                    ktiles = [0, 1]
                    mm(SB[:, 128:384], qsl, k_[:, 0:256], True, True, [q_, k_], SB)
                    fw.op("act", lambda e, S_=S_, SB=SB: e.copy(S_[:, 0:256], SB[:, 128:384]), reads=[SB], writes=[S_])
                else:
                    i = qt - 2
                    lo = min(max(i - 2, 0), 11)
                    pat = 0 if i == 0 else 1 if i == 1 else 3 if i == 14 else 4 if i == 15 else 2
                    nk = 896
                    ktiles = [2 + lo + t for t in range(5)] + [0, 1]
                    kc0 = CTX + lo * 128
                    mm(SA[:, 0:512], qsl, k_[:, kc0:kc0 + 512], True, True, [q_, k_], SA)
                    mm(SB[:, 0:128], qsl, k_[:, kc0 + 512:kc0 + 640], True, True, [q_, k_], SB)
                    mm(SB[:, 128:384], qsl, k_[:, 0:256], True, True, [q_, k_], SB)
                    fw.op("dve", lambda e, S_=S_, SA=SA, b_=b_, pat=pat: e.tensor_tensor(out=S_[:, 0:512], in0=SA[:, 0:512], in1=b_[:, pat, 0:512], op=ALU.add),
                          reads=[SA, b_], writes=[S_])
                    fw.op("dve", lambda e, S_=S_, SB=SB, b_=b_, pat=pat: e.tensor_tensor(out=S_[:, 512:640], in0=SB[:, 0:128], in1=b_[:, pat, 512:640], op=ALU.add),
                          reads=[SB, b_], writes=[S_])
                    fw.op("act", lambda e, S_=S_, SB=SB: e.copy(S_[:, 640:896], SB[:, 128:384]), reads=[SB], writes=[S_])
                fw.op("dve", lambda e, S_=S_, mx_=mx_, nk=nk: e.tensor_reduce(out=mx_[:, 0:1], in_=S_[:, 0:nk], axis=AX.X, op=ALU.max),
                      reads=[S_], writes=[mx_])
                fw.op("dve", lambda e, mx_=mx_: e.tensor_scalar(out=mx_[:, 1:2], in0=mx_[:, 0:1], scalar1=-1.0, scalar2=None, op0=ALU.mult),
                      reads=[mx_], writes=[mx_])
                fw.op("act", lambda e, S_=S_, P_=P_, mx_=mx_, nk=nk: e.activation(out=P_[:, 0:nk], in_=S_[:, 0:nk], func=AF.Exp, bias=mx_[:, 1:2]),
                      reads=[S_, mx_], writes=[P_])
                fw.op("dve", lambda e, P_=P_, sm_=sm_, nk=nk: e.tensor_reduce(out=sm_[:, 0:1], in_=P_[:, 0:nk], axis=AX.X, op=ALU.add),
                      reads=[P_], writes=[sm_])
                fw.op("dve", lambda e, sm_=sm_: e.reciprocal(sm_[:, 1:2], sm_[:, 0:1]), reads=[sm_], writes=[sm_])
                ptv = PTp[:].bitcast(BF16)
                nkt = len(ktiles)
                for t in range(nkt):
                    tr(ptv[:, t * 128:(t + 1) * 128], P_[:, t * 128:(t + 1) * 128], identb[:], [P_, identb], PTp)
                h1 = (nkt + 1) // 2 * 128
                fw.op("act", lambda e, PT_=PT_, ptv=ptv, h1=h1: e.copy(PT_[:, 0:h1], ptv[:, 0:h1]), reads=[PTp], writes=[PT_])
                if nkt * 128 > h1:
                    fw.op("dve", lambda e, PT_=PT_, ptv=ptv, h1=h1, nk=nk: e.tensor_copy(PT_[:, h1:nk], ptv[:, h1:nk]), reads=[PTp], writes=[PT_])
                for t, kt in enumerate(ktiles):
                    mm(OP[:, 0:128], PT_[:, t * 128:(t + 1) * 128], v_[:, kt, :], t == 0, t == nkt - 1, [PT_, v_], OP)
                fw.op("act", lambda e, on_=on_, OP=OP, sm_=sm_: e.activation(out=on_[:], in_=OP[:, 0:128], func=AF.Identity, scale=sm_[:, 1:2]),
                      reads=[OP, sm_], writes=[on_])
                opv = OP[:].bitcast(BF16)
                tr(opv[:, 512:640], on_[:], identb[:], [on_, identb], OP)
                fw.op("act", lambda e, y_=y_, opv=opv, qt=qt: e.copy(y_[:, qt * 128:(qt + 1) * 128], opv[:, 512:640]), reads=[OP], writes=[y_])
            fw.dma("sp", ynT[b, h * 128:(h + 1) * 128, :], y_[:], src=y_)

    def token_blocks(last):
        blks = []
        if not last:
            blks.append(("ctx", None, 2))
        for b in range(BPC):
            for j in range(4):
                blks.append(("lat", b, b))
        out = []
        jj = {0: 0, 1: 0}
        for kind, b, v in blks:
            if kind == "ctx":
                out.append(([(0, 0, CTX), (1, 0, CTX)], 2))
            else:
                out.append(([(b, CTX + 512 * jj[b], 512)], v))
                jj[b] += 1
        return out

    def outproj(li, last, KO):
        ynb = [fw.sb("ynb%d" % i, [128, KO, 512], BF16) for i in range(2)]
        wob = [fw.sb("wob%d" % i, [128, KO, 512], BF16) for i in range(2)]
        ybuf = fw.sb("ybuf", [128, KC, 512], F32)
        sqc = [fw.sb("sqc%d" % i, [128, 512], BF16) for i in range(2)]
        rstd = fw.sb("rstdo", [128, 512], F32)
        xc = [fw.sb("xc%d" % i, [128, 512], F32) for i in range(4)]
        tm = [fw.sb("tm%d" % i, [128, 512], F32) for i in range(2)]
        blks = token_blocks(last)

        def load_y(i, blk):
            buf = ynb[i % 2]
            segs, v = blk
            o = 0
            for (b, t0, nt) in segs:
                fw.dma("sp", buf[:, :, o:o + nt], ynT[b, 0:KO * 128, t0:t0 + nt].rearrange("(c p) t -> p c t", p=128), dst=buf)
                o += nt
            return buf
        sty = Stream(blks, load_y, 2)
        witems = [(bi, g) for bi in range(len(blks)) for g in range(4)]

        def load_w(i, it):
            buf = wob[i % 2]
            g = it[1]
            fw.dma("sp", buf[:], wo_b[0:KO * 128, g * 512:(g + 1) * 512].rearrange("(c p) n -> p c n", p=128), dst=buf)
            return buf
        stw = Stream(witems, load_w, 2)
        npb = 0
        nx = 0
        for bi, (segs, v) in enumerate(blks):
            yb = sty.get(bi)
            pss = PS[7]
            for g in range(4):
                wbuf = stw.get(bi * 4 + g)
                for j in range(4):
                    fc = g * 4 + j
                    pb = PS[npb % 6]
                    npb += 1
                    for kc in range(KO):
                        mm(pb[:], wbuf[:, kc, j * 128:(j + 1) * 128], yb[:, kc, :], kc == 0, kc == KO - 1, [wbuf, yb], pb)
                    fw.op("act", lambda e, fc=fc, pb=pb: e.copy(ybuf[:, fc, :], pb[:]), reads=[pb], writes=[ybuf])
                    sq = sqc[fc % 2]
                    fw.op("pool", lambda e, fc=fc, sq=sq: e.tensor_tensor(out=sq[:], in0=ybuf[:, fc, :], in1=ybuf[:, fc, :], op=ALU.mult),
                          reads=[ybuf], writes=[sq])
                    mm(pss[:], onesb[:], sq[:], fc == 0, fc == KC - 1, [onesb, sq], pss)
            rsqrt(rstd[:], rstd, pss[:], pss, 1.0)
            for fc in range(KC):
                x_ = xc[nx % 4]
                t_ = tm[nx % 2]
                nx += 1
                o = 0
                for (b, t0, nt) in segs:
                    fw.dma("sp", x_[:, o:o + nt], xT[b, fc * 128:(fc + 1) * 128, t0:t0 + nt], dst=x_)
                    o += nt
                fw.op("dve", lambda e, fc=fc, t_=t_: e.tensor_tensor(out=t_[:], in0=ybuf[:, fc, :], in1=rstd[:], op=ALU.mult),
                      reads=[ybuf, rstd], writes=[t_])
                fw.op("dve", lambda e, fc=fc, t_=t_, x_=x_, v=v: e.scalar_tensor_tensor(
                    out=x_[:], in0=t_[:], scalar=G1[:, fc, v:v + 1], in1=x_[:], op0=ALU.mult, op1=ALU.add),
                    reads=[t_, G1, x_], writes=[x_])
                o = 0
                for (b, t0, nt) in segs:
                    fw.dma("sp", xT[b, fc * 128:(fc + 1) * 128, t0:t0 + nt], x_[:, o:o + nt], src=x_)
                    o += nt

    def mlp(li, last):
        xb = fw.sb("xb", [128, KC, 512], F32)
        h2 = fw.sb("h2", [128, KC, 512], BF16)
        hid = fw.sb("hid", [128, 64, 512], BF16)
        w1b = [fw.sb("w1b%d" % i, [128, KC, 512], BF16) for i in range(2)]
        w2b = [fw.sb("w2b%d" % i, [128, 64, 128], BF16) for i in range(2)]
        sqb = fw.sb("sqb", [128, 2, 512], BF16)
        tmpb = fw.sb("tmpb", [128, 2, 512], F32)
        rstd = fw.sb("rstdm", [128, 512], F32)
        xc = [fw.sb("xcm%d" % i, [128, 512], F32) for i in range(3)]
        blks = token_blocks(last)
        w1items = [(bi, hb) for bi in range(len(blks)) for hb in range(16)]
        w2items = [(bi, fc) for bi in range(len(blks)) for fc in range(16)]

        def load_w1(i, it):
            buf = w1b[i % 2]
            hb = it[1]
            fw.dma("sp", buf[:], w1_b[:, hb * 512:(hb + 1) * 512].rearrange("(c p) n -> p c n", p=128), dst=buf)
            return buf

        def load_w2(i, it):
            buf = w2b[i % 2]
            fw.dma("sp", buf[:], w2_b[it[1]], dst=buf)
            return buf
        st1 = Stream(w1items, load_w1, 2)
        st2 = Stream(w2items, load_w2, 2)
        npb = 0
        nx = 0
        for bi, (segs, v) in enumerate(blks):
            o = 0
            for (b, t0, nt) in segs:
                fw.dma("sp", xb[:, :, o:o + nt], xT[b, :, t0:t0 + nt].rearrange("(c p) t -> p c t", p=128), dst=xb)
                o += nt
            norm_block(xb, 512, A2, 48, v, lambda c: (h2[:, c, :], h2), sqb, tmpb, rstd, PS[7])
            for hb in range(16):
                wbuf = st1.get(bi * 16 + hb)
                for j in range(4):
                    pb = PS[npb % 6]
                    npb += 1
                    for kc in range(KC):
                        mm(pb[:], wbuf[:, kc, j * 128:(j + 1) * 128], h2[:, kc, :], kc == 0, kc == KC - 1, [wbuf, h2], pb)
                    hc = hb * 4 + j
                    rl = tmpb
                    fw.op("act", lambda e, hc=hc, pb=pb: e.activation(out=tmpb[:, hc % 2, :], in_=pb[:], func=AF.Relu), reads=[pb], writes=[tmpb])
                    fw.op("dve", lambda e, hc=hc, pb=pb: e.tensor_tensor(out=hid[:, hc, :], in0=pb[:], in1=tmpb[:, hc % 2, :], op=ALU.mult),
                          reads=[pb, tmpb], writes=[hid])
            pss = PS[7]
            for fc in range(KC):
                wbuf = st2.get(bi * 16 + fc)
                pb = PS[npb % 6]
                npb += 1
                for kc in range(64):
                    mm(pb[:], wbuf[:, kc, :], hid[:, kc, :], kc == 0, kc == 63, [wbuf, hid], pb)
                fw.op("act", lambda e, fc=fc, pb=pb: e.copy(xb[:, fc, :], pb[:]), reads=[pb], writes=[xb])
                fw.op("pool", lambda e, fc=fc: e.tensor_tensor(out=sqb[:, fc % 2, :], in0=xb[:, fc, :], in1=xb[:, fc, :], op=ALU.mult),
                      reads=[xb], writes=[sqb])
                mm(pss[:], onesb[:], sqb[:, fc % 2, :], fc == 0, fc == KC - 1, [onesb, sqb], pss)
            rsqrt(rstd[:], rstd, pss[:], pss, 1.0)
            for fc in range(KC):
                x_ = xc[nx % 3]
                nx += 1
                o = 0
                for (b, t0, nt) in segs:
                    fw.dma("sp", x_[:, o:o + nt], xT[b, fc * 128:(fc + 1) * 128, t0:t0 + nt], dst=x_)
                    o += nt
                fw.op("dve", lambda e, fc=fc: e.tensor_tensor(out=tmpb[:, fc % 2, :], in0=xb[:, fc, :], in1=rstd[:], op=ALU.mult),
                      reads=[xb, rstd], writes=[tmpb])
                fw.op("dve", lambda e, fc=fc, x_=x_, v=v: e.scalar_tensor_tensor(
                    out=x_[:], in0=tmpb[:, fc % 2, :], scalar=G2[:, fc, v:v + 1], in1=x_[:], op0=ALU.mult, op1=ALU.add),
                    reads=[tmpb, G2, x_], writes=[x_])
                o = 0
                for (b, t0, nt) in segs:
                    fw.dma("sp", xT[b, fc * 128:(fc + 1) * 128, t0:t0 + nt], x_[:, o:o + nt], src=x_)
                    o += nt

    def norm1_to_hT(b, hT):
        xl = [fw.sb("xl%d" % i, [128, KC, 256], F32) for i in range(2)]
        sqb = fw.sb("sqn", [128, 2, 256], BF16)
        tmpb = fw.sb("tmpn", [128, 2, 256], F32)
        rstd = fw.sb("rstdn", [128, 256], F32)
        for i in range(T // 256):
            t0 = i * 256
            xb = xl[i % 2]
            fw.dma("sp", xb[:], xT[b, :, t0:t0 + 256].rearrange("(c p) t -> p c t", p=128), dst=xb)
            v = 2 if t0 < CTX else b
            norm_block(xb, 256, A1, 0, v, lambda c, t0=t0: (hT[:, c, t0:t0 + 256], hT), sqb, tmpb, rstd, PS[6 + i % 2])

    for li in layers:
        last = li == DEPTH - 1
        kind = li % 3
        adaln(li)
        if kind == 0:
            iret = li // 3
            W_in = ret_w_in[iret * D:(iret + 1) * D, :]
            W_out = ret_w_out[iret * 2 * D:(iret + 1) * 2 * D, :]
            KO = 32
        elif kind == 1:
            W_out = gdn_w_out
            KO = 32
        else:
            W_out = na_w_out
            KO = 16
        if not os.environ.get("KDBG_SKIP"):
            dmy = precast(li, W_out, KO)
        for b in range(BPC):
            if not os.environ.get("KDBG_SKIP"):
                hT = fw.sb("hT", [128, KC, T], BF16)
                m0 = fw.mark()
                norm1_to_hT(b, hT)
                fw.release_to(m0)
                if kind == 0:
                    ret_inproj(b, W_in, hT, last)
                elif kind == 1:
                    gdn_inproj(b, hT)
                else:
                    na_inproj(b, hT)
            fw.phase()
            if stop == "inproj":
                fw.emit()
                return nc, fw
            if kind == 0:
                ret_scan(b, last)
            elif kind == 2:
                na_attn(b)
            elif kind == 1:
                gdn_scan(b)
                if stop in ("gates", "g1", "g2", "g3"):
                    fw.emit()
                    return nc, fw
            fw.phase()
        outproj(li, last, KO)
        fw.phase()
        mlp(li, last)
        fw.phase()

    xo = [fw.sb("xo%d" % i, [128, KC, 128], F32) for i in range(2)]
    yo = [fw.sb("yo%d" % i, [128, D], F32) for i in range(2)]
    n = 0
    for b in range(BPC):
        for tt in range(SEQ // 128):
            xi_ = xo[n % 2]
            yo_ = yo[n % 2]
            t0 = CTX + tt * 128
            fw.dma("sp", xi_[:], xT[b, :, t0:t0 + 128].rearrange("(c p) t -> p c t", p=128), dst=xi_)
            for g4 in range(4):
                psb = PS[(n * 4 + g4) % 8]
                for j in range(4):
                    kc = g4 * 4 + j
                    tr(psb[:, j * 128:(j + 1) * 128], xi_[:, kc, :], ident[:], [xi_, ident], psb)
                if g4 % 2 == 0:
                    fw.op("act", lambda e, yo_=yo_, psb=psb, g4=g4: e.copy(yo_[:, g4 * 512:(g4 + 1) * 512], psb[:]), reads=[psb], writes=[yo_])
                else:
                    fw.op("dve", lambda e, yo_=yo_, psb=psb, g4=g4: e.tensor_copy(yo_[:, g4 * 512:(g4 + 1) * 512], psb[:]), reads=[psb], writes=[yo_])
            fw.dma("sp", y_out[b * SEQ + tt * 128: b * SEQ + (tt + 1) * 128, :], yo_[:], src=yo_)
            n += 1
    fw.barrier()
    fw.emit()
    return nc, fw


_CACHE = {}


def _consts():
    cosT, sinT = _rope_tables()
    maskT, xi, zeta, _ = _ret_tables()
    return {
        "ident": np.eye(128, dtype=np.float32),
        "ropecos": cosT, "ropesin": sinT,
        "ret_maskT": np.ascontiguousarray(maskT.reshape(16 * 128, 128)),
        "ret_xi": xi, "ret_zeta": zeta,
        "gdn_masks": _gdn_masks(),
        "gdn_lvmasks": _gdn_lvmasks(),
    }


def make_in_maps(inputs, cores):
    f = lambda a: np.ascontiguousarray(np.asarray(a, dtype=np.float32))
    shared = {
        "ada_w": f(inputs["ada_w"]).reshape(DEPTH * D, 6 * D),
        "ada_b": f(inputs["ada_b"]).reshape(DEPTH * 96, 128),
        "norm_g": f(inputs["norm_g"]).reshape(DEPTH * 64, 128),
        "mlp_w1": f(inputs["mlp_w1"]).reshape(DEPTH * D, 4 * D),
        "mlp_w2": f(inputs["mlp_w2"]).reshape(DEPTH * 4 * D, D),
        "ret_w_in": f(inputs["ret_w_in"]).reshape(2 * D, 6 * D),
        "ret_w_out": f(inputs["ret_w_out"]).reshape(2 * 2 * D, D),
        "gdn_w_in": f(inputs["gdn_w_in"]).reshape(D, 6 * D),
        "gdn_conv_w": f(inputs["gdn_conv_w"]).reshape(5 * 64, 128),
        "gdn_w_ab": f(inputs["gdn_w_ab"]).reshape(2 * D, 64),
        "gdn_a_log": f(inputs["gdn_a_log"]).reshape(1, 64),
        "gdn_dt_bias": f(inputs["gdn_dt_bias"]).reshape(1, 64),
        "gdn_norm_w": f(inputs["gdn_norm_w"]).reshape(1, 128),
        "gdn_w_out": f(inputs["gdn_w_out"]).reshape(2 * D, D),
        "na_w_in": f(inputs["na_w_in"]).reshape(D, 3 * D),
        "na_w_out": f(inputs["na_w_out"]).reshape(D, D),
        "na_bias": _na_bias(f(inputs["na_rpb"]).reshape(16, 15, 31)),
    }
    shared.update(_consts())
    x = f(inputs["x"])
    ctx = f(inputs["ctx"])
    c = f(inputs["c"])
    c_ctx = f(inputs["c_ctx"])
    maps = []
    for core in cores:
        b0 = core * BPC
        m = dict(shared)
        m["x"] = x[b0:b0 + BPC].reshape(BPC * SEQ, D)
        m["ctx"] = ctx[b0:b0 + BPC].reshape(BPC * CTX, D)
        m["cs"] = np.stack([c[b0], c[b0 + 1], c_ctx])
        maps.append(m)
    return maps


def kernel(**inputs):
    if "nc" not in _CACHE:
        _CACHE["nc"] = build_program()[0]
    nc = _CACHE["nc"]
    maps = make_in_maps(inputs, list(range(NCORE)))
    res = run_bass_kernel_spmd(nc, maps, core_ids=list(range(NCORE)))
    out = np.concatenate([r["y"].reshape(BPC, SEQ, D) for r in res.results], axis=0)
    return out.astype(np.float32)
```
